# Optimizing a Trainium2 kernel written in Bass

```python
import math
import jax, jax.numpy as jnp
from jax import lax
import numpy as np

D_MODEL = 2048
BATCH = 8
SEQ = 2048
DEPTH = 4

HEAD_DIM = 64
N_MIXERS = 4
GROUP_WIDTH = D_MODEL // N_MIXERS
N_HEADS = GROUP_WIDTH // HEAD_DIM
NSA_KV_HEADS = 2
NSA_GQA = N_HEADS // NSA_KV_HEADS
CMP_LEN = 32
CMP_STRIDE = 16
SLC_LEN = 64
SLC_TOP = 16
WINDOW = 512
FORCE_BONUS = 1.0e3
N_BUCKETS = 32
MAX_DISTANCE = 1024
CONV_W = 3
CHUNK = 128
Q_BLOCK = 128
D_FF = -(-8 * D_MODEL // (3 * 256)) * 256
NEG_INF = -1.0e30

A_Q_COLS = N_HEADS * HEAD_DIM
A_KV_COLS = 6 * NSA_KV_HEADS * HEAD_DIM
A_GATE_COLS = 3 * N_HEADS
B_COLS = 3 * GROUP_WIDTH
C_COLS = 2 * GROUP_WIDTH
D_COLS = 3 * GROUP_WIDTH
SPLIT_POINTS = (A_Q_COLS, A_Q_COLS + A_KV_COLS, A_Q_COLS + A_KV_COLS + A_GATE_COLS, A_Q_COLS + A_KV_COLS + A_GATE_COLS + B_COLS, A_Q_COLS + A_KV_COLS + A_GATE_COLS + B_COLS + C_COLS)
W_IN_COLS = SPLIT_POINTS[-1] + D_COLS

kernel_name = 'hybrid_nsa_conv_sgu_stickbreak'


def rms_norm(x, g, eps=1e-6):
    xf = x.astype(jnp.float32)
    y = xf * lax.rsqrt(jnp.mean(xf * xf, axis=-1, keepdims=True) + eps)
    return (y * g.astype(jnp.float32)).astype(x.dtype)


def layer_norm_noaffine(x, eps=1e-5):
    xf = x.astype(jnp.float32)
    mu = jnp.mean(xf, axis=-1, keepdims=True)
    var = jnp.mean(jnp.square(xf - mu), axis=-1, keepdims=True)
    return ((xf - mu) * lax.rsqrt(var + eps)).astype(x.dtype)


def gelu(x):
    return jax.nn.gelu(x, approximate=True)


def rel_bucket(dist):
    n = jnp.maximum(dist, 0)
    max_exact = N_BUCKETS // 2
    nf = jnp.maximum(n, 1).astype(jnp.float32)
    large = max_exact + (jnp.log(nf / max_exact) / math.log(MAX_DISTANCE / max_exact) * (N_BUCKETS - max_exact)).astype(jnp.int32)
    large = jnp.minimum(large, N_BUCKETS - 1)
    return jnp.where(n < max_exact, n, large)


def rel_bias_heads(dist, table):
    b = table[rel_bucket(dist)].astype(jnp.float32)
    b = jnp.moveaxis(b, -1, 0)
    return b.reshape(NSA_KV_HEADS, NSA_GQA, *dist.shape)


def masked_softmax(s, mask):
    s = jnp.where(mask, s, NEG_INF)
    p = jax.nn.softmax(s, axis=-1)
    return jnp.where(mask, p, 0.0)


def cmp_slc_overlap(n_cmp, n_slc):
    c0 = np.arange(n_cmp)[:, None] * CMP_STRIDE
    s0 = np.arange(n_slc)[None, :] * SLC_LEN
    ov = np.minimum(c0 + CMP_LEN, s0 + SLC_LEN) - np.maximum(c0, s0)
    return (np.maximum(ov, 0) / CMP_LEN).astype(np.float32)


def nsa_attention(q, k_c, v_c, k_s, v_s, k_w, v_w, gates, q_gain, k_gain, cmp_pos, cmp_w1, cmp_w2, rel_table):
    B, T = q.shape[0], q.shape[1]
    G, R, Dh = NSA_KV_HEADS, NSA_GQA, HEAD_DIM
    scale = Dh ** -0.5
    q = rms_norm(q, q_gain).reshape(B, T, G, R, Dh)
    k_s = rms_norm(k_s, k_gain)
    k_w = rms_norm(k_w, k_gain)
    t_pos = jnp.arange(T, dtype=jnp.int32)

    n_cmp = (T - CMP_LEN) // CMP_STRIDE + 1
    blk = np.arange(n_cmp)[:, None] * CMP_STRIDE + np.arange(CMP_LEN)[None, :]

    def compress(z, i):
        zb = z[:, blk] + cmp_pos[i][None, None, :, None, :]
        zb = zb.transpose(0, 1, 3, 2, 4).reshape(B, n_cmp, G, CMP_LEN * Dh)
        return gelu(zb @ cmp_w1[i]) @ cmp_w2[i]

    kc = rms_norm(compress(k_c, 0), k_gain)
    vc = compress(v_c, 1)
    cmp_end = jnp.asarray(blk[:, -1], jnp.int32)
    dist_c = t_pos[:, None] - cmp_end[None, :]
    s_c = jnp.einsum('btgrd,bngd->bgrtn', q, kc).astype(jnp.float32) * scale + rel_bias_heads(dist_c, rel_table)
    p_c = masked_softmax(s_c, dist_c >= 0)
    o_cmp = jnp.einsum('bgrtn,bngd->btgrd', p_c.astype(vc.dtype), vc)

    n_slc = T // SLC_LEN
    top = min(SLC_TOP, n_slc)
    imp = jnp.einsum('bgrtn,nj->bgtj', p_c, jnp.asarray(cmp_slc_overlap(n_cmp, n_slc)))
    j_idx = jnp.arange(n_slc, dtype=jnp.int32)
    cur = t_pos // SLC_LEN
    valid = (j_idx[None, :] * SLC_LEN) <= t_pos[:, None]
    forced = (j_idx[None, :] == 0) | (j_idx[None, :] == cur[:, None]) | (j_idx[None, :] == cur[:, None] - 1)
    score = jnp.where(valid, imp + jnp.where(forced, FORCE_BONUS, 0.0), NEG_INF)
    _, sel = lax.top_k(score, top)

    ks_blk = k_s.reshape(B, n_slc, SLC_LEN, G, Dh).transpose(0, 3, 1, 2, 4)
    vs_blk = v_s.reshape(B, n_slc, SLC_LEN, G, Dh).transpose(0, 3, 1, 2, 4)
    pad = ((0, 0), (WINDOW, 0), (0, 0), (0, 0))
    kw_pad = jnp.pad(k_w, pad)
    vw_pad = jnp.pad(v_w, pad)
    table_g = rel_table.reshape(N_BUCKETS, G, R)
    g_idx = jnp.arange(G)[None, :, None, None, None]
    gather_blocks = jax.vmap(jax.vmap(lambda blocks, ix: blocks[ix]))
    win_len = WINDOW + Q_BLOCK

    def block_fn(qb):
        qs = qb * Q_BLOCK
        qt = qs + jnp.arange(Q_BLOCK, dtype=jnp.int32)
        qblk = lax.dynamic_slice_in_dim(q, qs, Q_BLOCK, axis=1)
        ix = lax.dynamic_slice_in_dim(sel, qs, Q_BLOCK, axis=2)
        ksel = gather_blocks(ks_blk, ix)
        vsel = gather_blocks(vs_blk, ix)
        kpos = ix[..., None] * SLC_LEN + jnp.arange(SLC_LEN, dtype=jnp.int32)
        d_s = qt[None, None, :, None, None] - kpos
        b_s = jnp.moveaxis(table_g[rel_bucket(d_s), g_idx].astype(jnp.float32), -1, 2)
        s_s = jnp.einsum('btgrd,bgtkld->bgrtkl', qblk, ksel).astype(jnp.float32) * scale + b_s
        shp = s_s.shape
        p_s = masked_softmax(s_s.reshape(B, G, R, Q_BLOCK, -1), (d_s >= 0).reshape(B, G, 1, Q_BLOCK, -1)).reshape(shp)
        o_s = jnp.einsum('bgrtkl,bgtkld->btgrd', p_s.astype(vsel.dtype), vsel)
        kw = lax.dynamic_slice_in_dim(kw_pad, qs, win_len, axis=1)
        vw = lax.dynamic_slice_in_dim(vw_pad, qs, win_len, axis=1)
        kp = qs - WINDOW + jnp.arange(win_len, dtype=jnp.int32)
        d_w = qt[:, None] - kp[None, :]
        m_w = (d_w >= 0) & (d_w < WINDOW) & (kp[None, :] >= 0)
        s_w = jnp.einsum('btgrd,bsgd->bgrts', qblk, kw).astype(jnp.float32) * scale + rel_bias_heads(d_w, rel_table)
        p_w = masked_softmax(s_w, m_w)
        o_w = jnp.einsum('bgrts,bsgd->btgrd', p_w.astype(vw.dtype), vw)
        return o_s, o_w

    o_slc, o_win = lax.map(block_fn, jnp.arange(T // Q_BLOCK, dtype=jnp.int32))
    o_slc = jnp.moveaxis(o_slc, 0, 1).reshape(B, T, G, R, Dh)
    o_win = jnp.moveaxis(o_win, 0, 1).reshape(B, T, G, R, Dh)

    g = jax.nn.sigmoid(gates.astype(jnp.float32)).astype(q.dtype).reshape(B, T, G, R, 3)
    o = g[..., 0:1] * o_cmp + g[..., 1:2] * o_slc + g[..., 2:3] * o_win
    return o.reshape(B, T, G * R * Dh)


def short_conv_mixer(cols, conv_w):
    b_gate, c_gate, h = jnp.split(cols, 3, axis=-1)
    z = c_gate * h
    T = z.shape[1]
    zp = jnp.pad(z, ((0, 0), (CONV_W - 1, 0), (0, 0)))
    y = conv_w[0] * zp[:, 0:T]
    for i in range(1, CONV_W):
        y = y + conv_w[i] * zp[:, i:i + T]
    return b_gate * y


def spatial_gating_mixer(cols, sgu_w, sgu_b):
    B, T = cols.shape[0], cols.shape[1]
    u, v = jnp.split(gelu(cols), 2, axis=-1)
    v = layer_norm_noaffine(v).reshape(B, T // CHUNK, CHUNK, N_HEADS, HEAD_DIM)
    w = sgu_w * jnp.asarray(np.tril(np.ones((CHUNK, CHUNK), np.float32)), sgu_w.dtype)
    s = jnp.einsum('hpq,bcqhe->bcphe', w, v) + sgu_b.T[None, None, :, :, None]
    return u * s.reshape(B, T, GROUP_WIDTH)


def stick_breaking_attention(q, k, v):
    B, T, H, Dh = q.shape
    scale = Dh ** -0.5
    outs = []
    for qb in range(T // Q_BLOCK):
        qs, qe = qb * Q_BLOCK, (qb + 1) * Q_BLOCK
        z = jnp.einsum('bthd,bshd->bhts', q[:, qs:qe], k[:, :qe]).astype(jnp.float32) * scale
        mask = jnp.asarray(np.arange(qe)[None, :] < np.arange(qs, qe)[:, None])
        log_beta = jax.nn.log_sigmoid(z)
        log_1m = jnp.where(mask, log_beta - z, 0.0)
        tail = lax.cumsum(log_1m, axis=3, reverse=True) - log_1m
        a = jnp.where(mask, jnp.exp(log_beta + tail), 0.0)
        outs.append(jnp.einsum('bhts,bshd->bthd', a.astype(v.dtype), v[:, :qe]))
    return jnp.concatenate(outs, axis=1).reshape(B, T, H * Dh)


def setup_inputs(seed: int = 0) -> dict:
    key = jax.random.key(seed)
    ks = jax.random.split(key, 18)
    f32 = jnp.float32

    def nrm(k, shape, scale):
        return jax.random.normal(k, shape, f32) * scale

    res_scale = (2 * DEPTH) ** -0.5
    return {
        'x': nrm(ks[0], (BATCH, SEQ, D_MODEL), 1.0),
        'w_in': nrm(ks[1], (DEPTH, D_MODEL, W_IN_COLS), D_MODEL ** -0.5),
        'w_out': nrm(ks[2], (DEPTH, D_MODEL, D_MODEL), D_MODEL ** -0.5 * res_scale),
        'norm_mix': 1.0 + nrm(ks[3], (DEPTH, D_MODEL), 0.05),
        'norm_ffn': 1.0 + nrm(ks[4], (DEPTH, D_MODEL), 0.05),
        'q_gain': 1.0 + nrm(ks[5], (DEPTH, HEAD_DIM), 0.05),
        'k_gain': 1.0 + nrm(ks[6], (DEPTH, HEAD_DIM), 0.05),
        'cmp_pos': nrm(ks[7], (DEPTH, 2, CMP_LEN, HEAD_DIM), 0.1),
        'cmp_w1': nrm(ks[8], (DEPTH, 2, CMP_LEN * HEAD_DIM, HEAD_DIM), (CMP_LEN * HEAD_DIM) ** -0.5),
        'cmp_w2': nrm(ks[9], (DEPTH, 2, HEAD_DIM, HEAD_DIM), HEAD_DIM ** -0.5),
        'rel_table': nrm(ks[10], (N_BUCKETS, N_HEADS), 0.5),
        'conv_w': nrm(ks[11], (DEPTH, CONV_W, GROUP_WIDTH), CONV_W ** -0.5),
        'sgu_w': nrm(ks[12], (DEPTH, N_HEADS, CHUNK, CHUNK), CHUNK ** -0.5),
        'sgu_b': 1.0 + nrm(ks[13], (DEPTH, N_HEADS, CHUNK), 0.1),
        'group_gain': 1.0 + nrm(ks[14], (DEPTH, D_MODEL), 0.05),
        'w_ffn_gate': nrm(ks[15], (DEPTH, D_MODEL, D_FF), D_MODEL ** -0.5),
        'w_ffn_up': nrm(ks[16], (DEPTH, D_MODEL, D_FF), D_MODEL ** -0.5),
        'w_ffn_down': nrm(ks[17], (DEPTH, D_FF, D_MODEL), D_FF ** -0.5 * res_scale),
    }


def reference(x, w_in, w_out, norm_mix, norm_ffn, q_gain, k_gain, cmp_pos, cmp_w1, cmp_w2, rel_table, conv_w, sgu_w, sgu_b, group_gain, w_ffn_gate, w_ffn_up, w_ffn_down):
    B, T = x.shape[0], x.shape[1]
    for l in range(DEPTH):
        h = rms_norm(x, norm_mix[l])
        proj = h @ w_in[l]
        a_q, a_kv, a_g, b_cols, c_cols, d_cols = jnp.split(proj, SPLIT_POINTS, axis=-1)
        kv = a_kv.reshape(B, T, 6, NSA_KV_HEADS, HEAD_DIM)
        o_a = nsa_attention(a_q.reshape(B, T, N_HEADS, HEAD_DIM), kv[:, :, 0], kv[:, :, 1], kv[:, :, 2], kv[:, :, 3], kv[:, :, 4], kv[:, :, 5], a_g.reshape(B, T, N_HEADS, 3), q_gain[l], k_gain[l], cmp_pos[l], cmp_w1[l], cmp_w2[l], rel_table)
        o_b = short_conv_mixer(b_cols, conv_w[l])
        o_c = spatial_gating_mixer(c_cols, sgu_w[l], sgu_b[l])
        dq, dk, dv = jnp.split(d_cols, 3, axis=-1)
        o_d = stick_breaking_attention(dq.reshape(B, T, N_HEADS, HEAD_DIM), dk.reshape(B, T, N_HEADS, HEAD_DIM), dv.reshape(B, T, N_HEADS, HEAD_DIM))
        mixed = jnp.stack([o_a, o_b, o_c, o_d], axis=2)
        mixed = rms_norm(mixed, group_gain[l].reshape(N_MIXERS, GROUP_WIDTH))
        x = x + mixed.reshape(B, T, D_MODEL) @ w_out[l]
        h = rms_norm(x, norm_ffn[l])
        x = x + (jax.nn.silu(h @ w_ffn_gate[l]) * (h @ w_ffn_up[l])) @ w_ffn_down[l]
    return x
```

```python
import math
import os
import numpy as np
import ml_dtypes
import concourse.bass as bass
import concourse.mybir as mybir
from concourse.bass_utils import run_bass_kernel_spmd

F32 = mybir.dt.float32
BF16 = mybir.dt.bfloat16
AF = mybir.ActivationFunctionType
ALU = mybir.AluOpType
AX = mybir.AxisListType

D_MODEL = 2048
T = 2048
DEPTH = 4
HD = 64
GW = 512
NH = 8
G = 2
R = 4
CMP_LEN = 32
CMP_STRIDE = 16
N_CMP = 127
SLC_LEN = 64
N_SLC = 32
SLC_TOP = 16
WINDOW = 512
N_BUCKETS = 32
MAX_DISTANCE = 1024
D_FF = 5632
W_IN_COLS = 5400
NEG = -30000.0
NKC = D_MODEL // 128
NTT = T // 128
NFC = D_FF // 128


class Prog:
    ENGS = ("tensor", "vector", "scalar", "gpsimd", "sync")

    def __init__(self, nc, n_dma_sems=10):
        self.nc = nc
        self.ops = {e: [] for e in self.ENGS}
        self.sem = {e: nc.alloc_semaphore(name=f"sem_{e}") for e in ("tensor", "vector", "scalar", "gpsimd")}
        self.cnt = {e: 0 for e in self.sem}
        nsem = {"sync": n_dma_sems, "gpsimd": 4}
        self.dma_sems = {q: [nc.alloc_semaphore(name=f"dsem_{q}{i}") for i in range(nsem[q])] for q in ("sync", "gpsimd")}
        self.dma_cnt = {q: [0] * nsem[q] for q in ("sync", "gpsimd")}
        self.dma_rr = {q: 0 for q in ("sync", "gpsimd")}
        self.waited = {e: {} for e in self.ENGS}
        self.last_w = {}
        self.readers = {}
        self.sem_by_id = {}
        self.pending = {}

    def _ev_id(self, sem):
        i = id(sem)
        self.sem_by_id[i] = sem
        return i

    def _need(self, eng, ev, waits):
        if ev is None:
            return
        sid, val, src_eng = ev
        if src_eng == "tensor" and eng == "tensor":
            return
        if self.waited[eng].get(sid, 0) >= val:
            return
        waits[sid] = max(waits.get(sid, 0), val)

    def op(self, eng, fn, reads=(), writes=(), signal=True):
        waits = {}
        for k in reads:
            self._need(eng, self.last_w.get(k), waits)
        for k in writes:
            self._need(eng, self.last_w.get(k), waits)
            for ev in self.readers.get(k, {}).values():
                self._need(eng, ev, waits)
        for sid, val in waits.items():
            self.waited[eng][sid] = val
        ev = None
        if signal:
            self.cnt[eng] += 1
            ev = (self._ev_id(self.sem[eng]), self.cnt[eng], eng)
        self.ops[eng].append((list(waits.items()), fn, (self.sem[eng], 1) if signal else None))
        if ev is not None:
            pr, pw = self.pending.pop(eng, ([], []))
            for k in list(reads) + pr:
                self.readers.setdefault(k, {})[ev[0]] = ev
            for k in list(writes) + pw:
                self.last_w[k] = ev
                self.readers[k] = {}
        else:
            assert eng == "tensor"
            pr, pw = self.pending.setdefault(eng, ([], []))
            pr.extend(reads)
            pw.extend(writes)
        return ev

    def dma(self, q, fn, reads=(), writes=(), war=()):
        waits = {}
        for k in war:
            for ev in self.readers.get(k, {}).values():
                self._need(q, ev, waits)
        for k in reads:
            self._need(q, self.last_w.get(k), waits)
        for k in writes:
            self._need(q, self.last_w.get(k), waits)
            for ev in self.readers.get(k, {}).values():
                self._need(q, ev, waits)
        i = self.dma_rr[q]
        self.dma_rr[q] = (i + 1) % len(self.dma_sems[q])
        s = self.dma_sems[q][i]
        sid = self._ev_id(s)
        prev = self.dma_cnt[q][i]
        if prev > 0 and self.waited[q].get(sid, 0) < prev:
            waits[sid] = max(waits.get(sid, 0), prev)
        for sd, val in waits.items():
            self.waited[q][sd] = val
        self.dma_cnt[q][i] = prev + 16
        ev = (sid, prev + 16, "dma_" + q)
        self.ops[q].append((list(waits.items()), fn, (s, 16)))
        for k in reads:
            self.readers.setdefault(k, {})[("d", q, i)] = ev
        for k in writes:
            self.last_w[k] = ev
            self.readers[k] = {}
        return ev

    def finish(self, out_keys):
        waits = {}
        for k in out_keys:
            self._need("sync", self.last_w.get(k), waits)
        final_waits = list(waits.items())
        nc = self.nc
        prog = self

        def emit(e, name):
            for waits_, fn, sig in prog.ops[name]:
                for sid, val in waits_:
                    e.wait_ge(prog.sem_by_id[sid], val)
                if fn is None:
                    continue
                inst = fn(e)
                if sig is not None:
                    inst.then_inc(sig[0], sig[1])

        with nc.Block() as block:
            @block.tensor
            def _(e):
                emit(e, "tensor")

            @block.vector
            def _(e):
                emit(e, "vector")

            @block.scalar
            def _(e):
                emit(e, "scalar")

            @block.gpsimd
            def _(e):
                emit(e, "gpsimd")

            @block.sync
            def _(e):
                emit(e, "sync")
                for q in ("sync", "gpsimd"):
                    for i, s_ in enumerate(prog.dma_sems[q]):
                        if prog.dma_cnt[q][i] > 0:
                            e.wait_ge(s_, prog.dma_cnt[q][i])


class K:
    def __init__(self, nc):
        self.nc = nc
        self.P = Prog(nc)
        self.ps_rr = 0

    def act(self, out, in_, func, reads, writes, **kw):
        return self.P.op("scalar", lambda e: e.activation(out=out, in_=in_, func=func, **kw), reads, writes)

    def ts(self, eng, out, in0, s1, s2, op0, op1, reads, writes, **kw):
        if op1 is None:
            return self.P.op(eng, lambda e: e.tensor_scalar(out=out, in0=in0, scalar1=s1, scalar2=None, op0=op0, **kw), reads, writes)
        return self.P.op(eng, lambda e: e.tensor_scalar(out=out, in0=in0, scalar1=s1, scalar2=s2, op0=op0, op1=op1, **kw), reads, writes)

    def tt(self, eng, out, in0, in1, op, reads, writes):
        return self.P.op(eng, lambda e: e.tensor_tensor(out=out, in0=in0, in1=in1, op=op), reads, writes)

    def stt(self, out, in0, scalar, in1, op0, op1, reads, writes):
        return self.P.op("vector", lambda e: e.scalar_tensor_tensor(out=out, in0=in0, scalar=scalar, in1=in1, op0=op0, op1=op1), reads, writes)

    def copy(self, eng, out, in_, reads, writes):
        if eng == "scalar":
            return self.P.op("scalar", lambda e: e.activation(out=out, in_=in_, func=AF.Copy), reads, writes)
        return self.P.op(eng, lambda e: e.tensor_copy(out=out, in_=in_), reads, writes)

    def recip(self, out, in_, reads, writes):
        return self.P.op("vector", lambda e: e.reciprocal(out=out, in_=in_), reads, writes)

    def memset(self, eng, ap, val, writes):
        return self.P.op(eng, lambda e: e.memset(ap, val), (), writes)

    def mm(self, out, lhsT, rhs, start, stop, reads, writes, signal=None, **kw):
        if signal is None:
            signal = stop
        return self.P.op("tensor", lambda e: e.matmul(out, lhsT, rhs, start=start, stop=stop, **kw), reads, writes, signal=signal)

    def transpose(self, out, in_, ident, reads, writes, signal=True):
        return self.P.op("tensor", lambda e: e.transpose(out, in_, ident), reads, writes, signal=signal)

    def dma(self, q, out, in_, reads, writes, war=()):
        return self.P.dma(q, lambda e: e.dma_start(out=out, in_=in_), reads, writes, war)


class Mem:
    def __init__(self, nc):
        self.big = nc.alloc_sbuf_tensor("big", [128, 192 * 256], F32)
        self.top = 0
        self.floor = 0
        self.ps = nc.alloc_psum_tensor("ps", [128, 8 * 512], F32)

    def alloc(self, free_shape, dtype=F32):
        n = int(np.prod(free_shape))
        words = n if dtype == F32 else (n + 1) // 2
        words = (words + 15) // 16 * 16
        a = self.top
        self.top += words
        assert self.top <= 192 * 256, f"SBUF overflow {self.top}"
        v = self.big[:, a:a + words]
        if dtype != F32:
            v = v.bitcast(dtype)
        v = v[:, 0:n]
        if len(free_shape) == 2:
            v = v.rearrange("p (a b) -> p a b", b=free_shape[1])
        elif len(free_shape) == 3:
            v = v.rearrange("p (a b c) -> p a b c", b=free_shape[1], c=free_shape[2])
        return v

    def set_floor(self):
        self.floor = self.top

    def reset(self):
        self.top = self.floor

    def bank(self, i, dtype=F32):
        v = self.ps[:, i * 512:(i + 1) * 512]
        if dtype != F32:
            v = v.bitcast(dtype)
        return v


def _barrier(P):
    evs = []
    for e, s in P.sem.items():
        if P.cnt[e] > 0:
            evs.append((P._ev_id(s), P.cnt[e]))
    for q in ("sync", "gpsimd"):
        for i, s in enumerate(P.dma_sems[q]):
            if P.dma_cnt[q][i] > 0:
                evs.append((P._ev_id(s), P.dma_cnt[q][i]))
    for eng in P.ENGS:
        waits = []
        for sid, val in evs:
            if P.waited[eng].get(sid, 0) < val:
                waits.append((sid, val))
                P.waited[eng][sid] = val
        if waits:
            P.ops[eng].append((waits, None, None))
    P.last_w = {}
    P.readers = {}


FM_GROUPS = [
    [(128 * j, 64 * j, 64) for j in range(4)] + [(128 * j + 64, 64 * (4 + j), 64) for j in range(4)],
    [(0, 512, 128), (128, 640, 128), (256, 768, 128), (384, 1024, 128)],
    [(0, 1304, 512)], [(0, 1816, 512)], [(0, 2328, 512)],
    [(0, 2840, 512)],
    [(0, 3864, 512)], [(0, 4376, 512)],
]
TM_BLOCKS = [
    (0, [(0, 896, 128), (128, 1152, 128), (256, 1280, 24)], 280),
    (280, [(0, 3352, 512)], 512),
    (792, [(0, 4888, 512)], 512),
]
N_FM = 32
N_TM = 1304


def _rel_bucket_np(n):
    n = np.maximum(n, 0)
    max_exact = N_BUCKETS // 2
    nf = np.maximum(n, 1).astype(np.float32)
    large = max_exact + (np.log(nf / np.float32(max_exact)) / np.float32(math.log(MAX_DISTANCE / max_exact)) * np.float32(N_BUCKETS - max_exact)).astype(np.int32)
    large = np.minimum(large, N_BUCKETS - 1)
    return np.where(n < max_exact, n, large)


def host_consts():
    c = {}
    c["ident"] = np.eye(128, dtype=np.float32)
    c["ones"] = np.ones((128, 128), np.float32)
    blk = np.zeros((128, 128), np.float32)
    blk[:64, :64] = 1
    blk[64:, 64:] = 1
    c["blk64"] = blk
    i = np.arange(128)
    c["tril_qp"] = (i[:, None] <= i[None, :]).astype(np.float32)
    c["lt_st"] = (i[:, None] < i[None, :]).astype(np.float32)
    c["uincl"] = (i[:, None] >= i[None, :]).astype(np.float32)
    c["lstrict"] = (i[:, None] < i[None, :]).astype(np.float32)
    return c


class Ctx:
    pass


def setup_consts(k, m, d, C):
    nc = k.nc
    C.cf = {}
    C.cb = {}
    names = ["ident", "ones", "blk64", "tril_qp", "lt_st", "uincl", "lstrict"]
    for i, n in enumerate(names):
        if n in ("ident", "tril_qp", "ones"):
            t = m.alloc([128])
            k.dma("sync", t, d["cst"][:, i * 128:(i + 1) * 128], (), [("cf", n)])
            C.cf[n] = t
        tb = m.alloc([128], BF16)
        k.dma("gpsimd", tb, d["cst"][:, i * 128:(i + 1) * 128], (), [("cb", n)])
        C.cb[n] = tb
    C.zeros = m.alloc([512], BF16)
    k.memset("vector", C.zeros, 0.0, [("zeros",)])
    C.eps6 = m.alloc([1])
    k.memset("vector", C.eps6, 1e-6, [("eps6",)])
    C.eps5 = m.alloc([1])
    k.memset("vector", C.eps5, 1e-5, [("eps5",)])
    C.one1 = m.alloc([1])
    k.memset("vector", C.one1, 1.0, [("one1",)])
    C.gmix = m.alloc([DEPTH * 16])
    C.gffn = m.alloc([DEPTH * 16])
    C.ggrp = m.alloc([DEPTH * 16])
    for nm, t in (("norm_mix", C.gmix), ("norm_ffn", C.gffn), ("group_gain", C.ggrp)):
        k.dma("sync", t, d[nm].rearrange("l (kc p) -> p (l kc)", p=128), (), [("g", nm)])
    C.keys = [("cf", n) for n in C.cf] + [("cb", n) for n in C.cb] + [("zeros",), ("eps6",), ("eps5",), ("one1",), ("g", "norm_mix"), ("g", "norm_ffn"), ("g", "group_gain")]


def phase_norm(k, m, C, x_ap, gvec, hT):
    xt = [m.alloc([2048]) for _ in range(2)]
    xs = [m.alloc([2048], BF16) for _ in range(2)]
    junk = m.alloc([2048], BF16)
    st = m.alloc([64])
    for tt in range(NTT):
        s = tt % 2
        k.dma("sync", xt[s], x_ap[tt * 128:(tt + 1) * 128, :], (), [f"xt{s}"])
        k.act(junk, xt[s], AF.Square, [f"xt{s}"], ["junk", ("ss", tt)], accum_out=st[:, tt:tt + 1])
        k.act(st[:, 16 + tt:17 + tt], st[:, tt:tt + 1], AF.Sqrt, [("ss", tt), ("eps6",)], [("sq", tt)], scale=1.0 / D_MODEL, bias=C.eps6[:, 0:1])
        k.recip(st[:, 32 + tt:33 + tt], st[:, 16 + tt:17 + tt], [("sq", tt)], [("rs", tt)])
        k.ts("vector", xs[s], xt[s], st[:, 32 + tt:33 + tt], None, ALU.mult, None, [f"xt{s}", ("rs", tt)], [f"xs{s}"])
        for half in range(2):
            pst = m.bank(6 + half, BF16)
            for j in range(8):
                kc = half * 8 + j
                k.transpose(pst[:, j * 128:(j + 1) * 128], xs[s][:, kc * 128:(kc + 1) * 128], C.cb["ident"],
                            [f"xs{s}", ("cb", "ident")], [f"pst{half}"], signal=(j == 7))
            k.tt("vector", hT[:, half * 8:(half + 1) * 8, tt * 128:(tt + 1) * 128],
                 pst.rearrange("p (a b) -> p a b", b=128),
                 gvec[:, half * 8:(half + 1) * 8].unsqueeze(2).broadcast_to([128, 8, 128]),
                 ALU.mult, [f"pst{half}"] + C.keys, [("hT", tt)])


def phase_inproj(k, m, C, d, l, hT):
    w_l = d["w_in"][l].rearrange("(kc p) c -> p kc c", p=128)
    wt = [m.alloc([16, 512], BF16) for _ in range(2)]
    stage = [m.alloc([2048]) for _ in range(3)]
    ci = 0
    bi = 0
    for g, segs in enumerate(FM_GROUPS):
        s = g % 2
        for (dst, src, n) in segs:
            k.dma("gpsimd", wt[s][:, :, dst:dst + n], w_l[:, :, src:src + n], (), [(f"wt{s}", dst)], war=[f"wtall{s}"])
        for c in range(4):
            segkeys = [f"wtall{s}"] + [(f"wt{s}", dst) for (dst, src, n) in segs if dst < (c + 1) * 128 and dst + n > c * 128]
            sg = stage[ci % 3]
            for tb in range(4):
                b = bi % 6
                bi += 1
                bank = m.bank(b)
                for kc in range(16):
                    k.mm(bank, wt[s][:, kc, c * 128:(c + 1) * 128], hT[:, kc, tb * 512:(tb + 1) * 512], kc == 0, kc == 15,
                         segkeys + [("hT", 4 * tb + i) for i in range(4)], [f"ps{b}"])
                k.copy("scalar" if (bi % 2) else "vector", sg[:, tb * 512:(tb + 1) * 512], bank, [f"ps{b}"], [(f"stage{ci % 3}", tb)])
            k.dma("sync", d["projF"][4 * g + c], sg, [(f"stage{ci % 3}", tb) for tb in range(4)], [("projF", 4 * g + c)])
            ci += 1
    for bidx, (col0, segs, ncols) in enumerate(TM_BLOCKS):
        s = bidx % 2
        for (dst, src, n) in segs:
            k.dma("gpsimd", wt[s][:, :, dst:dst + n], w_l[:, :, src:src + n], (), [(f"wt{s}", dst)], war=[f"wtall{s}"])
        segkeys = [f"wtall{s}"] + [(f"wt{s}", dst) for (dst, src, n) in segs]
        for tt in range(NTT):
            b = bi % 6
            bi += 1
            bank = m.bank(b)
            sg = stage[ci % 3]
            for kc in range(16):
                k.mm(bank[:, 0:ncols], hT[:, kc, tt * 128:(tt + 1) * 128], wt[s][:, kc, 0:ncols], kc == 0, kc == 15,
                     segkeys + [("hT", tt)], [f"ps{b}"])
            k.copy("scalar" if (bi % 2) else "vector", sg[:, 0:ncols], bank[:, 0:ncols], [f"ps{b}"], [(f"stage{ci % 3}", 0)])
            k.dma("sync", d["projT"][tt * 128:(tt + 1) * 128, col0:col0 + ncols], sg[:, 0:ncols], [(f"stage{ci % 3}", 0)], [("projT", bidx, tt)])
            ci += 1


def phase_wout(k, m, C, d, l, x_in, x_out):
    mixT = m.alloc([16, 2048], BF16)
    xg = [m.alloc([4, 2048]) for _ in range(1)]
    sq = m.alloc([4, 2048], BF16)
    tmp = [m.alloc([512]) for _ in range(2)]
    bi = 0
    for grp in range(4):
        x4 = xg[0]
        for c in range(4):
            k.dma("sync", x4[:, c, :], d["mixF"][4 * grp + c], (), [("xg", c)])
            k.act(sq[:, c, :], x4[:, c, :], AF.Square, [("xg", c)], [("sq", c)])
        for tb in range(4):
            b = bi % 6
            bi += 1
            bank = m.bank(b)
            for c in range(4):
                k.mm(bank, C.cb["ones"], sq[:, c, tb * 512:(tb + 1) * 512], c == 0, c == 3, [("sq", c)] + C.keys, [f"ps{b}"])
            t_ = tmp[tb % 2]
            k.act(t_, bank, AF.Sqrt, [f"ps{b}"] + C.keys, [f"tmp{tb % 2}"], scale=1.0 / GW, bias=C.eps6[:, 0:1])
            k.recip(t_, t_, [f"tmp{tb % 2}"], [f"tmp{tb % 2}"])
            for c in range(4):
                ch = 4 * grp + c
                k.stt(mixT[:, ch, tb * 512:(tb + 1) * 512], x4[:, c, tb * 512:(tb + 1) * 512], C.ggrp[:, l * 16 + ch:l * 16 + ch + 1], t_,
                      ALU.mult, ALU.mult, [("xg", c), f"tmp{tb % 2}"] + C.keys, [("mixT", ch, tb)])
    w_l = d["w_out"][l].rearrange("(kc p) n -> p kc n", p=128)
    wt = [m.alloc([16, 512], BF16) for _ in range(2)]
    xin = [m.alloc([512]) for _ in range(3)]
    xi = 0
    for nb in range(4):
        s = nb % 2
        k.dma("gpsimd", wt[s], w_l[:, :, nb * 512:(nb + 1) * 512], (), [f"wo{s}"])
        for tt in range(NTT):
            b = bi % 6
            bi += 1
            bank = m.bank(b)
            xs_ = xin[xi % 3]
            k.dma("sync", xs_, x_in[tt * 128:(tt + 1) * 128, nb * 512:(nb + 1) * 512], (), [f"xin{xi % 3}"])
            for kc in range(16):
                k.mm(bank, mixT[:, kc, tt * 128:(tt + 1) * 128], wt[s][:, kc, :], kc == 0, kc == 15,
                     [f"wo{s}", ("mixT", kc, tt // 4)], [f"ps{b}"])
            k.tt("vector", xs_, bank, xs_, ALU.add, [f"ps{b}", f"xin{xi % 3}"], [f"xin{xi % 3}"])
            k.dma("sync", x_out[tt * 128:(tt + 1) * 128, nb * 512:(nb + 1) * 512], xs_, [f"xin{xi % 3}"], [("xout", nb, tt)])
            xi += 1


def phase_ffn_up(k, m, C, d, l, hT):
    wg_l = d["w_ffn_gate"][l].rearrange("(kc p) f -> p kc f", p=128)
    wu_l = d["w_ffn_up"][l].rearrange("(kc p) f -> p kc f", p=128)
    wg = [m.alloc([16, 512], BF16) for _ in range(2)]
    wu = [m.alloc([16, 512], BF16) for _ in range(2)]
    act = [m.alloc([2048], BF16) for _ in range(3)]
    sil = [m.alloc([512]) for _ in range(2)]
    bi = 0
    ai = 0
    for fg in range(NFC // 4):
        s = fg % 2
        k.dma("gpsimd", wg[s], wg_l[:, :, fg * 512:(fg + 1) * 512], (), [f"wg{s}"])
        k.dma("gpsimd", wu[s], wu_l[:, :, fg * 512:(fg + 1) * 512], (), [f"wu{s}"])
        for c in range(4):
            fc = fg * 4 + c
            a_ = act[ai % 3]
            for tb in range(4):
                bg = bi % 6
                bu = (bi + 1) % 6
                bi += 2
                hk = [("hT", 4 * tb + i) for i in range(4)]
                for kc in range(16):
                    k.mm(m.bank(bg), wg[s][:, kc, c * 128:(c + 1) * 128], hT[:, kc, tb * 512:(tb + 1) * 512], kc == 0, kc == 15, [f"wg{s}"] + hk, [f"ps{bg}"])
                for kc in range(16):
                    k.mm(m.bank(bu), wu[s][:, kc, c * 128:(c + 1) * 128], hT[:, kc, tb * 512:(tb + 1) * 512], kc == 0, kc == 15, [f"wu{s}"] + hk, [f"ps{bu}"])
                s_ = sil[tb % 2]
                k.act(s_, m.bank(bg), AF.Silu, [f"ps{bg}"], [f"sil{tb % 2}"])
                k.tt("vector", a_[:, tb * 512:(tb + 1) * 512], m.bank(bu), s_, ALU.mult, [f"ps{bu}", f"sil{tb % 2}"], [(f"act{ai % 3}", tb)])
            k.dma("sync", d["actD"][fc], a_, [(f"act{ai % 3}", tb) for tb in range(4)], [("actD", fc)])
            ai += 1


def phase_ffn_down(k, m, C, d, l, x_in, x_out):
    wd_l = d["w_ffn_down"][l].rearrange("(fc p) n -> p fc n", p=128)
    actv = d["actD"].rearrange("fc p t -> p fc t")
    wd = [m.alloc([NFC, 512], BF16) for _ in range(2)]
    ab = [m.alloc([NFC, 512], BF16) for _ in range(2)]
    xin = [m.alloc([512]) for _ in range(3)]
    bi = 0
    xi = 0
    ai = 0
    for nbi, nb in enumerate([int(c) for c in os.environ.get("NB_ORDER", "0123")]):
        s = nbi % 2
        for h in range(2):
            k.dma("gpsimd", wd[s][:, h * 22:(h + 1) * 22, :], wd_l[:, h * 22:(h + 1) * 22, nb * 512:(nb + 1) * 512], (), [(f"wd{s}", h)])
        for tg in range(4):
            a_ = ab[ai % 2]
            for h in range(2):
                k.dma("sync", a_[:, h * 22:(h + 1) * 22, :], actv[:, h * 22:(h + 1) * 22, tg * 512:(tg + 1) * 512], (), [(f"ab{ai % 2}", h)])
            for t4 in range(4):
                tt = tg * 4 + t4
                b = bi % 6
                bi += 1
                bank = m.bank(b)
                xs_ = xin[xi % 3]
                k.dma("sync", xs_, x_in[tt * 128:(tt + 1) * 128, nb * 512:(nb + 1) * 512], (), [f"xin{xi % 3}"])
                for fc in range(NFC):
                    k.mm(bank, a_[:, fc, t4 * 128:(t4 + 1) * 128], wd[s][:, fc, :], fc == 0, fc == NFC - 1,
                         [(f"wd{s}", fc // 22), (f"ab{ai % 2}", fc // 22)], [f"ps{b}"])
                k.tt("vector", xs_, bank, xs_, ALU.add, [f"ps{b}", f"xin{xi % 3}"], [f"xin{xi % 3}"])
                k.dma("sync", x_out[tt * 128:(tt + 1) * 128, nb * 512:(nb + 1) * 512], xs_, [f"xin{xi % 3}"], [("xout", nb, tt)])
                xi += 1
            ai += 1


INPUT_SPECS = [
    ("x", [T, D_MODEL]), ("w_in", [DEPTH, D_MODEL, W_IN_COLS]), ("w_out", [DEPTH, D_MODEL, D_MODEL]),
    ("norm_mix", [DEPTH, D_MODEL]), ("norm_ffn", [DEPTH, D_MODEL]), ("q_gain", [DEPTH, HD]), ("k_gain", [DEPTH, HD]),
    ("cmp_pos", [DEPTH, 2, CMP_LEN, HD]), ("cmp_w1", [DEPTH, 2, CMP_LEN * HD, HD]), ("cmp_w2", [DEPTH, 2, HD, HD]),
    ("rel_table", [N_BUCKETS, NH]), ("conv_w", [DEPTH, 3, GW]), ("sgu_w", [DEPTH, NH, 128, 128]), ("sgu_b", [DEPTH, NH, 128]),
    ("group_gain", [DEPTH, D_MODEL]), ("w_ffn_gate", [DEPTH, D_MODEL, D_FF]), ("w_ffn_up", [DEPTH, D_MODEL, D_FF]),
    ("w_ffn_down", [DEPTH, D_FF, D_MODEL]),
]
N_CST = 7


def build(n_layers=DEPTH, dbg=(), phases=None, mix=("conv", "sgu", "sb", "nsa")):
    nc = bass.Bass("TRN2", target_bir_lowering=False)
    d = {}
    for name, shape in INPUT_SPECS:
        d[name] = nc.dram_tensor(name, shape, F32, kind="ExternalInput").ap()
    d["cst"] = nc.dram_tensor("cst", [128, N_CST * 128], F32, kind="ExternalInput").ap()

    def scratch(name, shape, dt=F32):
        kind = "ExternalOutput" if name in dbg else "Internal"
        d[name] = nc.dram_tensor(name, shape, dt, kind=kind).ap()

    d["oh"] = nc.dram_tensor("oh", [33, OH_L], F32, kind="ExternalInput").ap()
    d["scadd"] = nc.dram_tensor("scadd", [T, N_SLC], F32, kind="ExternalInput").ap()
    d["esel"] = nc.dram_tensor("esel", [N_SLC, T], F32, kind="ExternalInput").ap()
    d["ovc"] = nc.dram_tensor("ovc", [N_CMP, N_SLC], F32, kind="ExternalInput").ap()
    scratch("R", [8, 128, OH_L])
    scratch("projF", [N_FM, 128, T])
    scratch("projT", [T, N_TM])
    scratch("mixF", [16, 128, T])
    scratch("actD", [NFC, 128, T], BF16)
    scratch("xa", [T, D_MODEL])
    scratch("xb", [T, D_MODEL])
    d["out"] = nc.dram_tensor("out", [T, D_MODEL], F32, kind="ExternalOutput").ap()

    k = K(nc)
    m = Mem(nc)
    C = Ctx()
    with nc.allow_non_contiguous_dma(reason="small parameter loads"):
        setup_consts(k, m, d, C)
        m.set_floor()
        _barrier(k.P)
        if "nsa" in mix:
            setup_nsa_tables(k, m, C, d)
            _barrier(k.P)
        x_cur = d["x"]
        for l in range(n_layers):
            ph = phases if phases is not None else ("norm1", "inproj", "mixers", "wout", "ffn")
            x2 = d["out"] if l == n_layers - 1 else d["xb"]
            if "inproj" in ph:
                m.reset()
                hT = m.alloc([16, T], BF16)
                phase_norm(k, m, C, x_cur, C.gmix[:, l * 16:(l + 1) * 16], hT)
                phase_inproj(k, m, C, d, l, hT)
                _barrier(k.P)
            if "mixers" in ph:
                phase_mixers(k, m, C, d, l, which=mix)
            if "wout" in ph:
                m.reset()
                phase_wout(k, m, C, d, l, x_cur, d["xa"])
                _barrier(k.P)
            if "ffn" in ph:
                m.reset()
                hT = m.alloc([16, T], BF16)
                phase_norm(k, m, C, d["xa"], C.gffn[:, l * 16:(l + 1) * 16], hT)
                phase_ffn_up(k, m, C, d, l, hT)
                _barrier(k.P)
                m.reset()
                phase_ffn_down(k, m, C, d, l, d["xa"], x2)
                _barrier(k.P)
            x_cur = x2
        k.P.finish([])
    return nc


def phase_conv(k, m, C, d, l):
    cw = m.alloc([12])
    k.dma("sync", cw.rearrange("p (w j) -> p w j", j=4), d["conv_w"][l].rearrange("w (j p) -> p w j", p=128), (), ["cw"])
    bg = [m.alloc([2048]) for _ in range(2)]
    cg = [m.alloc([2048]) for _ in range(2)]
    hh = [m.alloc([2048]) for _ in range(2)]
    z = [m.alloc([2050]) for _ in range(2)]
    y = [m.alloc([2048]) for _ in range(2)]
    for s in range(2):
        k.memset("vector", z[s][:, 0:2], 0.0, [f"z{s}"])
    for j in range(4):
        s = j % 2
        k.dma("sync", bg[s], d["projF"][8 + j], (), [f"bg{s}"])
        k.dma("sync", cg[s], d["projF"][12 + j], (), [f"cg{s}"])
        k.dma("sync", hh[s], d["projF"][16 + j], (), [f"hh{s}"])
        k.tt("gpsimd", z[s][:, 2:2050], cg[s], hh[s], ALU.mult, [f"cg{s}", f"hh{s}"], [f"z{s}"])
        k.ts("vector", y[s], z[s][:, 2:2050], cw[:, 8 + j:9 + j], None, ALU.mult, None, [f"z{s}", "cw"], [f"y{s}"])
        k.stt(y[s], z[s][:, 1:2049], cw[:, 4 + j:5 + j], y[s], ALU.mult, ALU.add, [f"z{s}", "cw", f"y{s}"], [f"y{s}"])
        k.stt(y[s], z[s][:, 0:2048], cw[:, j:j + 1], y[s], ALU.mult, ALU.add, [f"z{s}", "cw", f"y{s}"], [f"y{s}"])
        k.tt("vector", y[s], y[s], bg[s], ALU.mult, [f"y{s}", f"bg{s}"], [f"y{s}"])
        k.dma("sync", d["mixF"][4 + j], y[s], [f"y{s}"], [("mixF", 4 + j)])


def gelu_tanh(k, m, out, x, tmp, tmp2, rk, wk, tk):
    c = 1.5957691216057308
    k.tt("vector", tmp, x, x, ALU.mult, rk, tk)
    k.ts("vector", tmp, tmp, 0.044715 * c, c, ALU.mult, ALU.add, tk, tk)
    k.tt("vector", tmp, tmp, x, ALU.mult, rk + tk, tk)
    k.act(tmp2, tmp, AF.Sigmoid, tk, [tk[0] + "_2"])
    k.tt("vector", out, x, tmp2, ALU.mult, rk + [tk[0] + "_2"], wk)


def phase_sgu(k, m, C, d, l):
    wraw = m.alloc([8, 128])
    k.dma("sync", wraw, d["sgu_w"][l].rearrange("h p q -> p h q"), (), ["wraw"])
    wT = m.alloc([8, 128], BF16)
    bsb = m.alloc([8])
    k.dma("sync", bsb, d["sgu_b"][l].rearrange("h p -> p h"), (), ["bsb"])
    for h in range(8):
        b = h % 2
        k.transpose(m.bank(b)[:, 0:128], wraw[:, h, :], C.cf["ident"], ["wraw"] + C.keys, [f"ps{b}"])
        k.tt("vector", wT[:, h, :], m.bank(b)[:, 0:128], C.cf["tril_qp"], ALU.mult, [f"ps{b}"] + C.keys, [("wT", h)])
    uv = [m.alloc([1024]) for _ in range(2)]
    t1 = m.alloc([1024])
    t2 = m.alloc([1024])
    gl = [m.alloc([1024]) for _ in range(2)]
    vln = [m.alloc([512], BF16) for _ in range(2)]
    st = m.alloc([16])
    oc = [m.alloc([512]) for _ in range(2)]
    stage = [m.alloc([4, 512]) for _ in range(2)]
    for tt in range(NTT):
        s = tt % 2
        for c in range(4):
            k.dma("sync", stage[s][:, c, 0:128], d["projF"][20 + c][:, tt * 128:(tt + 1) * 128], (), [(f"ufm{s}", c)])
        for c in range(4):
            b = 2 + c % 2
            k.transpose(m.bank(b)[:, 0:128], stage[s][:, c, 0:128], C.cf["ident"], [(f"ufm{s}", c)] + C.keys, [f"ps{b}"])
            k.copy("scalar", uv[s][:, c * 128:(c + 1) * 128], m.bank(b)[:, 0:128], [f"ps{b}"], [(f"uv{s}", c)])
        k.dma("sync", uv[s][:, 512:1024], d["projT"][tt * 128:(tt + 1) * 128, 280:792], (), [(f"uv{s}", 4)])
        gelu_tanh(k, m, gl[s], uv[s], t1, t2, [(f"uv{s}", c) for c in range(5)], [f"gl{s}"], ["t1"])
        k.P.op("vector", (lambda o, i: (lambda e: e.bn_stats(out=o, in_=i)))(st[:, 0:6], gl[s][:, 512:1024]), [f"gl{s}"], ["bst"])
        k.P.op("vector", (lambda o, i: (lambda e: e.bn_aggr(out=o, in_=i)))(st[:, 8:10], st[:, 0:6]), ["bst"], ["bag"])
        k.act(st[:, 10:11], st[:, 9:10], AF.Sqrt, ["bag"] + C.keys, ["lnsd"], scale=1.0, bias=C.eps5[:, 0:1])
        k.recip(st[:, 11:12], st[:, 10:11], ["lnsd"], ["lnrs"])
        k.ts("vector", vln[s], gl[s][:, 512:1024], st[:, 8:9], st[:, 11:12], ALU.subtract, ALU.mult, [f"gl{s}", "bag", "lnrs"], [f"vln{s}"])
        b = 4 + tt % 2
        for h in range(8):
            k.mm(m.bank(b)[:, h * 64:(h + 1) * 64], wT[:, h, :], vln[s][:, h * 64:(h + 1) * 64], True, True,
                 [("wT", h), f"vln{s}"], [f"ps{b}"], signal=(h == 7))
        k.tt("vector", oc[s].rearrange("p (h e) -> p h e", e=64), m.bank(b).rearrange("p (h e) -> p h e", e=64),
             bsb.unsqueeze(2).broadcast_to([128, 8, 64]), ALU.add, [f"ps{b}", "bsb"], [f"oc{s}"])
        k.tt("vector", oc[s], oc[s], gl[s][:, 0:512], ALU.mult, [f"oc{s}", f"gl{s}"], [f"oc{s}"])
        for c in range(4):
            b2 = 2 + c % 2
            k.transpose(m.bank(b2)[:, 128:256], oc[s][:, c * 128:(c + 1) * 128], C.cf["ident"], [f"oc{s}"] + C.keys, [f"ps{b2}"])
            k.copy("scalar", stage[s][:, c, 128:256], m.bank(b2)[:, 128:256], [f"ps{b2}"], [(f"ofm{s}", c)])
            k.dma("sync", d["mixF"][8 + c][:, tt * 128:(tt + 1) * 128], stage[s][:, c, 128:256], [(f"ofm{s}", c)], [("mixF", 8 + c, tt)])


def phase_sb(k, m, C, d, l):
    scale = HD ** -0.5
    qf = m.alloc([2048])
    kf = m.alloc([2048])
    qs = m.alloc([2048], BF16)
    qn = m.alloc([2048], BF16)
    kb = m.alloc([2048], BF16)
    vb = m.alloc([16, 512], BF16)
    k.dma("gpsimd", vb, d["projT"][:, 792:1304].rearrange("(st p) c -> p st c", p=128), (), ["vb"])
    ef = [m.alloc([512]) for _ in range(2)]
    sp = [m.alloc([512], BF16) for _ in range(2)]
    aT = [m.alloc([512], BF16) for _ in range(2)]
    osb = [m.alloc([512]) for _ in range(2)]
    it = 0
    for j in range(4):
        k.dma("sync", qf, d["projF"][24 + j], (), ["qf"])
        k.dma("sync", kf, d["projF"][28 + j], (), ["kf"])
        k.ts("vector", qs, qf, scale, None, ALU.mult, None, ["qf"], ["qs"])
        k.ts("vector", qn, qf, -scale, None, ALU.mult, None, ["qf"], ["qn"])
        k.copy("gpsimd", kb, kf, ["kf"], ["kb"])
        for hh in range(2):
            h = 2 * j + hh
            pr = slice(64 * hh, 64 * hh + 64)
            for tb in range(4):
                bz, bc, bo = m.bank(0), m.bank(1), m.bank(2 + (tb % 2))
                bok = f"ps{2 + tb % 2}"
                t0 = tb * 512
                k.mm(bc, C.zeros[:, 0:128], C.zeros, True, False, [("zeros",)], ["ps1"], signal=True)
                k.mm(bo[0:64, :], C.zeros[:, 0:64], C.zeros, True, False, [("zeros",)], [bok], signal=True)
                for si in range(4 * tb + 3, -1, -1):
                    s0 = si * 128
                    c0 = max(0, s0 - t0)
                    diag = s0 >= t0
                    s_ = it % 2
                    it += 1
                    cols = slice(c0, 512)
                    tcols = slice(t0 + c0, t0 + 512)
                    k.mm(bz[:, cols], kb[pr, s0:s0 + 128], qs[pr, tcols], True, True, ["kb", "qs"], ["ps0"])
                    k.act(ef[s_][:, cols], bz[:, cols], AF.Exp, ["ps0"], [f"ef{s_}"])
                    k.act(sp[s_][:, cols], ef[s_][:, cols], AF.Ln, [f"ef{s_}"] + C.keys, [f"sp{s_}"], bias=C.one1[:, 0:1], scale=1.0)
                    if diag:
                        k.tt("gpsimd", sp[s_][:, c0:c0 + 128], sp[s_][:, c0:c0 + 128], C.cb["lt_st"], ALU.mult, [f"sp{s_}"] + C.keys, [f"sp{s_}"])
                    k.mm(bc[:, cols], C.cb["uincl"], sp[s_][:, cols], False, False, [f"sp{s_}"] + C.keys, ["ps1"], signal=False)
                    k.mm(bc[:, cols], kb[pr, s0:s0 + 128], qn[pr, tcols], False, False, ["kb", "qn"], ["ps1"], signal=True)
                    k.act(aT[s_][:, cols], bc[:, cols], AF.Exp, ["ps1"], [f"aT{s_}"], scale=-1.0)
                    if diag:
                        k.tt("gpsimd", aT[s_][:, c0:c0 + 128], aT[s_][:, c0:c0 + 128], C.cb["lt_st"], ALU.mult, [f"aT{s_}"] + C.keys, [f"aT{s_}"])
                    k.mm(bc[:, cols], kb[pr, s0:s0 + 128], qs[pr, tcols], False, False, ["kb", "qs"], ["ps1"], signal=False)
                    k.mm(bc[:, cols], C.cb["lstrict"], sp[s_][:, cols], False, si == 0, [f"sp{s_}"] + C.keys, ["ps1"], signal=True)
                    k.mm(bo[0:64, cols], vb[:, si, h * 64:(h + 1) * 64], aT[s_][:, cols], False, si == 0, ["vb", f"aT{s_}"], [bok], signal=True)
                o_ = osb[tb % 2]
                k.copy("vector", o_[0:64, :], bo[0:64, :], [bok], [f"osb{tb % 2}"])
                k.dma("sync", d["mixF"][12 + j][64 * hh:64 * hh + 64, t0:t0 + 512], o_[0:64, :], [f"osb{tb % 2}"], [("mixF", 12 + j, hh, tb)])


def phase_mixers(k, m, C, d, l, which=("conv", "sgu", "sb", "nsa")):
    if "conv" in which:
        m.reset()
        phase_conv(k, m, C, d, l)
        _barrier(k.P)
    if "sgu" in which:
        m.reset()
        phase_sgu(k, m, C, d, l)
        _barrier(k.P)
    if "sb" in which:
        m.reset()
        phase_sb(k, m, C, d, l)
        _barrier(k.P)
    if "nsa" in which:
        m.reset()
        phase_nsa(k, m, C, d, l)
        _barrier(k.P)


OH_L = 6366
OH_SO, OH_WO, OH_CO = 0, 1535, 2302


def host_nsa_consts():
    oh = np.zeros((33, OH_L), np.float32)
    x = np.arange(1535) - 127
    b = np.where(x < 0, 32, _rel_bucket_np(x))
    oh[b, OH_SO + np.arange(1535)] = 1
    x = np.arange(767) - 127
    b = np.where((x < 0) | (x >= WINDOW), 32, _rel_bucket_np(x))
    oh[b, OH_WO + np.arange(767)] = 1
    x = np.arange(4064) - 2016 - 31
    b = np.where(x < 0, 32, _rel_bucket_np(x))
    oh[b, OH_CO + np.arange(4064)] = 1
    t = np.arange(T)[:, None]
    jj = np.arange(N_SLC)[None, :]
    cur = t // SLC_LEN
    valid = jj * SLC_LEN <= t
    forced = (jj == 0) | (jj == cur) | (jj == cur - 1)
    scadd = np.where(valid, np.where(forced, 1000.0, 0.0), -1e30).astype(np.float32)
    esel = (np.arange(T)[None, :] // SLC_LEN == np.arange(N_SLC)[:, None]).astype(np.float32)
    c0 = np.arange(N_CMP)[:, None] * CMP_STRIDE
    s0 = np.arange(N_SLC)[None, :] * SLC_LEN
    ov = np.minimum(c0 + CMP_LEN, s0 + SLC_LEN) - np.maximum(c0, s0)
    ovc = (np.maximum(ov, 0) / CMP_LEN).astype(np.float32)
    return {"oh": oh, "scadd": scadd, "esel": esel, "ovc": ovc}


def setup_nsa_tables(k, m, C, d):
    tabx = m.alloc([8])
    k.memset("vector", tabx[0:64, :], NEG, ["tabx"])
    k.dma("sync", tabx[0:32, :], d["rel_table"], (), ["tabx"])
    oh = m.alloc([OH_L])
    k.dma("sync", oh[0:33, :], d["oh"], (), ["oh"])
    row = [m.alloc([OH_L]) for _ in range(2)]
    bi = 0
    for h in range(8):
        r_ = row[h % 2]
        for c0 in range(0, OH_L, 512):
            n = min(512, OH_L - c0)
            b = bi % 6
            bi += 1
            k.mm(m.bank(b)[:, 0:n], tabx[0:33, h:h + 1].broadcast_to([33, 128]), oh[0:33, c0:c0 + n], True, True, ["tabx", "oh"], [f"ps{b}"])
            k.copy("scalar" if bi % 2 else "vector", r_[:, c0:c0 + n], m.bank(b)[:, 0:n], [f"ps{b}"], [(f"row{h % 2}", c0)])
        k.dma("sync", d["R"][h], r_, [(f"row{h % 2}", c0) for c0 in range(0, OH_L, 512)], [("R", h)], war=[f"rowall{h % 2}"])


def phase_nsa(k, m, C, d, l):
    scale = HD ** -0.5
    STOP = float(os.environ.get("NSA_STOP", "99"))
    if STOP <= 0:
        return
    Rt = d["R"].tensor
    tabW = m.alloc([8, 2048], BF16)
    Wc = tabW
    Ws = tabW[:, :, 0:1408]
    Ww = tabW[:, :, 1408:2048]
    for h in range(8):
        base = h * 128 * OH_L
        k.dma("gpsimd", Wc[0:127, h, :], bass.AP(tensor=Rt, offset=base + OH_CO + 2016, ap=[[OH_L - 16, 127], [1, 2048]]), (), [("Wc", h)])
    esel = m.alloc([2048], BF16)
    k.dma("gpsimd", esel[0:32, :], d["esel"], (), ["esel"])
    scadd = m.alloc([16, 32])
    k.dma("sync", scadd, d["scadd"].rearrange("(tt p) j -> p tt j", p=128), (), ["scadd"])
    qg = m.alloc([2])
    kg = m.alloc([1])
    for half in range(2):
        k.dma("sync", qg[64 * half:64 * half + 64, 0:1], d["q_gain"][l].rearrange("(d o) -> d o", o=1), (), [("qg", half)])
        k.dma("sync", kg[64 * half:64 * half + 64, 0:1], d["k_gain"][l].rearrange("(d o) -> d o", o=1), (), [("kg", half)])
    k.ts("vector", qg[:, 1:2], qg[:, 0:1], scale, None, ALU.mult, None, [("qg", 0), ("qg", 1)], ["qgs"])
    qT = m.alloc([4, 2048], BF16)
    ksT = m.alloc([2048], BF16)
    kwT = m.alloc([2048], BF16)
    vx = m.alloc([16, 4 * 65], BF16)
    gt = m.alloc([16, 24])
    cacc = m.alloc([16, 8 * 97])
    rz = m.alloc([16, 8])
    negT = m.alloc([2, 2048], BF16)
    mark = m.top
    xf = m.alloc([2048])
    sq = m.alloc([2048], BF16)
    rs = [m.alloc([512]) for _ in range(2)]
    bi = [0]

    def nb():
        b = bi[0] % 6
        bi[0] += 1
        return b

    def headnorm(chunk, gain, gkeys, dst, dkey):
        k.dma("sync", xf, d["projF"][chunk], (), ["xf"])
        k.act(sq, xf, AF.Square, ["xf"], ["sq"])
        for tb in range(4):
            b = nb()
            cs = slice(tb * 512, (tb + 1) * 512)
            k.mm(m.bank(b), C.cb["blk64"], sq[:, cs], True, True, ["sq"] + C.keys, [f"ps{b}"])
            r_ = rs[tb % 2]
            k.act(r_, m.bank(b), AF.Sqrt, [f"ps{b}"] + C.keys, [f"rs{tb % 2}"], scale=1.0 / HD, bias=C.eps6[:, 0:1])
            k.recip(r_, r_, [f"rs{tb % 2}"], [f"rs{tb % 2}"])
            k.stt(dst[:, cs], xf[:, cs], gain, r_, ALU.mult, ALU.mult, ["xf", f"rs{tb % 2}"] + gkeys, [dkey])

    for j in range(4):
        headnorm(j, qg[:, 1:2], ["qgs"], qT[:, j, :], ("qT", j))
    headnorm(6, kg[:, 0:1], [("kg", 0), ("kg", 1)], ksT, "ksT")
    headnorm(7, kg[:, 0:1], [("kg", 0), ("kg", 1)], kwT, "kwT")
    kcb = m.alloc([2048], BF16)
    vcb = m.alloc([2048], BF16)
    k.dma("gpsimd", kcb, d["projF"][4], (), ["kcb"])
    k.dma("gpsimd", vcb, d["projF"][5], (), ["vcb"])
    k.memset("vector", vx, 1.0, ["vx"])
    vx5 = vx.rearrange("p st (a e) -> p st a e", e=65)
    for a in range(4):
        k.dma("gpsimd", vx5[:, :, a, 0:64], d["projT"][:, a * 64:(a + 1) * 64].rearrange("(st p) c -> p st c", p=128), ["vx"], [("vx", a)])
    vxk = ["vx"] + [("vx", a) for a in range(4)]
    k.dma("sync", gt, d["projT"][:, 256:280].rearrange("(tt p) c -> p tt c", p=128), (), ["gt"])
    k.act(gt, gt, AF.Sigmoid, ["gt"], ["gt"])
    if STOP <= 1:
        return
    W1 = m.alloc([2, 32, 128], BF16)
    W2 = m.alloc([2, 64], BF16)
    posT = m.alloc([2, 32], BF16)
    posF = m.alloc([2, 32])
    for half in range(2):
        pr = slice(64 * half, 64 * half + 64)
        for i in range(2):
            for dup in range(2):
                k.dma("gpsimd", W1[pr, i, :, dup * 64:(dup + 1) * 64], d["cmp_w1"][l, i].rearrange("(l d) e -> d l e", d=64), (), [("W1", half, i, dup)])
        k.dma("sync", posF[pr], d["cmp_pos"][l].rearrange("i l d -> d i l"), (), [("posF", half)])
        k.copy("vector", posT[pr], posF[pr], [("posF", half)], [("posT", half)])
    k.dma("gpsimd", W2[0:64], d["cmp_w2"][l].rearrange("i e f -> e i f"), (), ["W2"])
    wkeys = [("W1", a, b_, c_) for a in range(2) for b_ in range(2) for c_ in range(2)] + [("posT", 0), ("posT", 1), "W2"]
    if STOP <= 1.2:
        return
    hf = m.alloc([256])
    zb = m.alloc([32, 127], BF16)
    hid = m.alloc([256], BF16)
    t1 = m.alloc([256])
    t2 = m.alloc([256])
    kcT = m.alloc([128], BF16)
    k.memset("vector", kcT, 0.0, [("kcT", 0), ("kcT", 1)])
    kcn = m.alloc([256], BF16)
    rc = m.alloc([2, 97], BF16)
    k.memset("vector", rc, 1.0, ["rc"])
    ovf = m.alloc([32])
    k.dma("sync", ovf[0:127, :], d["ovc"], (), ["ovf"])
    for g in range(2):
        k.copy("vector", rc[0:127, g, 65:97], ovf[0:127, :], ["rc", "ovf"], [("rc", "ov", g)])
    if STOP <= 1.25:
        return
    for i, src, skey in ((0, kcb, "kcb"), (1, vcb, "vcb")):
        b = nb()
        sview = bass.AP(tensor=src.tensor, offset=src.offset, ap=[[src.ap[0][0], 128], [1, 32], [16, 127]])
        k.tt("vector", zb, sview, posT[:, i, :].unsqueeze(2).broadcast_to([128, 32, 127]), ALU.add, [skey, ("posT", 0), ("posT", 1)], ["zb"])
        if STOP <= 1.3:
            return
        for g in range(2):
            bg_ = nb()
            pr = slice(64 * g, 64 * g + 64)
            o_ = m.bank(bg_)[:, 0:127]
            for li in range(32):
                k.mm(o_, W1[pr, i, li, :], zb[pr, li, :], li == 0, li == 31, wkeys + ["zb"], [f"ps{bg_}"], signal=(li == 31))
            k.copy("vector", hf[0:64, g * 128:(g + 1) * 128], m.bank(bg_)[0:64, 0:128], [f"ps{bg_}"], [("hf", g)])
        if STOP <= 1.35:
            return
        k.tt("vector", hf[0:64, 0:1], hf[0:64, 0:1], hf[0:64, 0:1], ALU.max, [("hf", 0), ("hf", 1)], ["hf"])
        if STOP <= 1.4:
            return
        gelu_tanh(k, m, hid[0:64, :], hf[0:64, :], t1[0:64, :], t2[0:64, :], ["hf"], ["hid"], ["ct1"])
        if STOP <= 1.5:
            return
        if i == 0:
            b2 = nb()
            for g in range(2):
                k.mm(m.bank(b2)[0:64, g * 128:g * 128 + 127], W2[0:64, 0, :], hid[0:64, g * 128:g * 128 + 127], True, True, ["W2", "hid"], [f"ps{b2}"])
            k.copy("vector", hf[0:64, :], m.bank(b2)[0:64, 0:256], [f"ps{b2}"], ["hf"])
            k.act(sq[0:64, 0:256], hf[0:64, :], AF.Square, ["hf"], ["sq"])
            b3 = nb()
            k.mm(m.bank(b3)[0:64, 0:256], C.cb["ones"][0:64, 0:64], sq[0:64, 0:256], True, True, ["sq"] + C.keys, [f"ps{b3}"])
            k.act(t1[0:64, :], m.bank(b3)[0:64, 0:256], AF.Sqrt, [f"ps{b3}"] + C.keys, ["ct1"], scale=1.0 / HD, bias=C.eps6[0:64, 0:1])
            k.recip(t1[0:64, :], t1[0:64, :], ["ct1"], ["ct1"])
            k.stt(kcn[0:64, :], hf[0:64, :], kg[0:64, 0:1], t1[0:64, :], ALU.mult, ALU.mult, ["hf", "ct1", ("kg", 0)], ["kcn"])
            k.copy("vector", kcT[0:64, 0:127], kcn[0:64, 0:127], ["kcn"], [("kcT", 0)])
            k.copy("vector", kcT[64:128, 0:127], kcn[0:64, 128:255], ["kcn"], [("kcT", 1)])
        else:
            b2 = nb()
            for g in range(2):
                k.mm(m.bank(b2)[0:127, g * 64:(g + 1) * 64], hid[0:64, g * 128:g * 128 + 127], W2[0:64, 1, :], True, True, ["W2", "hid"], [f"ps{b2}"])
            k.copy("vector", rc[0:127, :, 0:64], m.bank(b2)[0:127, 0:128].rearrange("p (g e) -> p g e", e=64), [f"ps{b2}", "rc"], [("rc", "v")])
    rck = ["rc", ("rc", "ov", 0), ("rc", "ov", 1), ("rc", "v")]
    if STOP <= 2:
        return
    cacc4 = cacc.rearrange("p tt (h e) -> p tt h e", e=97)
    ecT = [m.alloc([512], BF16) for _ in range(2)]
    ei = 0
    for j in range(4):
        for g in range(2):
            h = 4 * g + j
            pr = slice(64 * g, 64 * g + 64)
            for tb in range(4):
                cs = slice(tb * 512, (tb + 1) * 512)
                b = nb()
                k.mm(m.bank(b)[:, :], kcT[pr, 0:128], qT[pr, j, cs], True, False, [("kcT", g), ("qT", j)], [f"ps{b}"], signal=False)
                k.mm(m.bank(b)[0:127, :], C.cb["ident"][0:127, 0:127], Wc[0:127, h, cs], False, True, [("Wc", h)] + C.keys, [f"ps{b}"])
                e_ = ecT[ei % 2]
                k.act(e_[0:127, :], m.bank(b)[0:127, :], AF.Exp, [f"ps{b}"], [f"ecT{ei % 2}"])
                for t4 in range(4):
                    tt = 4 * tb + t4
                    b2 = nb()
                    k.mm(m.bank(b2)[:, 0:97], e_[0:127, t4 * 128:(t4 + 1) * 128], rc[0:127, g, :], True, True, [f"ecT{ei % 2}"] + rck, [f"ps{b2}"])
                    k.copy("scalar" if t4 % 2 else "vector", cacc4[:, tt, h, :], m.bank(b2)[:, 0:97], [f"ps{b2}"], [("cacc", tt, h)])
                ei += 1
    if STOP <= 3:
        return
    k.ts("vector", rz, cacc4[:, :, :, 64], 1e-30, None, ALU.max, None, [("cacc", tt, h) for tt in range(16) for h in range(8)], ["rz"])
    k.recip(rz, rz, ["rz"], ["rz"])
    sc = m.alloc([32])
    sc2 = m.alloc([32])
    m8 = m.alloc([16])
    ngm = m.alloc([32])
    for tt in range(NTT):
        for g in range(2):
            for r in range(4):
                h = 4 * g + r
                if r == 0:
                    k.stt(sc, cacc4[:, tt, h, 65:97], rz[:, tt, h:h + 1], scadd[:, tt, :], ALU.mult, ALU.add, ["rz", "scadd"], ["sc"])
                else:
                    k.stt(sc, cacc4[:, tt, h, 65:97], rz[:, tt, h:h + 1], sc, ALU.mult, ALU.add, ["rz", "sc"], ["sc"])
            k.P.op("vector", (lambda o, i_: (lambda e: e.max(out=o, in_=i_)))(m8[:, 0:8], sc), ["sc"], ["m8a"])
            k.P.op("vector", (lambda o, r_, v_: (lambda e: e.match_replace(out=o, in_to_replace=r_, in_values=v_, imm_value=-3.0e38)))(sc2, m8[:, 0:8], sc), ["sc", "m8a"], ["sc2"])
            k.P.op("vector", (lambda o, i_: (lambda e: e.max(out=o, in_=i_)))(m8[:, 8:16], sc2), ["sc2"], ["m8b"])
            k.ts("vector", ngm, sc, m8[:, 15:16], NEG, ALU.is_lt, ALU.mult, ["sc", "m8b"], ["ngm"])
            b = nb()
            k.transpose(m.bank(b)[0:32, 0:128], ngm, C.cf["ident"], ["ngm"] + C.keys, [f"ps{b}"])
            k.copy("scalar", negT[0:32, g, tt * 128:(tt + 1) * 128], m.bank(b)[0:32, 0:128], [f"ps{b}"], [("negT", g, tt)])
    if STOP <= 4:
        return
    _barrier(k.P)
    m.top = mark
    for h in range(8):
        base = h * 128 * OH_L
        k.dma("gpsimd", Ws[:, h, :], bass.AP(tensor=Rt, offset=base + OH_SO + 127, ap=[[OH_L - 1, 128], [1, 1408]]), (), [("Ws", h)])
        k.dma("gpsimd", Ww[:, h, :], bass.AP(tensor=Rt, offset=base + OH_WO + 127, ap=[[OH_L - 1, 128], [1, 640]]), (), [("Ww", h)])
    oa = m.alloc([16, 512])
    PT = [m.alloc([512], BF16) for _ in range(3)]
    sacc = [m.alloc([4, 65]) for _ in range(2)]
    cf_ = m.alloc([16])
    tmp = m.alloc([4, 64])
    pi = 0
    ai = 0
    stage = m.alloc([4, 128])
    for tb in range(4):
        t0 = tb * 512
        for h in range(8):
            g = h // 4
            j = h % 4
            pr = slice(64 * g, 64 * g + 64)
            for br in range(2):
                bo = 4 + (ai % 2)
                bok = f"ps{bo}"
                bank_o = m.bank(bo)[:, 0:260].rearrange("p (a e) -> p a e", e=65)
                k.mm(m.bank(bo)[:, 0:260], C.zeros[:, 0:128], C.zeros[:, 0:260], True, False, [("zeros",)], [bok], signal=True)
                si_lo = 0 if br == 0 else max(0, 4 * tb - 4)
                si_list = list(range(si_lo, 4 * tb + 4))
                for si in si_list:
                    s0 = si * 128
                    c0 = max(0, s0 - t0)
                    c1 = 512 if br == 0 else min(512, s0 + 640 - t0)
                    cols = slice(c0, c1)
                    tcols = slice(t0 + c0, t0 + c1)
                    b = pi % 4
                    p_ = PT[pi % 3]
                    pk = f"PT{pi % 3}"
                    pi += 1
                    kT_ = ksT if br == 0 else kwT
                    W_ = Ws if br == 0 else Ww
                    m0 = t0 + c0 - s0
                    k.mm(m.bank(b)[:, cols], kT_[pr, s0:s0 + 128], qT[pr, j, tcols], True, False, ["ksT", "kwT", ("qT", j)], [f"ps{b}"], signal=False)
                    if br == 0 and m0 + (c1 - c0) > 1408:
                        for ca in range(c0, c1, 256):
                            cb_ = min(c1, ca + 256)
                            k.mm(m.bank(b)[:, ca:cb_], C.cb["ident"], W_[:, h, 1152:1152 + (cb_ - ca)], False, False, [("Ws", h)] + C.keys, [f"ps{b}"], signal=False)
                    else:
                        k.mm(m.bank(b)[:, cols], C.cb["ident"], W_[:, h, m0:m0 + (c1 - c0)], False, br == 1, [("Ws", h), ("Ww", h)] + C.keys, [f"ps{b}"], signal=(br == 1))
                    if br == 0:
                        k.mm(m.bank(b)[:, cols], esel[0:32, s0:s0 + 128], negT[0:32, g, tcols], False, True,
                             ["esel"] + [("negT", g, tt_) for tt_ in range(4 * tb, 4 * tb + 4)], [f"ps{b}"])
                    k.act(p_[:, cols], m.bank(b)[:, cols], AF.Exp, [f"ps{b}"], [pk])
                    for t4 in range(4):
                        if t4 * 128 < c0 or t4 * 128 >= c1:
                            continue
                        a = (0 if br == 0 else 1) * 2 + g
                        last = (si == si_list[-1]) and t4 == 3
                        k.mm(bank_o[:, t4, :], p_[:, t4 * 128:(t4 + 1) * 128], vx5[:, si, a, :], False, last, [pk] + vxk, [bok], signal=True)
                sa = sacc[ai % 2]
                sk = f"sacc{ai % 2}"
                ai += 1
                k.copy("vector", sa, bank_o, [bok], [sk])
                k.ts("vector", cf_[:, 0:4], sa[:, :, 64], 1e-30, None, ALU.max, None, [sk], ["cf"])
                k.recip(cf_[:, 0:4], cf_[:, 0:4], ["cf"], ["cf"])
                k.tt("vector", cf_[:, 4:8], cf_[:, 0:4], gt[:, 4 * tb:4 * tb + 4, 3 * h + 1 + br], ALU.mult, ["cf", "gt"], ["cf2"])
                dst = oa[:, 4 * tb:4 * tb + 4, h * 64:(h + 1) * 64]
                if br == 0:
                    k.tt("vector", cf_[:, 8:12], rz[:, 4 * tb:4 * tb + 4, h], gt[:, 4 * tb:4 * tb + 4, 3 * h], ALU.mult, ["rz", "gt"], ["cf3"])
                    k.tt("vector", dst, cacc4[:, 4 * tb:4 * tb + 4, h, 0:64], cf_[:, 8:12].unsqueeze(2).broadcast_to([128, 4, 64]), ALU.mult, ["cf3"], [("oa", tb, h)])
                k.tt("vector", tmp, sa[:, :, 0:64], cf_[:, 4:8].unsqueeze(2).broadcast_to([128, 4, 64]), ALU.mult, [sk, "cf2"], ["tmpo"])
                k.tt("vector", dst, dst, tmp, ALU.add, ["tmpo", ("oa", tb, h)], [("oa", tb, h)])
        for t4 in range(4):
            tt = 4 * tb + t4
            for c in range(4):
                b2 = nb() % 4
                k.transpose(m.bank(b2)[:, 384:512], oa[:, tt, c * 128:(c + 1) * 128], C.cf["ident"], [("oa", tb, h_) for h_ in (2 * c, 2 * c + 1)] + C.keys, [f"ps{b2}"])
                k.copy("scalar", stage[:, c, :], m.bank(b2)[:, 384:512], [f"ps{b2}"], [("stg", c)])
                k.dma("sync", d["mixF"][c][:, tt * 128:(tt + 1) * 128], stage[:, c, :], [("stg", c)], [("mixF", c, tt)])


_CST = None


def _get_cst():
    global _CST
    if _CST is None:
        c = host_consts()
        _CST = np.ascontiguousarray(np.concatenate([c[n] for n in ["ident", "ones", "blk64", "tril_qp", "lt_st", "uincl", "lstrict"]], axis=1).astype(np.float32))
    return _CST


def kernel(**inputs):
    x = np.asarray(inputs["x"], dtype=np.float32)
    nc = build()
    shared = {name: np.ascontiguousarray(np.asarray(inputs[name], dtype=np.float32)) for name, _ in INPUT_SPECS if name != "x"}
    shared["cst"] = _get_cst()
    shared.update(host_nsa_consts())
    in_maps = []
    for b in range(8):
        mp = dict(shared)
        mp["x"] = np.ascontiguousarray(x[b])
        in_maps.append(mp)
    res = run_bass_kernel_spmd(nc, in_maps, core_ids=list(range(8)))
    return np.stack([np.asarray(r["out"], dtype=np.float32) for r in res.results], axis=0)
```

```python
import math
import os
import numpy as np
import ml_dtypes
import concourse.bass as bass
import concourse.mybir as mybir
from concourse.bass_utils import run_bass_kernel_spmd

F32 = mybir.dt.float32
BF16 = mybir.dt.bfloat16
AF = mybir.ActivationFunctionType
ALU = mybir.AluOpType
AX = mybir.AxisListType

D_MODEL = 2048
T = 2048
DEPTH = 4
HD = 64
GW = 512
NH = 8
G = 2
R = 4
CMP_LEN = 32
CMP_STRIDE = 16
N_CMP = 127
SLC_LEN = 64
N_SLC = 32
SLC_TOP = 16
WINDOW = 512
N_BUCKETS = 32
MAX_DISTANCE = 1024
D_FF = 5632
W_IN_COLS = 5400
NEG = -30000.0
NKC = D_MODEL // 128
NTT = T // 128
NFC = D_FF // 128


class Prog:
    ENGS = ("tensor", "vector", "scalar", "gpsimd", "sync")

    def __init__(self, nc, n_dma_sems=10):
        self.nc = nc
        self.ops = {e: [] for e in self.ENGS}
        self.sem = {e: nc.alloc_semaphore(name=f"sem_{e}") for e in ("tensor", "vector", "scalar", "gpsimd")}
        self.cnt = {e: 0 for e in self.sem}
        nsem = {"sync": n_dma_sems, "gpsimd": 4}
        self.dma_sems = {q: [nc.alloc_semaphore(name=f"dsem_{q}{i}") for i in range(nsem[q])] for q in ("sync", "gpsimd")}
        self.dma_cnt = {q: [0] * nsem[q] for q in ("sync", "gpsimd")}
        self.dma_rr = {q: 0 for q in ("sync", "gpsimd")}
        self.waited = {e: {} for e in self.ENGS}
        self.last_w = {}
        self.readers = {}
        self.sem_by_id = {}
        self.pending = {}

    def _ev_id(self, sem):
        i = id(sem)
        self.sem_by_id[i] = sem
        return i

    def _need(self, eng, ev, waits):
        if ev is None:
            return
        sid, val, src_eng = ev
        if src_eng == "tensor" and eng == "tensor":
            return
        if self.waited[eng].get(sid, 0) >= val:
            return
        waits[sid] = max(waits.get(sid, 0), val)

    def op(self, eng, fn, reads=(), writes=(), signal=True):
        waits = {}
        for k in reads:
            self._need(eng, self.last_w.get(k), waits)
        for k in writes:
            self._need(eng, self.last_w.get(k), waits)
            for ev in self.readers.get(k, {}).values():
                self._need(eng, ev, waits)
        for sid, val in waits.items():
            self.waited[eng][sid] = val
        ev = None
        if signal:
            self.cnt[eng] += 1
            ev = (self._ev_id(self.sem[eng]), self.cnt[eng], eng)
        self.ops[eng].append((list(waits.items()), fn, (self.sem[eng], 1) if signal else None))
        if ev is not None:
            pr, pw = self.pending.pop(eng, ([], []))
            for k in list(reads) + pr:
                self.readers.setdefault(k, {})[ev[0]] = ev
            for k in list(writes) + pw:
                self.last_w[k] = ev
                self.readers[k] = {}
        else:
            assert eng == "tensor"
            pr, pw = self.pending.setdefault(eng, ([], []))
            pr.extend(reads)
            pw.extend(writes)
        return ev

    def dma(self, q, fn, reads=(), writes=(), war=()):
        waits = {}
        for k in war:
            for ev in self.readers.get(k, {}).values():
                self._need(q, ev, waits)
        for k in reads:
            self._need(q, self.last_w.get(k), waits)
        for k in writes:
            self._need(q, self.last_w.get(k), waits)
            for ev in self.readers.get(k, {}).values():
                self._need(q, ev, waits)
        i = self.dma_rr[q]
        self.dma_rr[q] = (i + 1) % len(self.dma_sems[q])
        s = self.dma_sems[q][i]
        sid = self._ev_id(s)
        prev = self.dma_cnt[q][i]
        if prev > 0 and self.waited[q].get(sid, 0) < prev:
            waits[sid] = max(waits.get(sid, 0), prev)
        for sd, val in waits.items():
            self.waited[q][sd] = val
        self.dma_cnt[q][i] = prev + 16
        ev = (sid, prev + 16, "dma_" + q)
        self.ops[q].append((list(waits.items()), fn, (s, 16)))
        for k in reads:
            self.readers.setdefault(k, {})[("d", q, i)] = ev
        for k in writes:
            self.last_w[k] = ev
            self.readers[k] = {}
        return ev

    def finish(self, out_keys):
        waits = {}
        for k in out_keys:
            self._need("sync", self.last_w.get(k), waits)
        final_waits = list(waits.items())
        nc = self.nc
        prog = self

        def emit(e, name):
            for waits_, fn, sig in prog.ops[name]:
                for sid, val in waits_:
                    e.wait_ge(prog.sem_by_id[sid], val)
                if fn is None:
                    continue
                inst = fn(e)
                if sig is not None:
                    inst.then_inc(sig[0], sig[1])

        with nc.Block() as block:
            @block.tensor
            def _(e):
                emit(e, "tensor")

            @block.vector
            def _(e):
                emit(e, "vector")

            @block.scalar
            def _(e):
                emit(e, "scalar")

            @block.gpsimd
            def _(e):
                emit(e, "gpsimd")

            @block.sync
            def _(e):
                emit(e, "sync")
                for q in ("sync", "gpsimd"):
                    for i, s_ in enumerate(prog.dma_sems[q]):
                        if prog.dma_cnt[q][i] > 0:
                            e.wait_ge(s_, prog.dma_cnt[q][i])


class K:
    def __init__(self, nc):
        self.nc = nc
        self.P = Prog(nc)
        self.ps_rr = 0

    def act(self, out, in_, func, reads, writes, **kw):
        return self.P.op("scalar", lambda e: e.activation(out=out, in_=in_, func=func, **kw), reads, writes)

    def ts(self, eng, out, in0, s1, s2, op0, op1, reads, writes, **kw):
        if op1 is None:
            return self.P.op(eng, lambda e: e.tensor_scalar(out=out, in0=in0, scalar1=s1, scalar2=None, op0=op0, **kw), reads, writes)
        return self.P.op(eng, lambda e: e.tensor_scalar(out=out, in0=in0, scalar1=s1, scalar2=s2, op0=op0, op1=op1, **kw), reads, writes)

    def tt(self, eng, out, in0, in1, op, reads, writes):
        return self.P.op(eng, lambda e: e.tensor_tensor(out=out, in0=in0, in1=in1, op=op), reads, writes)

    def stt(self, out, in0, scalar, in1, op0, op1, reads, writes):
        return self.P.op("vector", lambda e: e.scalar_tensor_tensor(out=out, in0=in0, scalar=scalar, in1=in1, op0=op0, op1=op1), reads, writes)

    def copy(self, eng, out, in_, reads, writes):
        if eng == "scalar":
            return self.P.op("scalar", lambda e: e.activation(out=out, in_=in_, func=AF.Copy), reads, writes)
        return self.P.op(eng, lambda e: e.tensor_copy(out=out, in_=in_), reads, writes)

    def recip(self, out, in_, reads, writes):
        return self.P.op("vector", lambda e: e.reciprocal(out=out, in_=in_), reads, writes)

    def memset(self, eng, ap, val, writes):
        return self.P.op(eng, lambda e: e.memset(ap, val), (), writes)

    def mm(self, out, lhsT, rhs, start, stop, reads, writes, signal=None, **kw):
        if signal is None:
            signal = stop
        return self.P.op("tensor", lambda e: e.matmul(out, lhsT, rhs, start=start, stop=stop, **kw), reads, writes, signal=signal)

    def transpose(self, out, in_, ident, reads, writes, signal=True):
        return self.P.op("tensor", lambda e: e.transpose(out, in_, ident), reads, writes, signal=signal)

    def dma(self, q, out, in_, reads, writes, war=()):
        return self.P.dma(q, lambda e: e.dma_start(out=out, in_=in_), reads, writes, war)


class Mem:
    def __init__(self, nc):
        self.big = nc.alloc_sbuf_tensor("big", [128, 192 * 256], F32)
        self.top = 0
        self.floor = 0
        self.ps = nc.alloc_psum_tensor("ps", [128, 8 * 512], F32)

    def alloc(self, free_shape, dtype=F32):
        n = int(np.prod(free_shape))
        words = n if dtype == F32 else (n + 1) // 2
        words = (words + 15) // 16 * 16
        a = self.top
        self.top += words
        assert self.top <= 192 * 256, f"SBUF overflow {self.top}"
        v = self.big[:, a:a + words]
        if dtype != F32:
            v = v.bitcast(dtype)
        v = v[:, 0:n]
        if len(free_shape) == 2:
            v = v.rearrange("p (a b) -> p a b", b=free_shape[1])
        elif len(free_shape) == 3:
            v = v.rearrange("p (a b c) -> p a b c", b=free_shape[1], c=free_shape[2])
        return v

    def set_floor(self):
        self.floor = self.top

    def reset(self):
        self.top = self.floor

    def bank(self, i, dtype=F32):
        v = self.ps[:, i * 512:(i + 1) * 512]
        if dtype != F32:
            v = v.bitcast(dtype)
        return v


def _barrier(P):
    evs = []
    for e, s in P.sem.items():
        if P.cnt[e] > 0:
            evs.append((P._ev_id(s), P.cnt[e]))
    for q in ("sync", "gpsimd"):
        for i, s in enumerate(P.dma_sems[q]):
            if P.dma_cnt[q][i] > 0:
                evs.append((P._ev_id(s), P.dma_cnt[q][i]))
    for eng in P.ENGS:
        waits = []
        for sid, val in evs:
            if P.waited[eng].get(sid, 0) < val:
                waits.append((sid, val))
                P.waited[eng][sid] = val
        if waits:
            P.ops[eng].append((waits, None, None))
    P.last_w = {}
    P.readers = {}


FM_GROUPS = [
    [(128 * j, 64 * j, 64) for j in range(4)] + [(128 * j + 64, 64 * (4 + j), 64) for j in range(4)],
    [(0, 512, 128), (128, 640, 128), (256, 768, 128), (384, 1024, 128)],
    [(0, 1304, 512)], [(0, 1816, 512)], [(0, 2328, 512)],
    [(0, 2840, 512)],
    [(0, 3864, 512)], [(0, 4376, 512)],
]
TM_BLOCKS = [
    (0, [(0, 896, 128), (128, 1152, 128), (256, 1280, 24)], 280),
    (280, [(0, 3352, 512)], 512),
    (792, [(0, 4888, 512)], 512),
]
N_FM = 32
N_TM = 1304


def _rel_bucket_np(n):
    n = np.maximum(n, 0)
    max_exact = N_BUCKETS // 2
    nf = np.maximum(n, 1).astype(np.float32)
    large = max_exact + (np.log(nf / np.float32(max_exact)) / np.float32(math.log(MAX_DISTANCE / max_exact)) * np.float32(N_BUCKETS - max_exact)).astype(np.int32)
    large = np.minimum(large, N_BUCKETS - 1)
    return np.where(n < max_exact, n, large)


def host_consts():
    c = {}
    c["ident"] = np.eye(128, dtype=np.float32)
    c["ones"] = np.ones((128, 128), np.float32)
    blk = np.zeros((128, 128), np.float32)
    blk[:64, :64] = 1
    blk[64:, 64:] = 1
    c["blk64"] = blk
    i = np.arange(128)
    c["tril_qp"] = (i[:, None] <= i[None, :]).astype(np.float32)
    c["lt_st"] = (i[:, None] < i[None, :]).astype(np.float32)
    c["uincl"] = (i[:, None] >= i[None, :]).astype(np.float32)
    c["lstrict"] = (i[:, None] < i[None, :]).astype(np.float32)
    return c


class Ctx:
    pass


def setup_consts(k, m, d, C):
    nc = k.nc
    C.cf = {}
    C.cb = {}
    names = ["ident", "ones", "blk64", "tril_qp", "lt_st", "uincl", "lstrict"]
    for i, n in enumerate(names):
        if n in ("ident", "tril_qp", "ones"):
            t = m.alloc([128])
            k.dma("sync", t, d["cst"][:, i * 128:(i + 1) * 128], (), [("cf", n)])
            C.cf[n] = t
        tb = m.alloc([128], BF16)
        k.dma("gpsimd", tb, d["cst"][:, i * 128:(i + 1) * 128], (), [("cb", n)])
        C.cb[n] = tb
    C.zeros = m.alloc([512], BF16)
    k.memset("vector", C.zeros, 0.0, [("zeros",)])
    C.eps6 = m.alloc([1])
    k.memset("vector", C.eps6, 1e-6, [("eps6",)])
    C.eps5 = m.alloc([1])
    k.memset("vector", C.eps5, 1e-5, [("eps5",)])
    C.one1 = m.alloc([1])
    k.memset("vector", C.one1, 1.0, [("one1",)])
    C.gmix = m.alloc([DEPTH * 16])
    C.gffn = m.alloc([DEPTH * 16])
    C.ggrp = m.alloc([DEPTH * 16])
    for nm, t in (("norm_mix", C.gmix), ("norm_ffn", C.gffn), ("group_gain", C.ggrp)):
        k.dma("sync", t, d[nm].rearrange("l (kc p) -> p (l kc)", p=128), (), [("g", nm)])
    C.keys = [("cf", n) for n in C.cf] + [("cb", n) for n in C.cb] + [("zeros",), ("eps6",), ("eps5",), ("one1",), ("g", "norm_mix"), ("g", "norm_ffn"), ("g", "group_gain")]


def phase_norm(k, m, C, x_ap, gvec, hT):
    xt = [m.alloc([2048]) for _ in range(2)]
    xs = [m.alloc([2048], BF16) for _ in range(2)]
    junk = m.alloc([2048], BF16)
    st = m.alloc([64])
    for tt in range(NTT):
        s = tt % 2
        k.dma("sync", xt[s], x_ap[tt * 128:(tt + 1) * 128, :], (), [f"xt{s}"])
        k.act(junk, xt[s], AF.Square, [f"xt{s}"], ["junk", ("ss", tt)], accum_out=st[:, tt:tt + 1])
        k.act(st[:, 16 + tt:17 + tt], st[:, tt:tt + 1], AF.Sqrt, [("ss", tt), ("eps6",)], [("sq", tt)], scale=1.0 / D_MODEL, bias=C.eps6[:, 0:1])
        k.recip(st[:, 32 + tt:33 + tt], st[:, 16 + tt:17 + tt], [("sq", tt)], [("rs", tt)])
        k.ts("vector", xs[s], xt[s], st[:, 32 + tt:33 + tt], None, ALU.mult, None, [f"xt{s}", ("rs", tt)], [f"xs{s}"])
        for half in range(2):
            pst = m.bank(6 + half, BF16)
            for j in range(8):
                kc = half * 8 + j
                k.transpose(pst[:, j * 128:(j + 1) * 128], xs[s][:, kc * 128:(kc + 1) * 128], C.cb["ident"],
                            [f"xs{s}", ("cb", "ident")], [f"pst{half}"], signal=(j == 7))
            k.tt("vector", hT[:, half * 8:(half + 1) * 8, tt * 128:(tt + 1) * 128],
                 pst.rearrange("p (a b) -> p a b", b=128),
                 gvec[:, half * 8:(half + 1) * 8].unsqueeze(2).broadcast_to([128, 8, 128]),
                 ALU.mult, [f"pst{half}"] + C.keys, [("hT", tt)])


def phase_inproj(k, m, C, d, l, hT):
    w_l = d["w_in"][l].rearrange("(kc p) c -> p kc c", p=128)
    wt = [m.alloc([16, 512], BF16) for _ in range(2)]
    stage = [m.alloc([2048]) for _ in range(3)]
    ci = 0
    bi = 0
    for g, segs in enumerate(FM_GROUPS):
        s = g % 2
        for (dst, src, n) in segs:
            k.dma("gpsimd", wt[s][:, :, dst:dst + n], w_l[:, :, src:src + n], (), [(f"wt{s}", dst)], war=[f"wtall{s}"])
        for c in range(4):
            segkeys = [f"wtall{s}"] + [(f"wt{s}", dst) for (dst, src, n) in segs if dst < (c + 1) * 128 and dst + n > c * 128]
            sg = stage[ci % 3]
            for tb in range(4):
                b = bi % 6
                bi += 1
                bank = m.bank(b)
                for kc in range(16):
                    k.mm(bank, wt[s][:, kc, c * 128:(c + 1) * 128], hT[:, kc, tb * 512:(tb + 1) * 512], kc == 0, kc == 15,
                         segkeys + [("hT", 4 * tb + i) for i in range(4)], [f"ps{b}"])
                k.copy("scalar" if (bi % 2) else "vector", sg[:, tb * 512:(tb + 1) * 512], bank, [f"ps{b}"], [(f"stage{ci % 3}", tb)])
            k.dma("sync", d["projF"][4 * g + c], sg, [(f"stage{ci % 3}", tb) for tb in range(4)], [("projF", 4 * g + c)])
            ci += 1
    for bidx, (col0, segs, ncols) in enumerate(TM_BLOCKS):
        s = bidx % 2
        for (dst, src, n) in segs:
            k.dma("gpsimd", wt[s][:, :, dst:dst + n], w_l[:, :, src:src + n], (), [(f"wt{s}", dst)], war=[f"wtall{s}"])
        segkeys = [f"wtall{s}"] + [(f"wt{s}", dst) for (dst, src, n) in segs]
        for tt in range(NTT):
            b = bi % 6
            bi += 1
            bank = m.bank(b)
            sg = stage[ci % 3]
            for kc in range(16):
                k.mm(bank[:, 0:ncols], hT[:, kc, tt * 128:(tt + 1) * 128], wt[s][:, kc, 0:ncols], kc == 0, kc == 15,
                     segkeys + [("hT", tt)], [f"ps{b}"])
            k.copy("scalar" if (bi % 2) else "vector", sg[:, 0:ncols], bank[:, 0:ncols], [f"ps{b}"], [(f"stage{ci % 3}", 0)])
            k.dma("sync", d["projT"][tt * 128:(tt + 1) * 128, col0:col0 + ncols], sg[:, 0:ncols], [(f"stage{ci % 3}", 0)], [("projT", bidx, tt)])
            ci += 1


def phase_wout(k, m, C, d, l, x_in, x_out):
    mixT = m.alloc([16, 2048], BF16)
    xg = [m.alloc([4, 2048]) for _ in range(1)]
    sq = m.alloc([4, 2048], BF16)
    tmp = [m.alloc([512]) for _ in range(2)]
    bi = 0
    for grp in range(4):
        x4 = xg[0]
        for c in range(4):
            k.dma("sync", x4[:, c, :], d["mixF"][4 * grp + c], (), [("xg", c)])
            k.act(sq[:, c, :], x4[:, c, :], AF.Square, [("xg", c)], [("sq", c)])
        for tb in range(4):
            b = bi % 6
            bi += 1
            bank = m.bank(b)
            for c in range(4):
                k.mm(bank, C.cb["ones"], sq[:, c, tb * 512:(tb + 1) * 512], c == 0, c == 3, [("sq", c)] + C.keys, [f"ps{b}"])
            t_ = tmp[tb % 2]
            k.act(t_, bank, AF.Sqrt, [f"ps{b}"] + C.keys, [f"tmp{tb % 2}"], scale=1.0 / GW, bias=C.eps6[:, 0:1])
            k.recip(t_, t_, [f"tmp{tb % 2}"], [f"tmp{tb % 2}"])
            for c in range(4):
                ch = 4 * grp + c
                k.stt(mixT[:, ch, tb * 512:(tb + 1) * 512], x4[:, c, tb * 512:(tb + 1) * 512], C.ggrp[:, l * 16 + ch:l * 16 + ch + 1], t_,
                      ALU.mult, ALU.mult, [("xg", c), f"tmp{tb % 2}"] + C.keys, [("mixT", ch, tb)])
    w_l = d["w_out"][l].rearrange("(kc p) n -> p kc n", p=128)
    wt = [m.alloc([16, 512], BF16) for _ in range(2)]
    xin = [m.alloc([512]) for _ in range(3)]
    xi = 0
    for nb in range(4):
        s = nb % 2
        k.dma("gpsimd", wt[s], w_l[:, :, nb * 512:(nb + 1) * 512], (), [f"wo{s}"])
        for tt in range(NTT):
            b = bi % 6
            bi += 1
            bank = m.bank(b)
            xs_ = xin[xi % 3]
            k.dma("sync", xs_, x_in[tt * 128:(tt + 1) * 128, nb * 512:(nb + 1) * 512], (), [f"xin{xi % 3}"])
            for kc in range(16):
                k.mm(bank, mixT[:, kc, tt * 128:(tt + 1) * 128], wt[s][:, kc, :], kc == 0, kc == 15,
                     [f"wo{s}", ("mixT", kc, tt // 4)], [f"ps{b}"])
            k.tt("vector", xs_, bank, xs_, ALU.add, [f"ps{b}", f"xin{xi % 3}"], [f"xin{xi % 3}"])
            k.dma("sync", x_out[tt * 128:(tt + 1) * 128, nb * 512:(nb + 1) * 512], xs_, [f"xin{xi % 3}"], [("xout", nb, tt)])
            xi += 1


def phase_ffn_up(k, m, C, d, l, hT):
    wg_l = d["w_ffn_gate"][l].rearrange("(kc p) f -> p kc f", p=128)
    wu_l = d["w_ffn_up"][l].rearrange("(kc p) f -> p kc f", p=128)
    wg = [m.alloc([16, 512], BF16) for _ in range(2)]
    wu = [m.alloc([16, 512], BF16) for _ in range(2)]
    act = [m.alloc([2048], BF16) for _ in range(3)]
    sil = [m.alloc([512]) for _ in range(2)]
    bi = 0
    ai = 0
    for fg in range(NFC // 4):
        s = fg % 2
        k.dma("gpsimd", wg[s], wg_l[:, :, fg * 512:(fg + 1) * 512], (), [f"wg{s}"])
        k.dma("gpsimd", wu[s], wu_l[:, :, fg * 512:(fg + 1) * 512], (), [f"wu{s}"])
        for c in range(4):
            fc = fg * 4 + c
            a_ = act[ai % 3]
            for tb in range(4):
                bg = bi % 6
                bu = (bi + 1) % 6
                bi += 2
                hk = [("hT", 4 * tb + i) for i in range(4)]
                for kc in range(16):
                    k.mm(m.bank(bg), wg[s][:, kc, c * 128:(c + 1) * 128], hT[:, kc, tb * 512:(tb + 1) * 512], kc == 0, kc == 15, [f"wg{s}"] + hk, [f"ps{bg}"])
                for kc in range(16):
                    k.mm(m.bank(bu), wu[s][:, kc, c * 128:(c + 1) * 128], hT[:, kc, tb * 512:(tb + 1) * 512], kc == 0, kc == 15, [f"wu{s}"] + hk, [f"ps{bu}"])
                s_ = sil[tb % 2]
                k.act(s_, m.bank(bg), AF.Silu, [f"ps{bg}"], [f"sil{tb % 2}"])
                k.tt("vector", a_[:, tb * 512:(tb + 1) * 512], m.bank(bu), s_, ALU.mult, [f"ps{bu}", f"sil{tb % 2}"], [(f"act{ai % 3}", tb)])
            k.dma("sync", d["actD"][fc], a_, [(f"act{ai % 3}", tb) for tb in range(4)], [("actD", fc)])
            ai += 1


def phase_ffn_down(k, m, C, d, l, x_in, x_out):
    wd_l = d["w_ffn_down"][l].rearrange("(fc p) n -> p fc n", p=128)
    actv = d["actD"].rearrange("fc p t -> p fc t")
    wd = [m.alloc([NFC, 512], BF16) for _ in range(2)]
    ab = [m.alloc([NFC, 512], BF16) for _ in range(2)]
    xin = [m.alloc([512]) for _ in range(3)]
    bi = 0
    xi = 0
    ai = 0
    for nbi, nb in enumerate([int(c) for c in os.environ.get("NB_ORDER", "0123")]):
        s = nbi % 2
        for h in range(2):
            k.dma("gpsimd", wd[s][:, h * 22:(h + 1) * 22, :], wd_l[:, h * 22:(h + 1) * 22, nb * 512:(nb + 1) * 512], (), [(f"wd{s}", h)])
        for tg in range(4):
            a_ = ab[ai % 2]
            for h in range(2):
                k.dma("sync", a_[:, h * 22:(h + 1) * 22, :], actv[:, h * 22:(h + 1) * 22, tg * 512:(tg + 1) * 512], (), [(f"ab{ai % 2}", h)])
            for t4 in range(4):
                tt = tg * 4 + t4
                b = bi % 6
                bi += 1
                bank = m.bank(b)
                xs_ = xin[xi % 3]
                k.dma("sync", xs_, x_in[tt * 128:(tt + 1) * 128, nb * 512:(nb + 1) * 512], (), [f"xin{xi % 3}"])
                for fc in range(NFC):
                    k.mm(bank, a_[:, fc, t4 * 128:(t4 + 1) * 128], wd[s][:, fc, :], fc == 0, fc == NFC - 1,
                         [(f"wd{s}", fc // 22), (f"ab{ai % 2}", fc // 22)], [f"ps{b}"])
                k.tt("vector", xs_, bank, xs_, ALU.add, [f"ps{b}", f"xin{xi % 3}"], [f"xin{xi % 3}"])
                k.dma("sync", x_out[tt * 128:(tt + 1) * 128, nb * 512:(nb + 1) * 512], xs_, [f"xin{xi % 3}"], [("xout", nb, tt)])
                xi += 1
            ai += 1


INPUT_SPECS = [
    ("x", [T, D_MODEL]), ("w_in", [DEPTH, D_MODEL, W_IN_COLS]), ("w_out", [DEPTH, D_MODEL, D_MODEL]),
    ("norm_mix", [DEPTH, D_MODEL]), ("norm_ffn", [DEPTH, D_MODEL]), ("q_gain", [DEPTH, HD]), ("k_gain", [DEPTH, HD]),
    ("cmp_pos", [DEPTH, 2, CMP_LEN, HD]), ("cmp_w1", [DEPTH, 2, CMP_LEN * HD, HD]), ("cmp_w2", [DEPTH, 2, HD, HD]),
    ("rel_table", [N_BUCKETS, NH]), ("conv_w", [DEPTH, 3, GW]), ("sgu_w", [DEPTH, NH, 128, 128]), ("sgu_b", [DEPTH, NH, 128]),
    ("group_gain", [DEPTH, D_MODEL]), ("w_ffn_gate", [DEPTH, D_MODEL, D_FF]), ("w_ffn_up", [DEPTH, D_MODEL, D_FF]),
    ("w_ffn_down", [DEPTH, D_FF, D_MODEL]),
]
N_CST = 7


def build(n_layers=DEPTH, dbg=(), phases=None, mix=("conv", "sgu", "sb", "nsa")):
    nc = bass.Bass("TRN2", target_bir_lowering=False)
    d = {}
    for name, shape in INPUT_SPECS:
        d[name] = nc.dram_tensor(name, shape, F32, kind="ExternalInput").ap()
    d["cst"] = nc.dram_tensor("cst", [128, N_CST * 128], F32, kind="ExternalInput").ap()

    def scratch(name, shape, dt=F32):
        kind = "ExternalOutput" if name in dbg else "Internal"
        d[name] = nc.dram_tensor(name, shape, dt, kind=kind).ap()

    d["oh"] = nc.dram_tensor("oh", [33, OH_L], F32, kind="ExternalInput").ap()
    d["scadd"] = nc.dram_tensor("scadd", [T, N_SLC], F32, kind="ExternalInput").ap()
    d["esel"] = nc.dram_tensor("esel", [N_SLC, T], F32, kind="ExternalInput").ap()
    d["ovc"] = nc.dram_tensor("ovc", [N_CMP, N_SLC], F32, kind="ExternalInput").ap()
    scratch("R", [8, 128, OH_L])
    scratch("projF", [N_FM, 128, T])
    scratch("projT", [T, N_TM])
    scratch("mixF", [16, 128, T])
    scratch("actD", [NFC, 128, T], BF16)
    scratch("xa", [T, D_MODEL])
    scratch("xb", [T, D_MODEL])
    d["out"] = nc.dram_tensor("out", [T, D_MODEL], F32, kind="ExternalOutput").ap()

    k = K(nc)
    m = Mem(nc)
    C = Ctx()
    with nc.allow_non_contiguous_dma(reason="small parameter loads"):
        setup_consts(k, m, d, C)
        m.set_floor()
        _barrier(k.P)
        if "nsa" in mix:
            setup_nsa_tables(k, m, C, d)
            _barrier(k.P)
        x_cur = d["x"]
        for l in range(n_layers):
            ph = phases if phases is not None else ("norm1", "inproj", "mixers", "wout", "ffn")
            x2 = d["out"] if l == n_layers - 1 else d["xb"]
            if "inproj" in ph:
                m.reset()
                hT = m.alloc([16, T], BF16)
                phase_norm(k, m, C, x_cur, C.gmix[:, l * 16:(l + 1) * 16], hT)
                phase_inproj(k, m, C, d, l, hT)
                _barrier(k.P)
            if "mixers" in ph:
                phase_mixers(k, m, C, d, l, which=mix)
            if "wout" in ph:
                m.reset()
                phase_wout(k, m, C, d, l, x_cur, d["xa"])
                _barrier(k.P)
            if "ffn" in ph:
                m.reset()
                hT = m.alloc([16, T], BF16)
                phase_norm(k, m, C, d["xa"], C.gffn[:, l * 16:(l + 1) * 16], hT)
                phase_ffn_up(k, m, C, d, l, hT)
                _barrier(k.P)
                m.reset()
                phase_ffn_down(k, m, C, d, l, d["xa"], x2)
                _barrier(k.P)
            x_cur = x2
        k.P.finish([])
    return nc


def phase_conv(k, m, C, d, l):
    cw = m.alloc([12])
    k.dma("sync", cw.rearrange("p (w j) -> p w j", j=4), d["conv_w"][l].rearrange("w (j p) -> p w j", p=128), (), ["cw"])
    bg = [m.alloc([2048]) for _ in range(2)]
    cg = [m.alloc([2048]) for _ in range(2)]
    hh = [m.alloc([2048]) for _ in range(2)]
    z = [m.alloc([2050]) for _ in range(2)]
    y = [m.alloc([2048]) for _ in range(2)]
    for s in range(2):
        k.memset("vector", z[s][:, 0:2], 0.0, [f"z{s}"])
    for j in range(4):
        s = j % 2
        k.dma("sync", bg[s], d["projF"][8 + j], (), [f"bg{s}"])
        k.dma("sync", cg[s], d["projF"][12 + j], (), [f"cg{s}"])
        k.dma("sync", hh[s], d["projF"][16 + j], (), [f"hh{s}"])
        k.tt("gpsimd", z[s][:, 2:2050], cg[s], hh[s], ALU.mult, [f"cg{s}", f"hh{s}"], [f"z{s}"])
        k.ts("vector", y[s], z[s][:, 2:2050], cw[:, 8 + j:9 + j], None, ALU.mult, None, [f"z{s}", "cw"], [f"y{s}"])
        k.stt(y[s], z[s][:, 1:2049], cw[:, 4 + j:5 + j], y[s], ALU.mult, ALU.add, [f"z{s}", "cw", f"y{s}"], [f"y{s}"])
        k.stt(y[s], z[s][:, 0:2048], cw[:, j:j + 1], y[s], ALU.mult, ALU.add, [f"z{s}", "cw", f"y{s}"], [f"y{s}"])
        k.tt("vector", y[s], y[s], bg[s], ALU.mult, [f"y{s}", f"bg{s}"], [f"y{s}"])
        k.dma("sync", d["mixF"][4 + j], y[s], [f"y{s}"], [("mixF", 4 + j)])


def gelu_tanh(k, m, out, x, tmp, tmp2, rk, wk, tk):
    c = 1.5957691216057308
    k.tt("vector", tmp, x, x, ALU.mult, rk, tk)
    k.ts("vector", tmp, tmp, 0.044715 * c, c, ALU.mult, ALU.add, tk, tk)
    k.tt("vector", tmp, tmp, x, ALU.mult, rk + tk, tk)
    k.act(tmp2, tmp, AF.Sigmoid, tk, [tk[0] + "_2"])
    k.tt("vector", out, x, tmp2, ALU.mult, rk + [tk[0] + "_2"], wk)


def phase_sgu(k, m, C, d, l):
    wraw = m.alloc([8, 128])
    k.dma("sync", wraw, d["sgu_w"][l].rearrange("h p q -> p h q"), (), ["wraw"])
    wT = m.alloc([8, 128], BF16)
    bsb = m.alloc([8])
    k.dma("sync", bsb, d["sgu_b"][l].rearrange("h p -> p h"), (), ["bsb"])
    for h in range(8):
        b = h % 2
        k.transpose(m.bank(b)[:, 0:128], wraw[:, h, :], C.cf["ident"], ["wraw"] + C.keys, [f"ps{b}"])
        k.tt("vector", wT[:, h, :], m.bank(b)[:, 0:128], C.cf["tril_qp"], ALU.mult, [f"ps{b}"] + C.keys, [("wT", h)])
    uv = [m.alloc([1024]) for _ in range(2)]
    t1 = m.alloc([1024])
    t2 = m.alloc([1024])
    gl = [m.alloc([1024]) for _ in range(2)]
    vln = [m.alloc([512], BF16) for _ in range(2)]
    st = m.alloc([16])
    oc = [m.alloc([512]) for _ in range(2)]
    stage = [m.alloc([4, 512]) for _ in range(2)]
    for tt in range(NTT):
        s = tt % 2
        for c in range(4):
            k.dma("sync", stage[s][:, c, 0:128], d["projF"][20 + c][:, tt * 128:(tt + 1) * 128], (), [(f"ufm{s}", c)])
        for c in range(4):
            b = 2 + c % 2
            k.transpose(m.bank(b)[:, 0:128], stage[s][:, c, 0:128], C.cf["ident"], [(f"ufm{s}", c)] + C.keys, [f"ps{b}"])
            k.copy("scalar", uv[s][:, c * 128:(c + 1) * 128], m.bank(b)[:, 0:128], [f"ps{b}"], [(f"uv{s}", c)])
        k.dma("sync", uv[s][:, 512:1024], d["projT"][tt * 128:(tt + 1) * 128, 280:792], (), [(f"uv{s}", 4)])
        gelu_tanh(k, m, gl[s], uv[s], t1, t2, [(f"uv{s}", c) for c in range(5)], [f"gl{s}"], ["t1"])
        k.P.op("vector", (lambda o, i: (lambda e: e.bn_stats(out=o, in_=i)))(st[:, 0:6], gl[s][:, 512:1024]), [f"gl{s}"], ["bst"])
        k.P.op("vector", (lambda o, i: (lambda e: e.bn_aggr(out=o, in_=i)))(st[:, 8:10], st[:, 0:6]), ["bst"], ["bag"])
        k.act(st[:, 10:11], st[:, 9:10], AF.Sqrt, ["bag"] + C.keys, ["lnsd"], scale=1.0, bias=C.eps5[:, 0:1])
        k.recip(st[:, 11:12], st[:, 10:11], ["lnsd"], ["lnrs"])
        k.ts("vector", vln[s], gl[s][:, 512:1024], st[:, 8:9], st[:, 11:12], ALU.subtract, ALU.mult, [f"gl{s}", "bag", "lnrs"], [f"vln{s}"])
        b = 4 + tt % 2
        for h in range(8):
            k.mm(m.bank(b)[:, h * 64:(h + 1) * 64], wT[:, h, :], vln[s][:, h * 64:(h + 1) * 64], True, True,
                 [("wT", h), f"vln{s}"], [f"ps{b}"], signal=(h == 7))
        k.tt("vector", oc[s].rearrange("p (h e) -> p h e", e=64), m.bank(b).rearrange("p (h e) -> p h e", e=64),
             bsb.unsqueeze(2).broadcast_to([128, 8, 64]), ALU.add, [f"ps{b}", "bsb"], [f"oc{s}"])
        k.tt("vector", oc[s], oc[s], gl[s][:, 0:512], ALU.mult, [f"oc{s}", f"gl{s}"], [f"oc{s}"])
        for c in range(4):
            b2 = 2 + c % 2
            k.transpose(m.bank(b2)[:, 128:256], oc[s][:, c * 128:(c + 1) * 128], C.cf["ident"], [f"oc{s}"] + C.keys, [f"ps{b2}"])
            k.copy("scalar", stage[s][:, c, 128:256], m.bank(b2)[:, 128:256], [f"ps{b2}"], [(f"ofm{s}", c)])
            k.dma("sync", d["mixF"][8 + c][:, tt * 128:(tt + 1) * 128], stage[s][:, c, 128:256], [(f"ofm{s}", c)], [("mixF", 8 + c, tt)])


def phase_sb(k, m, C, d, l):
    scale = HD ** -0.5
    qf = m.alloc([2048])
    kf = m.alloc([2048])
    qs = [m.alloc([2048], BF16) for _ in range(2)]
    qn = [m.alloc([2048], BF16) for _ in range(2)]
    kb = [m.alloc([2048], BF16) for _ in range(2)]
    vb = m.alloc([16, 512], BF16)
    k.dma("gpsimd", vb, d["projT"][:, 792:1304].rearrange("(st p) c -> p st c", p=128), (), ["vb"])
    ef = [[m.alloc([512]) for _ in range(2)] for _ in range(2)]
    sp = [[m.alloc([512], BF16) for _ in range(2)] for _ in range(2)]
    aT = [[m.alloc([512], BF16) for _ in range(2)] for _ in range(2)]
    osb = [[m.alloc([512]) for _ in range(2)] for _ in range(2)]
    for j in range(4):
        jj = j % 2
        k.dma("sync", qf, d["projF"][24 + j], (), ["qf"])
        k.dma("sync", kf, d["projF"][28 + j], (), ["kf"])
        k.ts("vector", qs[jj], qf, scale, None, ALU.mult, None, ["qf"], [f"qs{jj}"])
        k.ts("gpsimd", qn[jj], qf, -scale, None, ALU.mult, None, ["qf"], [f"qn{jj}"])
        k.copy("gpsimd", kb[jj], kf, ["kf"], [f"kb{jj}"])
        QS, QN, KB = qs[jj], qn[jj], kb[jj]
        qsk, qnk, kbk = f"qs{jj}", f"qn{jj}", f"kb{jj}"
        for tb in range(4):
            t0 = tb * 512
            steps = list(range(4 * tb + 3, -1, -1))
            for ch in range(2):
                k.mm(m.bank(3 * ch + 1), C.zeros[:, 0:128], C.zeros, True, False, [("zeros",)], [f"ps{3 * ch + 1}"], signal=True)
                k.mm(m.bank(3 * ch + 2)[0:64, :], C.zeros[:, 0:64], C.zeros, True, False, [("zeros",)], [f"ps{3 * ch + 2}"], signal=True)

            def geom(si):
                s0 = si * 128
                c0 = max(0, s0 - t0)
                return s0, c0, s0 >= t0, slice(c0, 512), slice(t0 + c0, t0 + 512)

            def sp_stage(ch, n):
                si = steps[n]
                s0, c0, diag, cols, tcols = geom(si)
                pr = slice(64 * ch, 64 * ch + 64)
                sl = n % 2
                bz = m.bank(3 * ch)
                zk = f"ps{3 * ch}"
                k.mm(bz[:, cols], KB[pr, s0:s0 + 128], QS[pr, tcols], True, True, [kbk, qsk], [zk])
                k.act(ef[ch][sl][:, cols], bz[:, cols], AF.Exp, [zk], [f"ef{ch}{sl}"])
                k.act(sp[ch][sl][:, cols], ef[ch][sl][:, cols], AF.Ln, [f"ef{ch}{sl}"] + C.keys, [f"sp{ch}{sl}"], bias=C.one1[:, 0:1], scale=1.0)
                if diag:
                    k.tt("gpsimd", sp[ch][sl][:, c0:c0 + 128], sp[ch][sl][:, c0:c0 + 128], C.cb["lt_st"], ALU.mult, [f"sp{ch}{sl}"] + C.keys, [f"sp{ch}{sl}"])

            def chain_a(ch, n):
                si = steps[n]
                s0, c0, diag, cols, tcols = geom(si)
                pr = slice(64 * ch, 64 * ch + 64)
                sl = n % 2
                bc = m.bank(3 * ch + 1)
                ck = f"ps{3 * ch + 1}"
                k.mm(bc[:, cols], C.cb["uincl"], sp[ch][sl][:, cols], False, False, [f"sp{ch}{sl}"] + C.keys, [ck], signal=False)
                k.mm(bc[:, cols], KB[pr, s0:s0 + 128], QN[pr, tcols], False, False, [kbk, qnk], [ck], signal=True)

            def chain_b(ch, n):
                si = steps[n]
                s0, c0, diag, cols, tcols = geom(si)
                sl = n % 2
                bc = m.bank(3 * ch + 1)
                ck = f"ps{3 * ch + 1}"
                k.act(aT[ch][sl][:, cols], bc[:, cols], AF.Exp, [ck], [f"aT{ch}{sl}"], scale=-1.0)
                if diag:
                    k.tt("gpsimd", aT[ch][sl][:, c0:c0 + 128], aT[ch][sl][:, c0:c0 + 128], C.cb["lt_st"], ALU.mult, [f"aT{ch}{sl}"] + C.keys, [f"aT{ch}{sl}"])

            def chain_c(ch, n):
                si = steps[n]
                s0, c0, diag, cols, tcols = geom(si)
                pr = slice(64 * ch, 64 * ch + 64)
                sl = n % 2
                h = 2 * j + ch
                bc = m.bank(3 * ch + 1)
                ck = f"ps{3 * ch + 1}"
                bo = m.bank(3 * ch + 2)
                ok_ = f"ps{3 * ch + 2}"
                k.mm(bc[:, cols], KB[pr, s0:s0 + 128], QS[pr, tcols], False, False, [kbk, qsk], [ck], signal=False)
                k.mm(bc[:, cols], C.cb["lstrict"], sp[ch][sl][:, cols], False, si == 0, [f"sp{ch}{sl}"] + C.keys, [ck], signal=True)
                k.mm(bo[0:64, cols], vb[:, si, h * 64:(h + 1) * 64], aT[ch][sl][:, cols], False, si == 0, ["vb", f"aT{ch}{sl}"], [ok_], signal=True)

            ns = len(steps)
            for n in range(ns + 1):
                if n < ns:
                    for ch in range(2):
                        sp_stage(ch, n)
                if n >= 1:
                    for ch in range(2):
                        chain_a(ch, n - 1)
                    for ch in range(2):
                        chain_b(ch, n - 1)
                    for ch in range(2):
                        chain_c(ch, n - 1)
            for ch in range(2):
                o_ = osb[ch][tb % 2]
                ok_ = f"ps{3 * ch + 2}"
                k.copy("vector", o_[0:64, :], m.bank(3 * ch + 2)[0:64, :], [ok_], [f"osb{ch}{tb % 2}"])
                k.dma("sync", d["mixF"][12 + j][64 * ch:64 * ch + 64, t0:t0 + 512], o_[0:64, :], [f"osb{ch}{tb % 2}"], [("mixF", 12 + j, ch, tb)])


def phase_mixers(k, m, C, d, l, which=("conv", "sgu", "sb", "nsa")):
    if "conv" in which:
        m.reset()
        phase_conv(k, m, C, d, l)
        _barrier(k.P)
    if "sgu" in which:
        m.reset()
        phase_sgu(k, m, C, d, l)
        _barrier(k.P)
    if "sb" in which:
        m.reset()
        phase_sb(k, m, C, d, l)
        _barrier(k.P)
    if "nsa" in which:
        m.reset()
        phase_nsa(k, m, C, d, l)
        _barrier(k.P)


OH_L = 6366
OH_SO, OH_WO, OH_CO = 0, 1535, 2302


def host_nsa_consts():
    oh = np.zeros((33, OH_L), np.float32)
    x = np.arange(1535) - 127
    b = np.where(x < 0, 32, _rel_bucket_np(x))
    oh[b, OH_SO + np.arange(1535)] = 1
    x = np.arange(767) - 127
    b = np.where((x < 0) | (x >= WINDOW), 32, _rel_bucket_np(x))
    oh[b, OH_WO + np.arange(767)] = 1
    x = np.arange(4064) - 2016 - 31
    b = np.where(x < 0, 32, _rel_bucket_np(x))
    oh[b, OH_CO + np.arange(4064)] = 1
    t = np.arange(T)[:, None]
    jj = np.arange(N_SLC)[None, :]
    cur = t // SLC_LEN
    valid = jj * SLC_LEN <= t
    forced = (jj == 0) | (jj == cur) | (jj == cur - 1)
    scadd = np.where(valid, np.where(forced, 1000.0, 0.0), -1e30).astype(np.float32)
    esel = (np.arange(T)[None, :] // SLC_LEN == np.arange(N_SLC)[:, None]).astype(np.float32)
    c0 = np.arange(N_CMP)[:, None] * CMP_STRIDE
    s0 = np.arange(N_SLC)[None, :] * SLC_LEN
    ov = np.minimum(c0 + CMP_LEN, s0 + SLC_LEN) - np.maximum(c0, s0)
    ovc = (np.maximum(ov, 0) / CMP_LEN).astype(np.float32)
    return {"oh": oh, "scadd": scadd, "esel": esel, "ovc": ovc}


def setup_nsa_tables(k, m, C, d):
    tabx = m.alloc([8])
    k.memset("vector", tabx[0:64, :], NEG, ["tabx"])
    k.dma("sync", tabx[0:32, :], d["rel_table"], (), ["tabx"])
    oh = m.alloc([OH_L])
    k.dma("sync", oh[0:33, :], d["oh"], (), ["oh"])
    row = [m.alloc([OH_L]) for _ in range(2)]
    bi = 0
    for h in range(8):
        r_ = row[h % 2]
        for c0 in range(0, OH_L, 512):
            n = min(512, OH_L - c0)
            b = bi % 6
            bi += 1
            k.mm(m.bank(b)[:, 0:n], tabx[0:33, h:h + 1].broadcast_to([33, 128]), oh[0:33, c0:c0 + n], True, True, ["tabx", "oh"], [f"ps{b}"])
            k.copy("scalar" if bi % 2 else "vector", r_[:, c0:c0 + n], m.bank(b)[:, 0:n], [f"ps{b}"], [(f"row{h % 2}", c0)])
        k.dma("sync", d["R"][h], r_, [(f"row{h % 2}", c0) for c0 in range(0, OH_L, 512)], [("R", h)], war=[f"rowall{h % 2}"])


def phase_nsa(k, m, C, d, l):
    scale = HD ** -0.5
    STOP = float(os.environ.get("NSA_STOP", "99"))
    if STOP <= 0:
        return
    Rt = d["R"].tensor
    tabW = m.alloc([8, 2048], BF16)
    Wc = tabW
    Ws = tabW[:, :, 0:1408]
    Ww = tabW[:, :, 1408:2048]
    for h in range(8):
        base = h * 128 * OH_L
        k.dma("gpsimd", Wc[0:127, h, :], bass.AP(tensor=Rt, offset=base + OH_CO + 2016, ap=[[OH_L - 16, 127], [1, 2048]]), (), [("Wc", h)])
    esel = m.alloc([2048], BF16)
    k.dma("gpsimd", esel[0:32, :], d["esel"], (), ["esel"])
    scadd = m.alloc([16, 32])
    k.dma("sync", scadd, d["scadd"].rearrange("(tt p) j -> p tt j", p=128), (), ["scadd"])
    qg = m.alloc([2])
    kg = m.alloc([1])
    for half in range(2):
        k.dma("sync", qg[64 * half:64 * half + 64, 0:1], d["q_gain"][l].rearrange("(d o) -> d o", o=1), (), [("qg", half)])
        k.dma("sync", kg[64 * half:64 * half + 64, 0:1], d["k_gain"][l].rearrange("(d o) -> d o", o=1), (), [("kg", half)])
    k.ts("vector", qg[:, 1:2], qg[:, 0:1], scale, None, ALU.mult, None, [("qg", 0), ("qg", 1)], ["qgs"])
    qT = m.alloc([4, 2048], BF16)
    ksT = m.alloc([2048], BF16)
    kwT = m.alloc([2048], BF16)
    vx = m.alloc([16, 4 * 65], BF16)
    gt = m.alloc([16, 24])
    cacc = m.alloc([16, 8 * 97])
    rz = m.alloc([16, 8])
    negT = m.alloc([2, 2048], BF16)
    mark = m.top
    xf = m.alloc([2048])
    sq = m.alloc([2048], BF16)
    rs = [m.alloc([512]) for _ in range(2)]
    bi = [0]

    def nb():
        b = bi[0] % 6
        bi[0] += 1
        return b

    def headnorm(chunk, gain, gkeys, dst, dkey):
        k.dma("sync", xf, d["projF"][chunk], (), ["xf"])
        k.act(sq, xf, AF.Square, ["xf"], ["sq"])
        for tb in range(4):
            b = nb()
            cs = slice(tb * 512, (tb + 1) * 512)
            k.mm(m.bank(b), C.cb["blk64"], sq[:, cs], True, True, ["sq"] + C.keys, [f"ps{b}"])
            r_ = rs[tb % 2]
            k.act(r_, m.bank(b), AF.Sqrt, [f"ps{b}"] + C.keys, [f"rs{tb % 2}"], scale=1.0 / HD, bias=C.eps6[:, 0:1])
            k.recip(r_, r_, [f"rs{tb % 2}"], [f"rs{tb % 2}"])
            k.stt(dst[:, cs], xf[:, cs], gain, r_, ALU.mult, ALU.mult, ["xf", f"rs{tb % 2}"] + gkeys, [dkey])

    for j in range(4):
        headnorm(j, qg[:, 1:2], ["qgs"], qT[:, j, :], ("qT", j))
    headnorm(6, kg[:, 0:1], [("kg", 0), ("kg", 1)], ksT, "ksT")
    headnorm(7, kg[:, 0:1], [("kg", 0), ("kg", 1)], kwT, "kwT")
    kcb = m.alloc([2048], BF16)
    vcb = m.alloc([2048], BF16)
    k.dma("gpsimd", kcb, d["projF"][4], (), ["kcb"])
    k.dma("gpsimd", vcb, d["projF"][5], (), ["vcb"])
    k.memset("vector", vx, 1.0, ["vx"])
    vx5 = vx.rearrange("p st (a e) -> p st a e", e=65)
    for a in range(4):
        k.dma("gpsimd", vx5[:, :, a, 0:64], d["projT"][:, a * 64:(a + 1) * 64].rearrange("(st p) c -> p st c", p=128), ["vx"], [("vx", a)])
    vxk = ["vx"] + [("vx", a) for a in range(4)]
    k.dma("sync", gt, d["projT"][:, 256:280].rearrange("(tt p) c -> p tt c", p=128), (), ["gt"])
    k.act(gt, gt, AF.Sigmoid, ["gt"], ["gt"])
    if STOP <= 1:
        return
    W1 = m.alloc([2, 32, 128], BF16)
    W2 = m.alloc([2, 64], BF16)
    posT = m.alloc([2, 32], BF16)
    posF = m.alloc([2, 32])
    for half in range(2):
        pr = slice(64 * half, 64 * half + 64)
        for i in range(2):
            for dup in range(2):
                k.dma("gpsimd", W1[pr, i, :, dup * 64:(dup + 1) * 64], d["cmp_w1"][l, i].rearrange("(l d) e -> d l e", d=64), (), [("W1", half, i, dup)])
        k.dma("sync", posF[pr], d["cmp_pos"][l].rearrange("i l d -> d i l"), (), [("posF", half)])
        k.copy("vector", posT[pr], posF[pr], [("posF", half)], [("posT", half)])
    k.dma("gpsimd", W2[0:64], d["cmp_w2"][l].rearrange("i e f -> e i f"), (), ["W2"])
    wkeys = [("W1", a, b_, c_) for a in range(2) for b_ in range(2) for c_ in range(2)] + [("posT", 0), ("posT", 1), "W2"]
    if STOP <= 1.2:
        return
    hf = m.alloc([256])
    zb = m.alloc([32, 127], BF16)
    hid = m.alloc([256], BF16)
    t1 = m.alloc([256])
    t2 = m.alloc([256])
    kcT = m.alloc([128], BF16)
    k.memset("vector", kcT, 0.0, [("kcT", 0), ("kcT", 1)])
    kcn = m.alloc([256], BF16)
    rc = m.alloc([2, 97], BF16)
    k.memset("vector", rc, 1.0, ["rc"])
    ovf = m.alloc([32])
    k.dma("sync", ovf[0:127, :], d["ovc"], (), ["ovf"])
    for g in range(2):
        k.copy("vector", rc[0:127, g, 65:97], ovf[0:127, :], ["rc", "ovf"], [("rc", "ov", g)])
    if STOP <= 1.25:
        return
    for i, src, skey in ((0, kcb, "kcb"), (1, vcb, "vcb")):
        b = nb()
        sview = bass.AP(tensor=src.tensor, offset=src.offset, ap=[[src.ap[0][0], 128], [1, 32], [16, 127]])
        k.tt("vector", zb, sview, posT[:, i, :].unsqueeze(2).broadcast_to([128, 32, 127]), ALU.add, [skey, ("posT", 0), ("posT", 1)], ["zb"])
        if STOP <= 1.3:
            return
        for g in range(2):
            bg_ = nb()
            pr = slice(64 * g, 64 * g + 64)
            o_ = m.bank(bg_)[:, 0:127]
            for li in range(32):
                k.mm(o_, W1[pr, i, li, :], zb[pr, li, :], li == 0, li == 31, wkeys + ["zb"], [f"ps{bg_}"], signal=(li == 31))
            k.copy("vector", hf[0:64, g * 128:(g + 1) * 128], m.bank(bg_)[0:64, 0:128], [f"ps{bg_}"], [("hf", g)])
        if STOP <= 1.35:
            return
        k.tt("vector", hf[0:64, 0:1], hf[0:64, 0:1], hf[0:64, 0:1], ALU.max, [("hf", 0), ("hf", 1)], ["hf"])
        if STOP <= 1.4:
            return
        gelu_tanh(k, m, hid[0:64, :], hf[0:64, :], t1[0:64, :], t2[0:64, :], ["hf"], ["hid"], ["ct1"])
        if STOP <= 1.5:
            return
        if i == 0:
            b2 = nb()
            for g in range(2):
                k.mm(m.bank(b2)[0:64, g * 128:g * 128 + 127], W2[0:64, 0, :], hid[0:64, g * 128:g * 128 + 127], True, True, ["W2", "hid"], [f"ps{b2}"])
            k.copy("vector", hf[0:64, :], m.bank(b2)[0:64, 0:256], [f"ps{b2}"], ["hf"])
            k.act(sq[0:64, 0:256], hf[0:64, :], AF.Square, ["hf"], ["sq"])
            b3 = nb()
            k.mm(m.bank(b3)[0:64, 0:256], C.cb["ones"][0:64, 0:64], sq[0:64, 0:256], True, True, ["sq"] + C.keys, [f"ps{b3}"])
            k.act(t1[0:64, :], m.bank(b3)[0:64, 0:256], AF.Sqrt, [f"ps{b3}"] + C.keys, ["ct1"], scale=1.0 / HD, bias=C.eps6[0:64, 0:1])
            k.recip(t1[0:64, :], t1[0:64, :], ["ct1"], ["ct1"])
            k.stt(kcn[0:64, :], hf[0:64, :], kg[0:64, 0:1], t1[0:64, :], ALU.mult, ALU.mult, ["hf", "ct1", ("kg", 0)], ["kcn"])
            k.copy("vector", kcT[0:64, 0:127], kcn[0:64, 0:127], ["kcn"], [("kcT", 0)])
            k.copy("vector", kcT[64:128, 0:127], kcn[0:64, 128:255], ["kcn"], [("kcT", 1)])
        else:
            b2 = nb()
            for g in range(2):
                k.mm(m.bank(b2)[0:127, g * 64:(g + 1) * 64], hid[0:64, g * 128:g * 128 + 127], W2[0:64, 1, :], True, True, ["W2", "hid"], [f"ps{b2}"])
            k.copy("vector", rc[0:127, :, 0:64], m.bank(b2)[0:127, 0:128].rearrange("p (g e) -> p g e", e=64), [f"ps{b2}", "rc"], [("rc", "v")])
    rck = ["rc", ("rc", "ov", 0), ("rc", "ov", 1), ("rc", "v")]
    if STOP <= 2:
        return
    cacc4 = cacc.rearrange("p tt (h e) -> p tt h e", e=97)
    ecT = [m.alloc([512], BF16) for _ in range(2)]
    ei = 0
    for j in range(4):
        for g in range(2):
            h = 4 * g + j
            pr = slice(64 * g, 64 * g + 64)
            for tb in range(4):
                cs = slice(tb * 512, (tb + 1) * 512)
                b = nb()
                k.mm(m.bank(b)[:, :], kcT[pr, 0:128], qT[pr, j, cs], True, False, [("kcT", g), ("qT", j)], [f"ps{b}"], signal=False)
                k.mm(m.bank(b)[0:127, :], C.cb["ident"][0:127, 0:127], Wc[0:127, h, cs], False, True, [("Wc", h)] + C.keys, [f"ps{b}"])
                e_ = ecT[ei % 2]
                k.act(e_[0:127, :], m.bank(b)[0:127, :], AF.Exp, [f"ps{b}"], [f"ecT{ei % 2}"])
                for t4 in range(4):
                    tt = 4 * tb + t4
                    b2 = nb()
                    k.mm(m.bank(b2)[:, 0:97], e_[0:127, t4 * 128:(t4 + 1) * 128], rc[0:127, g, :], True, True, [f"ecT{ei % 2}"] + rck, [f"ps{b2}"])
                    k.copy("scalar" if t4 % 2 else "vector", cacc4[:, tt, h, :], m.bank(b2)[:, 0:97], [f"ps{b2}"], [("cacc", tt, h)])
                ei += 1
    if STOP <= 3:
        return
    k.ts("vector", rz, cacc4[:, :, :, 64], 1e-30, None, ALU.max, None, [("cacc", tt, h) for tt in range(16) for h in range(8)], ["rz"])
    k.recip(rz, rz, ["rz"], ["rz"])
    sc = m.alloc([32])
    sc2 = m.alloc([32])
    m8 = m.alloc([16])
    ngm = m.alloc([32])
    for tt in range(NTT):
        for g in range(2):
            for r in range(4):
                h = 4 * g + r
                if r == 0:
                    k.stt(sc, cacc4[:, tt, h, 65:97], rz[:, tt, h:h + 1], scadd[:, tt, :], ALU.mult, ALU.add, ["rz", "scadd"], ["sc"])
                else:
                    k.stt(sc, cacc4[:, tt, h, 65:97], rz[:, tt, h:h + 1], sc, ALU.mult, ALU.add, ["rz", "sc"], ["sc"])
            k.P.op("vector", (lambda o, i_: (lambda e: e.max(out=o, in_=i_)))(m8[:, 0:8], sc), ["sc"], ["m8a"])
            k.P.op("vector", (lambda o, r_, v_: (lambda e: e.match_replace(out=o, in_to_replace=r_, in_values=v_, imm_value=-3.0e38)))(sc2, m8[:, 0:8], sc), ["sc", "m8a"], ["sc2"])
            k.P.op("vector", (lambda o, i_: (lambda e: e.max(out=o, in_=i_)))(m8[:, 8:16], sc2), ["sc2"], ["m8b"])
            k.ts("vector", ngm, sc, m8[:, 15:16], NEG, ALU.is_lt, ALU.mult, ["sc", "m8b"], ["ngm"])
            b = nb()
            k.transpose(m.bank(b)[0:32, 0:128], ngm, C.cf["ident"], ["ngm"] + C.keys, [f"ps{b}"])
            k.copy("scalar", negT[0:32, g, tt * 128:(tt + 1) * 128], m.bank(b)[0:32, 0:128], [f"ps{b}"], [("negT", g, tt)])
    if STOP <= 4:
        return
    _barrier(k.P)
    m.top = mark
    for h in range(8):
        base = h * 128 * OH_L
        k.dma("gpsimd", Ws[:, h, :], bass.AP(tensor=Rt, offset=base + OH_SO + 127, ap=[[OH_L - 1, 128], [1, 1408]]), (), [("Ws", h)])
        k.dma("gpsimd", Ww[:, h, :], bass.AP(tensor=Rt, offset=base + OH_WO + 127, ap=[[OH_L - 1, 128], [1, 640]]), (), [("Ww", h)])
    oa = m.alloc([16, 512])
    PT = [m.alloc([512], BF16) for _ in range(3)]
    sacc = [m.alloc([4, 65]) for _ in range(2)]
    cf_ = m.alloc([16])
    tmp = m.alloc([4, 64])
    stage = m.alloc([4, 128])
    items = []
    gi = 0
    for tb in range(4):
        for h in range(8):
            for br in range(2):
                si_lo = 0 if br == 0 else max(0, 4 * tb - 4)
                si_list = list(range(si_lo, 4 * tb + 4))
                for si in si_list:
                    items.append(dict(tb=tb, h=h, br=br, si=si, first=(si == si_list[0]), last=(si == si_list[-1]), gi=gi,
                                      last_of_tb=(h == 7 and br == 1 and si == si_list[-1])))
                gi += 1

    def geom(it):
        tb, si, br = it["tb"], it["si"], it["br"]
        t0 = tb * 512
        s0 = si * 128
        c0 = max(0, s0 - t0)
        c1 = 512 if br == 0 else min(512, s0 + 640 - t0)
        return t0, s0, c0, c1

    def score_stage(n):
        it = items[n]
        tb, h, br, si = it["tb"], it["h"], it["br"], it["si"]
        g = h // 4
        j = h % 4
        pr = slice(64 * g, 64 * g + 64)
        t0, s0, c0, c1 = geom(it)
        cols = slice(c0, c1)
        tcols = slice(t0 + c0, t0 + c1)
        b = n % 4
        p_ = PT[n % 3]
        pk = f"PT{n % 3}"
        kT_ = ksT if br == 0 else kwT
        W_ = Ws if br == 0 else Ww
        m0 = t0 + c0 - s0
        k.mm(m.bank(b)[:, cols], kT_[pr, s0:s0 + 128], qT[pr, j, tcols], True, False, [], [f"ps{b}"], signal=False)
        if br == 0 and m0 + (c1 - c0) > 1408:
            for ca in range(c0, c1, 256):
                cb_ = min(c1, ca + 256)
                k.mm(m.bank(b)[:, ca:cb_], C.cb["ident"], W_[:, h, 1152:1152 + (cb_ - ca)], False, False, [("Ws", h)], [f"ps{b}"], signal=False)
        else:
            k.mm(m.bank(b)[:, cols], C.cb["ident"], W_[:, h, m0:m0 + (c1 - c0)], False, br == 1, [("Ws", h), ("Ww", h)], [f"ps{b}"], signal=(br == 1))
        if br == 0:
            k.mm(m.bank(b)[:, cols], esel[0:32, s0:s0 + 128], negT[0:32, g, tcols], False, True, ["esel"], [f"ps{b}"])
        k.act(p_[:, cols], m.bank(b)[:, cols], AF.Exp, [f"ps{b}"], [pk])

    def pv_stage(n):
        it = items[n]
        tb, h, br, si = it["tb"], it["h"], it["br"], it["si"]
        g = h // 4
        t0, s0, c0, c1 = geom(it)
        bo = 4 + (it["gi"] % 2)
        bok = f"ps{bo}"
        bank_o = m.bank(bo)[:, 0:260].rearrange("p (a e) -> p a e", e=65)
        p_ = PT[n % 3]
        pk = f"PT{n % 3}"
        if it["first"]:
            k.mm(m.bank(bo)[:, 0:260], C.zeros[:, 0:128], C.zeros[:, 0:260], True, False, [("zeros",)], [bok], signal=True)
        for t4 in range(4):
            if t4 * 128 < c0 or t4 * 128 >= c1:
                continue
            a = br * 2 + g
            k.mm(bank_o[:, t4, :], p_[:, t4 * 128:(t4 + 1) * 128], vx5[:, si, a, :], False, it["last"] and t4 == 3, [pk], [bok], signal=True)
        if not it["last"]:
            return
        sa = sacc[it["gi"] % 2]
        sk = f"sacc{it['gi'] % 2}"
        k.copy("vector", sa, bank_o, [bok], [sk])
        k.ts("vector", cf_[:, 0:4], sa[:, :, 64], 1e-30, None, ALU.max, None, [sk], ["cf"])
        k.recip(cf_[:, 0:4], cf_[:, 0:4], ["cf"], ["cf"])
        k.tt("vector", cf_[:, 4:8], cf_[:, 0:4], gt[:, 4 * tb:4 * tb + 4, 3 * h + 1 + br], ALU.mult, ["cf"], ["cf2"])
        dst = oa[:, 4 * tb:4 * tb + 4, h * 64:(h + 1) * 64]
        if br == 0:
            k.tt("vector", cf_[:, 8:12], rz[:, 4 * tb:4 * tb + 4, h], gt[:, 4 * tb:4 * tb + 4, 3 * h], ALU.mult, [], ["cf3"])
            k.tt("vector", dst, cacc4[:, 4 * tb:4 * tb + 4, h, 0:64], cf_[:, 8:12].unsqueeze(2).broadcast_to([128, 4, 64]), ALU.mult, ["cf3"], [("oa", tb, h)])
        k.tt("vector", tmp, sa[:, :, 0:64], cf_[:, 4:8].unsqueeze(2).broadcast_to([128, 4, 64]), ALU.mult, [sk, "cf2"], ["tmpo"])
        k.tt("vector", dst, dst, tmp, ALU.add, ["tmpo", ("oa", tb, h)], [("oa", tb, h)])
        if it["last_of_tb"]:
            for t4 in range(4):
                tt = 4 * tb + t4
                for c in range(4):
                    b2 = 6 + (c % 2)
                    k.transpose(m.bank(b2)[:, 0:128], oa[:, tt, c * 128:(c + 1) * 128], C.cf["ident"], [("oa", tb, h_) for h_ in (2 * c, 2 * c + 1)], [f"ps{b2}"])
                    k.copy("scalar", stage[:, c, :], m.bank(b2)[:, 0:128], [f"ps{b2}"], [("stg", c)])
                    k.dma("sync", d["mixF"][c][:, tt * 128:(tt + 1) * 128], stage[:, c, :], [("stg", c)], [("mixF", c, tt)])

    for n in range(len(items) + 1):
        if n < len(items):
            score_stage(n)
        if n >= 1:
            pv_stage(n - 1)


_CST = None


def _get_cst():
    global _CST
    if _CST is None:
        c = host_consts()
        _CST = np.ascontiguousarray(np.concatenate([c[n] for n in ["ident", "ones", "blk64", "tril_qp", "lt_st", "uincl", "lstrict"]], axis=1).astype(np.float32))
    return _CST


def kernel(**inputs):
    x = np.asarray(inputs["x"], dtype=np.float32)
    nc = build()
    shared = {name: np.ascontiguousarray(np.asarray(inputs[name], dtype=np.float32)) for name, _ in INPUT_SPECS if name != "x"}
    shared["cst"] = _get_cst()
    shared.update(host_nsa_consts())
    in_maps = []
    for b in range(8):
        mp = dict(shared)
        mp["x"] = np.ascontiguousarray(x[b])
        in_maps.append(mp)
    res = run_bass_kernel_spmd(nc, in_maps, core_ids=list(range(8)))
    return np.stack([np.asarray(r["out"], dtype=np.float32) for r in res.results], axis=0)
```

```python
import math
import os
import numpy as np
import ml_dtypes
import concourse.bass as bass
import concourse.mybir as mybir
from concourse.bass_utils import run_bass_kernel_spmd

F32 = mybir.dt.float32
BF16 = mybir.dt.bfloat16
AF = mybir.ActivationFunctionType
ALU = mybir.AluOpType
AX = mybir.AxisListType

D_MODEL = 2048
T = 2048
DEPTH = 4
HD = 64
GW = 512
NH = 8
G = 2
R = 4
CMP_LEN = 32
CMP_STRIDE = 16
N_CMP = 127
SLC_LEN = 64
N_SLC = 32
SLC_TOP = 16
WINDOW = 512
N_BUCKETS = 32
MAX_DISTANCE = 1024
D_FF = 5632
W_IN_COLS = 5400
NEG = -30000.0
NKC = D_MODEL // 128
NTT = T // 128
NFC = D_FF // 128


class Prog:
    ENGS = ("tensor", "vector", "scalar", "gpsimd", "sync")

    def __init__(self, nc, n_dma_sems=10):
        self.nc = nc
        self.ops = {e: [] for e in self.ENGS}
        self.sem = {e: nc.alloc_semaphore(name=f"sem_{e}") for e in ("tensor", "vector", "scalar", "gpsimd")}
        self.cnt = {e: 0 for e in self.sem}
        nsem = {"sync": n_dma_sems, "gpsimd": 4}
        self.dma_sems = {q: [nc.alloc_semaphore(name=f"dsem_{q}{i}") for i in range(nsem[q])] for q in ("sync", "gpsimd")}
        self.dma_cnt = {q: [0] * nsem[q] for q in ("sync", "gpsimd")}
        self.dma_rr = {q: 0 for q in ("sync", "gpsimd")}
        self.waited = {e: {} for e in self.ENGS}
        self.last_w = {}
        self.readers = {}
        self.sem_by_id = {}
        self.pending = {}

    def _ev_id(self, sem):
        i = id(sem)
        self.sem_by_id[i] = sem
        return i

    def _need(self, eng, ev, waits):
        if ev is None:
            return
        sid, val, src_eng = ev
        if src_eng == "tensor" and eng == "tensor":
            return
        if self.waited[eng].get(sid, 0) >= val:
            return
        waits[sid] = max(waits.get(sid, 0), val)

    def op(self, eng, fn, reads=(), writes=(), signal=True):
        waits = {}
        for k in reads:
            self._need(eng, self.last_w.get(k), waits)
        for k in writes:
            self._need(eng, self.last_w.get(k), waits)
            for ev in self.readers.get(k, {}).values():
                self._need(eng, ev, waits)
        for sid, val in waits.items():
            self.waited[eng][sid] = val
        ev = None
        if signal:
            self.cnt[eng] += 1
            ev = (self._ev_id(self.sem[eng]), self.cnt[eng], eng)
        self.ops[eng].append((list(waits.items()), fn, (self.sem[eng], 1) if signal else None))
        if ev is not None:
            pr, pw = self.pending.pop(eng, ([], []))
            for k in list(reads) + pr:
                self.readers.setdefault(k, {})[ev[0]] = ev
            for k in list(writes) + pw:
                self.last_w[k] = ev
                self.readers[k] = {}
        else:
            assert eng == "tensor"
            pr, pw = self.pending.setdefault(eng, ([], []))
            pr.extend(reads)
            pw.extend(writes)
        return ev

    def dma(self, q, fn, reads=(), writes=(), war=()):
        waits = {}
        for k in war:
            for ev in self.readers.get(k, {}).values():
                self._need(q, ev, waits)
        for k in reads:
            self._need(q, self.last_w.get(k), waits)
        for k in writes:
            self._need(q, self.last_w.get(k), waits)
            for ev in self.readers.get(k, {}).values():
                self._need(q, ev, waits)
        i = self.dma_rr[q]
        self.dma_rr[q] = (i + 1) % len(self.dma_sems[q])
        s = self.dma_sems[q][i]
        sid = self._ev_id(s)
        prev = self.dma_cnt[q][i]
        if prev > 0 and self.waited[q].get(sid, 0) < prev:
            waits[sid] = max(waits.get(sid, 0), prev)
        for sd, val in waits.items():
            self.waited[q][sd] = val
        self.dma_cnt[q][i] = prev + 16
        ev = (sid, prev + 16, "dma_" + q)
        self.ops[q].append((list(waits.items()), fn, (s, 16)))
        for k in reads:
            self.readers.setdefault(k, {})[("d", q, i)] = ev
        for k in writes:
            self.last_w[k] = ev
            self.readers[k] = {}
        return ev

    def finish(self, out_keys):
        waits = {}
        for k in out_keys:
            self._need("sync", self.last_w.get(k), waits)
        final_waits = list(waits.items())
        nc = self.nc
        prog = self

        def emit(e, name):
            for waits_, fn, sig in prog.ops[name]:
                for sid, val in waits_:
                    e.wait_ge(prog.sem_by_id[sid], val)
                if fn is None:
                    continue
                inst = fn(e)
                if sig is not None:
                    inst.then_inc(sig[0], sig[1])

        with nc.Block() as block:
            @block.tensor
            def _(e):
                emit(e, "tensor")

            @block.vector
            def _(e):
                emit(e, "vector")

            @block.scalar
            def _(e):
                emit(e, "scalar")

            @block.gpsimd
            def _(e):
                emit(e, "gpsimd")

            @block.sync
            def _(e):
                emit(e, "sync")
                for q in ("sync", "gpsimd"):
                    for i, s_ in enumerate(prog.dma_sems[q]):
                        if prog.dma_cnt[q][i] > 0:
                            e.wait_ge(s_, prog.dma_cnt[q][i])


class K:
    def __init__(self, nc):
        self.nc = nc
        self.P = Prog(nc)
        self.ps_rr = 0

    def act(self, out, in_, func, reads, writes, **kw):
        return self.P.op("scalar", lambda e: e.activation(out=out, in_=in_, func=func, **kw), reads, writes)

    def ts(self, eng, out, in0, s1, s2, op0, op1, reads, writes, **kw):
        if op1 is None:
            return self.P.op(eng, lambda e: e.tensor_scalar(out=out, in0=in0, scalar1=s1, scalar2=None, op0=op0, **kw), reads, writes)
        return self.P.op(eng, lambda e: e.tensor_scalar(out=out, in0=in0, scalar1=s1, scalar2=s2, op0=op0, op1=op1, **kw), reads, writes)

    def tt(self, eng, out, in0, in1, op, reads, writes):
        return self.P.op(eng, lambda e: e.tensor_tensor(out=out, in0=in0, in1=in1, op=op), reads, writes)

    def stt(self, out, in0, scalar, in1, op0, op1, reads, writes):
        return self.P.op("vector", lambda e: e.scalar_tensor_tensor(out=out, in0=in0, scalar=scalar, in1=in1, op0=op0, op1=op1), reads, writes)

    def copy(self, eng, out, in_, reads, writes):
        if eng == "scalar":
            return self.P.op("scalar", lambda e: e.activation(out=out, in_=in_, func=AF.Copy), reads, writes)
        return self.P.op(eng, lambda e: e.tensor_copy(out=out, in_=in_), reads, writes)

    def recip(self, out, in_, reads, writes):
        return self.P.op("vector", lambda e: e.reciprocal(out=out, in_=in_), reads, writes)

    def memset(self, eng, ap, val, writes):
        return self.P.op(eng, lambda e: e.memset(ap, val), (), writes)

    def mm(self, out, lhsT, rhs, start, stop, reads, writes, signal=None, **kw):
        if signal is None:
            signal = stop
        return self.P.op("tensor", lambda e: e.matmul(out, lhsT, rhs, start=start, stop=stop, **kw), reads, writes, signal=signal)

    def transpose(self, out, in_, ident, reads, writes, signal=True):
        return self.P.op("tensor", lambda e: e.transpose(out, in_, ident), reads, writes, signal=signal)

    def dma(self, q, out, in_, reads, writes, war=()):
        return self.P.dma(q, lambda e: e.dma_start(out=out, in_=in_), reads, writes, war)


class Mem:
    def __init__(self, nc):
        self.big = nc.alloc_sbuf_tensor("big", [128, 192 * 256], F32)
        self.top = 0
        self.floor = 0
        self.ps = nc.alloc_psum_tensor("ps", [128, 8 * 512], F32)

    def alloc(self, free_shape, dtype=F32):
        n = int(np.prod(free_shape))
        words = n if dtype == F32 else (n + 1) // 2
        words = (words + 15) // 16 * 16
        a = self.top
        self.top += words
        assert self.top <= 192 * 256, f"SBUF overflow {self.top}"
        v = self.big[:, a:a + words]
        if dtype != F32:
            v = v.bitcast(dtype)
        v = v[:, 0:n]
        if len(free_shape) == 2:
            v = v.rearrange("p (a b) -> p a b", b=free_shape[1])
        elif len(free_shape) == 3:
            v = v.rearrange("p (a b c) -> p a b c", b=free_shape[1], c=free_shape[2])
        return v

    def set_floor(self):
        self.floor = self.top

    def reset(self):
        self.top = self.floor

    def bank(self, i, dtype=F32):
        v = self.ps[:, i * 512:(i + 1) * 512]
        if dtype != F32:
            v = v.bitcast(dtype)
        return v


def _barrier(P):
    evs = []
    for e, s in P.sem.items():
        if P.cnt[e] > 0:
            evs.append((P._ev_id(s), P.cnt[e]))
    for q in ("sync", "gpsimd"):
        for i, s in enumerate(P.dma_sems[q]):
            if P.dma_cnt[q][i] > 0:
                evs.append((P._ev_id(s), P.dma_cnt[q][i]))
    for eng in P.ENGS:
        waits = []
        for sid, val in evs:
            if P.waited[eng].get(sid, 0) < val:
                waits.append((sid, val))
                P.waited[eng][sid] = val
        if waits:
            P.ops[eng].append((waits, None, None))
    P.last_w = {}
    P.readers = {}


FM_GROUPS = [
    [(128 * j, 64 * j, 64) for j in range(4)] + [(128 * j + 64, 64 * (4 + j), 64) for j in range(4)],
    [(0, 512, 128), (128, 640, 128), (256, 768, 128), (384, 1024, 128)],
    [(0, 1304, 512)], [(0, 1816, 512)], [(0, 2328, 512)],
    [(0, 2840, 512)],
    [(0, 3864, 512)], [(0, 4376, 512)],
]
TM_BLOCKS = [
    (0, [(0, 896, 128), (128, 1152, 128), (256, 1280, 24)], 280),
    (280, [(0, 3352, 512)], 512),
    (792, [(0, 4888, 512)], 512),
]
N_FM = 32
N_TM = 1304


def _rel_bucket_np(n):
    n = np.maximum(n, 0)
    max_exact = N_BUCKETS // 2
    nf = np.maximum(n, 1).astype(np.float32)
    large = max_exact + (np.log(nf / np.float32(max_exact)) / np.float32(math.log(MAX_DISTANCE / max_exact)) * np.float32(N_BUCKETS - max_exact)).astype(np.int32)
    large = np.minimum(large, N_BUCKETS - 1)
    return np.where(n < max_exact, n, large)


def host_consts():
    c = {}
    c["ident"] = np.eye(128, dtype=np.float32)
    c["ones"] = np.ones((128, 128), np.float32)
    blk = np.zeros((128, 128), np.float32)
    blk[:64, :64] = 1
    blk[64:, 64:] = 1
    c["blk64"] = blk
    i = np.arange(128)
    c["tril_qp"] = (i[:, None] <= i[None, :]).astype(np.float32)
    c["lt_st"] = (i[:, None] < i[None, :]).astype(np.float32)
    c["uincl"] = (i[:, None] >= i[None, :]).astype(np.float32)
    c["lstrict"] = (i[:, None] < i[None, :]).astype(np.float32)
    return c


class Ctx:
    pass


def setup_consts(k, m, d, C):
    nc = k.nc
    C.cf = {}
    C.cb = {}
    names = ["ident", "ones", "blk64", "tril_qp", "lt_st", "uincl", "lstrict"]
    for i, n in enumerate(names):
        if n in ("ident", "tril_qp", "ones"):
            t = m.alloc([128])
            k.dma("sync", t, d["cst"][:, i * 128:(i + 1) * 128], (), [("cf", n)])
            C.cf[n] = t
        tb = m.alloc([128], BF16)
        k.dma("gpsimd", tb, d["cst"][:, i * 128:(i + 1) * 128], (), [("cb", n)])
        C.cb[n] = tb
    C.zeros = m.alloc([512], BF16)
    k.memset("vector", C.zeros, 0.0, [("zeros",)])
    C.eps6 = m.alloc([1])
    k.memset("vector", C.eps6, 1e-6, [("eps6",)])
    C.eps5 = m.alloc([1])
    k.memset("vector", C.eps5, 1e-5, [("eps5",)])
    C.one1 = m.alloc([1])
    k.memset("vector", C.one1, 1.0, [("one1",)])
    C.gmix = m.alloc([DEPTH * 16])
    C.gffn = m.alloc([DEPTH * 16])
    C.ggrp = m.alloc([DEPTH * 16])
    for nm, t in (("norm_mix", C.gmix), ("norm_ffn", C.gffn), ("group_gain", C.ggrp)):
        k.dma("sync", t, d[nm].rearrange("l (kc p) -> p (l kc)", p=128), (), [("g", nm)])
    C.keys = [("cf", n) for n in C.cf] + [("cb", n) for n in C.cb] + [("zeros",), ("eps6",), ("eps5",), ("one1",), ("g", "norm_mix"), ("g", "norm_ffn"), ("g", "group_gain")]


def phase_norm(k, m, C, x_ap, gvec, hT):
    xt = [m.alloc([2048]) for _ in range(2)]
    xs = [m.alloc([2048], BF16) for _ in range(2)]
    junk = m.alloc([2048], BF16)
    st = m.alloc([64])
    for tt in range(NTT):
        s = tt % 2
        k.dma("sync", xt[s], x_ap[tt * 128:(tt + 1) * 128, :], (), [f"xt{s}"])
        k.act(junk, xt[s], AF.Square, [f"xt{s}"], ["junk", ("ss", tt)], accum_out=st[:, tt:tt + 1])
        k.act(st[:, 16 + tt:17 + tt], st[:, tt:tt + 1], AF.Sqrt, [("ss", tt), ("eps6",)], [("sq", tt)], scale=1.0 / D_MODEL, bias=C.eps6[:, 0:1])
        k.recip(st[:, 32 + tt:33 + tt], st[:, 16 + tt:17 + tt], [("sq", tt)], [("rs", tt)])
        k.ts("vector", xs[s], xt[s], st[:, 32 + tt:33 + tt], None, ALU.mult, None, [f"xt{s}", ("rs", tt)], [f"xs{s}"])
        for half in range(2):
            pst = m.bank(6 + half, BF16)
            for j in range(8):
                kc = half * 8 + j
                k.transpose(pst[:, j * 128:(j + 1) * 128], xs[s][:, kc * 128:(kc + 1) * 128], C.cb["ident"],
                            [f"xs{s}", ("cb", "ident")], [f"pst{half}"], signal=(j == 7))
            k.tt("vector", hT[:, half * 8:(half + 1) * 8, tt * 128:(tt + 1) * 128],
                 pst.rearrange("p (a b) -> p a b", b=128),
                 gvec[:, half * 8:(half + 1) * 8].unsqueeze(2).broadcast_to([128, 8, 128]),
                 ALU.mult, [f"pst{half}"] + C.keys, [("hT", tt)])


def phase_inproj(k, m, C, d, l, hT):
    w_l = d["w_in"][l].rearrange("(kc p) c -> p kc c", p=128)
    wt = [m.alloc([16, 512], BF16) for _ in range(2)]
    stage = [m.alloc([2048]) for _ in range(3)]
    ci = 0
    bi = 0
    for g, segs in enumerate(FM_GROUPS):
        s = g % 2
        for (dst, src, n) in segs:
            k.dma("gpsimd", wt[s][:, :, dst:dst + n], w_l[:, :, src:src + n], (), [(f"wt{s}", dst)], war=[f"wtall{s}"])
        for c in range(4):
            segkeys = [f"wtall{s}"] + [(f"wt{s}", dst) for (dst, src, n) in segs if dst < (c + 1) * 128 and dst + n > c * 128]
            sg = stage[ci % 3]
            for tb in range(4):
                b = bi % 6
                bi += 1
                bank = m.bank(b)
                for kc in range(16):
                    k.mm(bank, wt[s][:, kc, c * 128:(c + 1) * 128], hT[:, kc, tb * 512:(tb + 1) * 512], kc == 0, kc == 15,
                         segkeys + [("hT", 4 * tb + i) for i in range(4)], [f"ps{b}"])
                k.copy("scalar" if (bi % 2) else "vector", sg[:, tb * 512:(tb + 1) * 512], bank, [f"ps{b}"], [(f"stage{ci % 3}", tb)])
            k.dma("sync", d["projF"][4 * g + c], sg, [(f"stage{ci % 3}", tb) for tb in range(4)], [("projF", 4 * g + c)])
            ci += 1
    for bidx, (col0, segs, ncols) in enumerate(TM_BLOCKS):
        s = bidx % 2
        for (dst, src, n) in segs:
            k.dma("gpsimd", wt[s][:, :, dst:dst + n], w_l[:, :, src:src + n], (), [(f"wt{s}", dst)], war=[f"wtall{s}"])
        segkeys = [f"wtall{s}"] + [(f"wt{s}", dst) for (dst, src, n) in segs]
        for tt in range(NTT):
            b = bi % 6
            bi += 1
            bank = m.bank(b)
            sg = stage[ci % 3]
            for kc in range(16):
                k.mm(bank[:, 0:ncols], hT[:, kc, tt * 128:(tt + 1) * 128], wt[s][:, kc, 0:ncols], kc == 0, kc == 15,
                     segkeys + [("hT", tt)], [f"ps{b}"])
            k.copy("scalar" if (bi % 2) else "vector", sg[:, 0:ncols], bank[:, 0:ncols], [f"ps{b}"], [(f"stage{ci % 3}", 0)])
            k.dma("sync", d["projT"][tt * 128:(tt + 1) * 128, col0:col0 + ncols], sg[:, 0:ncols], [(f"stage{ci % 3}", 0)], [("projT", bidx, tt)])
            ci += 1


def phase_wout(k, m, C, d, l, x_in, x_out):
    mixT = m.alloc([16, 2048], BF16)
    xg = [m.alloc([4, 2048]) for _ in range(1)]
    sq = m.alloc([4, 2048], BF16)
    tmp = [m.alloc([512]) for _ in range(2)]
    bi = 0
    for grp in range(4):
        x4 = xg[0]
        for c in range(4):
            k.dma("sync", x4[:, c, :], d["mixF"][4 * grp + c], (), [("xg", c)])
            k.act(sq[:, c, :], x4[:, c, :], AF.Square, [("xg", c)], [("sq", c)])
        for tb in range(4):
            b = bi % 6
            bi += 1
            bank = m.bank(b)
            for c in range(4):
                k.mm(bank, C.cb["ones"], sq[:, c, tb * 512:(tb + 1) * 512], c == 0, c == 3, [("sq", c)] + C.keys, [f"ps{b}"])
            t_ = tmp[tb % 2]
            k.act(t_, bank, AF.Sqrt, [f"ps{b}"] + C.keys, [f"tmp{tb % 2}"], scale=1.0 / GW, bias=C.eps6[:, 0:1])
            k.recip(t_, t_, [f"tmp{tb % 2}"], [f"tmp{tb % 2}"])
            for c in range(4):
                ch = 4 * grp + c
                k.stt(mixT[:, ch, tb * 512:(tb + 1) * 512], x4[:, c, tb * 512:(tb + 1) * 512], C.ggrp[:, l * 16 + ch:l * 16 + ch + 1], t_,
                      ALU.mult, ALU.mult, [("xg", c), f"tmp{tb % 2}"] + C.keys, [("mixT", ch, tb)])
    w_l = d["w_out"][l].rearrange("(kc p) n -> p kc n", p=128)
    wt = [m.alloc([16, 512], BF16) for _ in range(2)]
    xin = [m.alloc([512]) for _ in range(3)]
    xi = 0
    for nb in range(4):
        s = nb % 2
        k.dma("gpsimd", wt[s], w_l[:, :, nb * 512:(nb + 1) * 512], (), [f"wo{s}"])
        for tt in range(NTT):
            b = bi % 6
            bi += 1
            bank = m.bank(b)
            xs_ = xin[xi % 3]
            k.dma("sync", xs_, x_in[tt * 128:(tt + 1) * 128, nb * 512:(nb + 1) * 512], (), [f"xin{xi % 3}"])
            for kc in range(16):
                k.mm(bank, mixT[:, kc, tt * 128:(tt + 1) * 128], wt[s][:, kc, :], kc == 0, kc == 15,
                     [f"wo{s}", ("mixT", kc, tt // 4)], [f"ps{b}"])
            k.tt("vector", xs_, bank, xs_, ALU.add, [f"ps{b}", f"xin{xi % 3}"], [f"xin{xi % 3}"])
            k.dma("sync", x_out[tt * 128:(tt + 1) * 128, nb * 512:(nb + 1) * 512], xs_, [f"xin{xi % 3}"], [("xout", nb, tt)])
            xi += 1


def phase_ffn_up(k, m, C, d, l, hT):
    wg_l = d["w_ffn_gate"][l].rearrange("(kc p) f -> p kc f", p=128)
    wu_l = d["w_ffn_up"][l].rearrange("(kc p) f -> p kc f", p=128)
    wg = [m.alloc([16, 512], BF16) for _ in range(2)]
    wu = [m.alloc([16, 512], BF16) for _ in range(2)]
    act = [m.alloc([2048], BF16) for _ in range(3)]
    sil = [m.alloc([512]) for _ in range(2)]
    bi = 0
    ai = 0
    for fg in range(NFC // 4):
        s = fg % 2
        k.dma("gpsimd", wg[s], wg_l[:, :, fg * 512:(fg + 1) * 512], (), [f"wg{s}"])
        k.dma("gpsimd", wu[s], wu_l[:, :, fg * 512:(fg + 1) * 512], (), [f"wu{s}"])
        for c in range(4):
            fc = fg * 4 + c
            a_ = act[ai % 3]
            for tb in range(4):
                bg = bi % 6
                bu = (bi + 1) % 6
                bi += 2
                hk = [("hT", 4 * tb + i) for i in range(4)]
                for kc in range(16):
                    k.mm(m.bank(bg), wg[s][:, kc, c * 128:(c + 1) * 128], hT[:, kc, tb * 512:(tb + 1) * 512], kc == 0, kc == 15, [f"wg{s}"] + hk, [f"ps{bg}"])
                for kc in range(16):
                    k.mm(m.bank(bu), wu[s][:, kc, c * 128:(c + 1) * 128], hT[:, kc, tb * 512:(tb + 1) * 512], kc == 0, kc == 15, [f"wu{s}"] + hk, [f"ps{bu}"])
                s_ = sil[tb % 2]
                k.act(s_, m.bank(bg), AF.Silu, [f"ps{bg}"], [f"sil{tb % 2}"])
                k.tt("vector", a_[:, tb * 512:(tb + 1) * 512], m.bank(bu), s_, ALU.mult, [f"ps{bu}", f"sil{tb % 2}"], [(f"act{ai % 3}", tb)])
            k.dma("sync", d["actD"][fc], a_, [(f"act{ai % 3}", tb) for tb in range(4)], [("actD", fc)])
            ai += 1


def phase_ffn_down(k, m, C, d, l, x_in, x_out):
    wd_l = d["w_ffn_down"][l].rearrange("(fc p) n -> p fc n", p=128)
    actv = d["actD"].rearrange("fc p t -> p fc t")
    wd = [m.alloc([NFC, 512], BF16) for _ in range(2)]
    ab = [m.alloc([NFC, 512], BF16) for _ in range(2)]
    xin = [m.alloc([512]) for _ in range(3)]
    bi = 0
    xi = 0
    ai = 0
    for nbi, nb in enumerate([int(c) for c in os.environ.get("NB_ORDER", "0123")]):
        s = nbi % 2
        for h in range(2):
            k.dma("gpsimd", wd[s][:, h * 22:(h + 1) * 22, :], wd_l[:, h * 22:(h + 1) * 22, nb * 512:(nb + 1) * 512], (), [(f"wd{s}", h)])
        for tg in range(4):
            a_ = ab[ai % 2]
            for h in range(2):
                k.dma("sync", a_[:, h * 22:(h + 1) * 22, :], actv[:, h * 22:(h + 1) * 22, tg * 512:(tg + 1) * 512], (), [(f"ab{ai % 2}", h)])
            for t4 in range(4):
                tt = tg * 4 + t4
                b = bi % 6
                bi += 1
                bank = m.bank(b)
                xs_ = xin[xi % 3]
                k.dma("sync", xs_, x_in[tt * 128:(tt + 1) * 128, nb * 512:(nb + 1) * 512], (), [f"xin{xi % 3}"])
                for fc in range(NFC):
                    k.mm(bank, a_[:, fc, t4 * 128:(t4 + 1) * 128], wd[s][:, fc, :], fc == 0, fc == NFC - 1,
                         [(f"wd{s}", fc // 22), (f"ab{ai % 2}", fc // 22)], [f"ps{b}"])
                k.tt("vector", xs_, bank, xs_, ALU.add, [f"ps{b}", f"xin{xi % 3}"], [f"xin{xi % 3}"])
                k.dma("sync", x_out[tt * 128:(tt + 1) * 128, nb * 512:(nb + 1) * 512], xs_, [f"xin{xi % 3}"], [("xout", nb, tt)])
                xi += 1
            ai += 1


INPUT_SPECS = [
    ("x", [T, D_MODEL]), ("w_in", [DEPTH, D_MODEL, W_IN_COLS]), ("w_out", [DEPTH, D_MODEL, D_MODEL]),
    ("norm_mix", [DEPTH, D_MODEL]), ("norm_ffn", [DEPTH, D_MODEL]), ("q_gain", [DEPTH, HD]), ("k_gain", [DEPTH, HD]),
    ("cmp_pos", [DEPTH, 2, CMP_LEN, HD]), ("cmp_w1", [DEPTH, 2, CMP_LEN * HD, HD]), ("cmp_w2", [DEPTH, 2, HD, HD]),
    ("rel_table", [N_BUCKETS, NH]), ("conv_w", [DEPTH, 3, GW]), ("sgu_w", [DEPTH, NH, 128, 128]), ("sgu_b", [DEPTH, NH, 128]),
    ("group_gain", [DEPTH, D_MODEL]), ("w_ffn_gate", [DEPTH, D_MODEL, D_FF]), ("w_ffn_up", [DEPTH, D_MODEL, D_FF]),
    ("w_ffn_down", [DEPTH, D_FF, D_MODEL]),
]
N_CST = 7


def build(n_layers=DEPTH, dbg=(), phases=None, mix=("conv", "sgu", "sb", "nsa")):
    nc = bass.Bass("TRN2", target_bir_lowering=False)
    d = {}
    for name, shape in INPUT_SPECS:
        d[name] = nc.dram_tensor(name, shape, F32, kind="ExternalInput").ap()
    d["cst"] = nc.dram_tensor("cst", [128, N_CST * 128], F32, kind="ExternalInput").ap()

    def scratch(name, shape, dt=F32):
        kind = "ExternalOutput" if name in dbg else "Internal"
        d[name] = nc.dram_tensor(name, shape, dt, kind=kind).ap()

    d["oh"] = nc.dram_tensor("oh", [33, OH_L], F32, kind="ExternalInput").ap()
    d["scadd"] = nc.dram_tensor("scadd", [T, N_SLC], F32, kind="ExternalInput").ap()
    d["esel"] = nc.dram_tensor("esel", [N_SLC, T], F32, kind="ExternalInput").ap()
    d["ovc"] = nc.dram_tensor("ovc", [N_CMP, N_SLC], F32, kind="ExternalInput").ap()
    scratch("R", [8, 128, OH_L])
    scratch("projF", [N_FM, 128, T])
    scratch("projT", [T, N_TM])
    scratch("mixF", [16, 128, T])
    scratch("actD", [NFC, 128, T], BF16)
    scratch("xa", [T, D_MODEL])
    scratch("xb", [T, D_MODEL])
    d["out"] = nc.dram_tensor("out", [T, D_MODEL], F32, kind="ExternalOutput").ap()

    k = K(nc)
    m = Mem(nc)
    C = Ctx()
    with nc.allow_non_contiguous_dma(reason="small parameter loads"):
        setup_consts(k, m, d, C)
        m.set_floor()
        _barrier(k.P)
        if "nsa" in mix:
            setup_nsa_tables(k, m, C, d)
            _barrier(k.P)
        x_cur = d["x"]
        for l in range(n_layers):
            ph = phases if phases is not None else ("norm1", "inproj", "mixers", "wout", "ffn")
            x2 = d["out"] if l == n_layers - 1 else d["xb"]
            if "inproj" in ph:
                m.reset()
                hT = m.alloc([16, T], BF16)
                phase_norm(k, m, C, x_cur, C.gmix[:, l * 16:(l + 1) * 16], hT)
                phase_inproj(k, m, C, d, l, hT)
                _barrier(k.P)
            if "mixers" in ph:
                phase_mixers(k, m, C, d, l, which=mix)
            if "wout" in ph:
                m.reset()
                phase_wout(k, m, C, d, l, x_cur, d["xa"])
                _barrier(k.P)
            if "ffn" in ph:
                m.reset()
                hT = m.alloc([16, T], BF16)
                phase_norm(k, m, C, d["xa"], C.gffn[:, l * 16:(l + 1) * 16], hT)
                phase_ffn_up(k, m, C, d, l, hT)
                _barrier(k.P)
                m.reset()
                phase_ffn_down(k, m, C, d, l, d["xa"], x2)
                _barrier(k.P)
            x_cur = x2
        k.P.finish([])
    return nc


def phase_conv(k, m, C, d, l):
    cw = m.alloc([12])
    k.dma("sync", cw.rearrange("p (w j) -> p w j", j=4), d["conv_w"][l].rearrange("w (j p) -> p w j", p=128), (), ["cw"])
    bg = [m.alloc([2048]) for _ in range(2)]
    cg = [m.alloc([2048]) for _ in range(2)]
    hh = [m.alloc([2048]) for _ in range(2)]
    z = [m.alloc([2050]) for _ in range(2)]
    y = [m.alloc([2048]) for _ in range(2)]
    for s in range(2):
        k.memset("vector", z[s][:, 0:2], 0.0, [f"z{s}"])
    for j in range(4):
        s = j % 2
        k.dma("sync", bg[s], d["projF"][8 + j], (), [f"bg{s}"])
        k.dma("sync", cg[s], d["projF"][12 + j], (), [f"cg{s}"])
        k.dma("sync", hh[s], d["projF"][16 + j], (), [f"hh{s}"])
        k.tt("gpsimd", z[s][:, 2:2050], cg[s], hh[s], ALU.mult, [f"cg{s}", f"hh{s}"], [f"z{s}"])
        k.ts("vector", y[s], z[s][:, 2:2050], cw[:, 8 + j:9 + j], None, ALU.mult, None, [f"z{s}", "cw"], [f"y{s}"])
        k.stt(y[s], z[s][:, 1:2049], cw[:, 4 + j:5 + j], y[s], ALU.mult, ALU.add, [f"z{s}", "cw", f"y{s}"], [f"y{s}"])
        k.stt(y[s], z[s][:, 0:2048], cw[:, j:j + 1], y[s], ALU.mult, ALU.add, [f"z{s}", "cw", f"y{s}"], [f"y{s}"])
        k.tt("vector", y[s], y[s], bg[s], ALU.mult, [f"y{s}", f"bg{s}"], [f"y{s}"])
        k.dma("sync", d["mixF"][4 + j], y[s], [f"y{s}"], [("mixF", 4 + j)])


def gelu_tanh(k, m, out, x, tmp, tmp2, rk, wk, tk):
    c = 1.5957691216057308
    k.tt("vector", tmp, x, x, ALU.mult, rk, tk)
    k.ts("vector", tmp, tmp, 0.044715 * c, c, ALU.mult, ALU.add, tk, tk)
    k.tt("vector", tmp, tmp, x, ALU.mult, rk + tk, tk)
    k.act(tmp2, tmp, AF.Sigmoid, tk, [tk[0] + "_2"])
    k.tt("vector", out, x, tmp2, ALU.mult, rk + [tk[0] + "_2"], wk)


def phase_sgu(k, m, C, d, l):
    wraw = m.alloc([8, 128])
    k.dma("sync", wraw, d["sgu_w"][l].rearrange("h p q -> p h q"), (), ["wraw"])
    wT = m.alloc([8, 128], BF16)
    bsb = m.alloc([8])
    k.dma("sync", bsb, d["sgu_b"][l].rearrange("h p -> p h"), (), ["bsb"])
    for h in range(8):
        b = h % 2
        k.transpose(m.bank(b)[:, 0:128], wraw[:, h, :], C.cf["ident"], ["wraw"] + C.keys, [f"ps{b}"])
        k.tt("vector", wT[:, h, :], m.bank(b)[:, 0:128], C.cf["tril_qp"], ALU.mult, [f"ps{b}"] + C.keys, [("wT", h)])
    uv = [m.alloc([1024]) for _ in range(2)]
    t1 = m.alloc([1024])
    t2 = m.alloc([1024])
    gl = [m.alloc([1024]) for _ in range(2)]
    vln = [m.alloc([512], BF16) for _ in range(2)]
    st = m.alloc([16])
    oc = [m.alloc([512]) for _ in range(2)]
    stage = [m.alloc([4, 512]) for _ in range(2)]
    for tt in range(NTT):
        s = tt % 2
        for c in range(4):
            k.dma("sync", stage[s][:, c, 0:128], d["projF"][20 + c][:, tt * 128:(tt + 1) * 128], (), [(f"ufm{s}", c)])
        for c in range(4):
            b = 2 + c % 2
            k.transpose(m.bank(b)[:, 0:128], stage[s][:, c, 0:128], C.cf["ident"], [(f"ufm{s}", c)] + C.keys, [f"ps{b}"])
            k.copy("scalar", uv[s][:, c * 128:(c + 1) * 128], m.bank(b)[:, 0:128], [f"ps{b}"], [(f"uv{s}", c)])
        k.dma("sync", uv[s][:, 512:1024], d["projT"][tt * 128:(tt + 1) * 128, 280:792], (), [(f"uv{s}", 4)])
        gelu_tanh(k, m, gl[s], uv[s], t1, t2, [(f"uv{s}", c) for c in range(5)], [f"gl{s}"], ["t1"])
        k.P.op("vector", (lambda o, i: (lambda e: e.bn_stats(out=o, in_=i)))(st[:, 0:6], gl[s][:, 512:1024]), [f"gl{s}"], ["bst"])
        k.P.op("vector", (lambda o, i: (lambda e: e.bn_aggr(out=o, in_=i)))(st[:, 8:10], st[:, 0:6]), ["bst"], ["bag"])
        k.act(st[:, 10:11], st[:, 9:10], AF.Sqrt, ["bag"] + C.keys, ["lnsd"], scale=1.0, bias=C.eps5[:, 0:1])
        k.recip(st[:, 11:12], st[:, 10:11], ["lnsd"], ["lnrs"])
        k.ts("vector", vln[s], gl[s][:, 512:1024], st[:, 8:9], st[:, 11:12], ALU.subtract, ALU.mult, [f"gl{s}", "bag", "lnrs"], [f"vln{s}"])
        b = 4 + tt % 2
        for h in range(8):
            k.mm(m.bank(b)[:, h * 64:(h + 1) * 64], wT[:, h, :], vln[s][:, h * 64:(h + 1) * 64], True, True,
                 [("wT", h), f"vln{s}"], [f"ps{b}"], signal=(h == 7))
        k.tt("vector", oc[s].rearrange("p (h e) -> p h e", e=64), m.bank(b).rearrange("p (h e) -> p h e", e=64),
             bsb.unsqueeze(2).broadcast_to([128, 8, 64]), ALU.add, [f"ps{b}", "bsb"], [f"oc{s}"])
        k.tt("vector", oc[s], oc[s], gl[s][:, 0:512], ALU.mult, [f"oc{s}", f"gl{s}"], [f"oc{s}"])
        for c in range(4):
            b2 = 2 + c % 2
            k.transpose(m.bank(b2)[:, 128:256], oc[s][:, c * 128:(c + 1) * 128], C.cf["ident"], [f"oc{s}"] + C.keys, [f"ps{b2}"])
            k.copy("scalar", stage[s][:, c, 128:256], m.bank(b2)[:, 128:256], [f"ps{b2}"], [(f"ofm{s}", c)])
            k.dma("sync", d["mixF"][8 + c][:, tt * 128:(tt + 1) * 128], stage[s][:, c, 128:256], [(f"ofm{s}", c)], [("mixF", 8 + c, tt)])


def phase_sb(k, m, C, d, l):
    scale = HD ** -0.5
    qf = m.alloc([2048])
    kf = m.alloc([2048])
    qs = [m.alloc([2048], BF16) for _ in range(2)]
    qn = [m.alloc([2048], BF16) for _ in range(2)]
    kb = [m.alloc([2048], BF16) for _ in range(2)]
    vb = m.alloc([16, 512], BF16)
    k.dma("gpsimd", vb, d["projT"][:, 792:1304].rearrange("(st p) c -> p st c", p=128), (), ["vb"])
    ef = [[m.alloc([512]) for _ in range(3)] for _ in range(2)]
    sp = [[m.alloc([512], BF16) for _ in range(3)] for _ in range(2)]
    aT = [[m.alloc([512], BF16) for _ in range(2)] for _ in range(2)]
    osb = [[m.alloc([512]) for _ in range(2)] for _ in range(2)]
    for j in range(4):
        jj = j % 2
        k.dma("sync", qf, d["projF"][24 + j], (), ["qf"])
        k.dma("sync", kf, d["projF"][28 + j], (), ["kf"])
        k.ts("vector", qs[jj], qf, scale, None, ALU.mult, None, ["qf"], [f"qs{jj}"])
        k.ts("gpsimd", qn[jj], qf, -scale, None, ALU.mult, None, ["qf"], [f"qn{jj}"])
        k.copy("gpsimd", kb[jj], kf, ["kf"], [f"kb{jj}"])
        QS, QN, KB = qs[jj], qn[jj], kb[jj]
        qsk, qnk, kbk = f"qs{jj}", f"qn{jj}", f"kb{jj}"
        for tb in range(4):
            t0 = tb * 512
            steps = list(range(4 * tb + 3, -1, -1))
            for ch in range(2):
                k.mm(m.bank(3 * ch + 1), C.zeros[:, 0:128], C.zeros, True, False, [("zeros",)], [f"ps{3 * ch + 1}"], signal=True)
                k.mm(m.bank(3 * ch + 2)[0:64, :], C.zeros[:, 0:64], C.zeros, True, False, [("zeros",)], [f"ps{3 * ch + 2}"], signal=True)

            def geom(si):
                s0 = si * 128
                c0 = max(0, s0 - t0)
                return s0, c0, s0 >= t0, slice(c0, 512), slice(t0 + c0, t0 + 512)

            def sp_qk(ch, n):
                si = steps[n]
                s0, c0, diag, cols, tcols = geom(si)
                pr = slice(64 * ch, 64 * ch + 64)
                bz = m.bank(3 * ch)
                zk = f"ps{3 * ch}"
                k.mm(bz[:, cols], KB[pr, s0:s0 + 128], QS[pr, tcols], True, True, [kbk, qsk], [zk])

            def sp_act(ch, n):
                si = steps[n]
                s0, c0, diag, cols, tcols = geom(si)
                sl = n % 3
                bz = m.bank(3 * ch)
                zk = f"ps{3 * ch}"
                k.act(ef[ch][sl][:, cols], bz[:, cols], AF.Exp, [zk], [f"ef{ch}{sl}"])
                k.act(sp[ch][sl][:, cols], ef[ch][sl][:, cols], AF.Ln, [f"ef{ch}{sl}"] + C.keys, [f"sp{ch}{sl}"], bias=C.one1[:, 0:1], scale=1.0)
                if diag:
                    k.tt("gpsimd", sp[ch][sl][:, c0:c0 + 128], sp[ch][sl][:, c0:c0 + 128], C.cb["lt_st"], ALU.mult, [f"sp{ch}{sl}"] + C.keys, [f"sp{ch}{sl}"])

            def chain_a(ch, n):
                si = steps[n]
                s0, c0, diag, cols, tcols = geom(si)
                pr = slice(64 * ch, 64 * ch + 64)
                sl = n % 3
                bc = m.bank(3 * ch + 1)
                ck = f"ps{3 * ch + 1}"
                k.mm(bc[:, cols], C.cb["uincl"], sp[ch][sl][:, cols], False, False, [f"sp{ch}{sl}"] + C.keys, [ck], signal=False)
                k.mm(bc[:, cols], KB[pr, s0:s0 + 128], QN[pr, tcols], False, False, [kbk, qnk], [ck], signal=True)

            def chain_b(ch, n):
                si = steps[n]
                s0, c0, diag, cols, tcols = geom(si)
                sl = n % 2
                bc = m.bank(3 * ch + 1)
                ck = f"ps{3 * ch + 1}"
                k.act(aT[ch][sl][:, cols], bc[:, cols], AF.Exp, [ck], [f"aT{ch}{sl}"], scale=-1.0)
                if diag:
                    k.tt("gpsimd", aT[ch][sl][:, c0:c0 + 128], aT[ch][sl][:, c0:c0 + 128], C.cb["lt_st"], ALU.mult, [f"aT{ch}{sl}"] + C.keys, [f"aT{ch}{sl}"])

            def chain_c(ch, n):
                si = steps[n]
                s0, c0, diag, cols, tcols = geom(si)
                pr = slice(64 * ch, 64 * ch + 64)
                sl = n % 2
                h = 2 * j + ch
                bc = m.bank(3 * ch + 1)
                ck = f"ps{3 * ch + 1}"
                bo = m.bank(3 * ch + 2)
                ok_ = f"ps{3 * ch + 2}"
                k.mm(bc[:, cols], KB[pr, s0:s0 + 128], QS[pr, tcols], False, False, [kbk, qsk], [ck], signal=False)
                s3 = n % 3
                k.mm(bc[:, cols], C.cb["lstrict"], sp[ch][s3][:, cols], False, si == 0, [f"sp{ch}{s3}"] + C.keys, [ck], signal=True)
                k.mm(bo[0:64, cols], vb[:, si, h * 64:(h + 1) * 64], aT[ch][sl][:, cols], False, si == 0, ["vb", f"aT{ch}{sl}"], [ok_], signal=True)

            ns = len(steps)
            for n0 in range(min(2, ns)):
                for ch in range(2):
                    sp_qk(ch, n0)
                    sp_act(ch, n0)
            for n in range(ns):
                for ch in range(2):
                    chain_a(ch, n)
                if n + 2 < ns:
                    for ch in range(2):
                        sp_qk(ch, n + 2)
                for ch in range(2):
                    chain_b(ch, n)
                if n + 2 < ns:
                    for ch in range(2):
                        sp_act(ch, n + 2)
                for ch in range(2):
                    chain_c(ch, n)
            for ch in range(2):
                o_ = osb[ch][tb % 2]
                ok_ = f"ps{3 * ch + 2}"
                k.copy("vector", o_[0:64, :], m.bank(3 * ch + 2)[0:64, :], [ok_], [f"osb{ch}{tb % 2}"])
                k.dma("sync", d["mixF"][12 + j][64 * ch:64 * ch + 64, t0:t0 + 512], o_[0:64, :], [f"osb{ch}{tb % 2}"], [("mixF", 12 + j, ch, tb)])


def phase_mixers(k, m, C, d, l, which=("conv", "sgu", "sb", "nsa")):
    if "conv" in which:
        m.reset()
        phase_conv(k, m, C, d, l)
        _barrier(k.P)
    if "sgu" in which:
        m.reset()
        phase_sgu(k, m, C, d, l)
        _barrier(k.P)
    if "sb" in which:
        m.reset()
        phase_sb(k, m, C, d, l)
        _barrier(k.P)
    if "nsa" in which:
        m.reset()
        phase_nsa(k, m, C, d, l)
        _barrier(k.P)


OH_L = 6366
OH_SO, OH_WO, OH_CO = 0, 1535, 2302


def host_nsa_consts():
    oh = np.zeros((33, OH_L), np.float32)
    x = np.arange(1535) - 127
    b = np.where(x < 0, 32, _rel_bucket_np(x))
    oh[b, OH_SO + np.arange(1535)] = 1
    x = np.arange(767) - 127
    b = np.where((x < 0) | (x >= WINDOW), 32, _rel_bucket_np(x))
    oh[b, OH_WO + np.arange(767)] = 1
    x = np.arange(4064) - 2016 - 31
    b = np.where(x < 0, 32, _rel_bucket_np(x))
    oh[b, OH_CO + np.arange(4064)] = 1
    t = np.arange(T)[:, None]
    jj = np.arange(N_SLC)[None, :]
    cur = t // SLC_LEN
    valid = jj * SLC_LEN <= t
    forced = (jj == 0) | (jj == cur) | (jj == cur - 1)
    scadd = np.where(valid, np.where(forced, 1000.0, 0.0), -1e30).astype(np.float32)
    esel = (np.arange(T)[None, :] // SLC_LEN == np.arange(N_SLC)[:, None]).astype(np.float32)
    c0 = np.arange(N_CMP)[:, None] * CMP_STRIDE
    s0 = np.arange(N_SLC)[None, :] * SLC_LEN
    ov = np.minimum(c0 + CMP_LEN, s0 + SLC_LEN) - np.maximum(c0, s0)
    ovc = (np.maximum(ov, 0) / CMP_LEN).astype(np.float32)
    return {"oh": oh, "scadd": scadd, "esel": esel, "ovc": ovc}


def setup_nsa_tables(k, m, C, d):
    tabx = m.alloc([8])
    k.memset("vector", tabx[0:64, :], NEG, ["tabx"])
    k.dma("sync", tabx[0:32, :], d["rel_table"], (), ["tabx"])
    oh = m.alloc([OH_L])
    k.dma("sync", oh[0:33, :], d["oh"], (), ["oh"])
    row = [m.alloc([OH_L]) for _ in range(2)]
    bi = 0
    for h in range(8):
        r_ = row[h % 2]
        for c0 in range(0, OH_L, 512):
            n = min(512, OH_L - c0)
            b = bi % 6
            bi += 1
            k.mm(m.bank(b)[:, 0:n], tabx[0:33, h:h + 1].broadcast_to([33, 128]), oh[0:33, c0:c0 + n], True, True, ["tabx", "oh"], [f"ps{b}"])
            k.copy("scalar" if bi % 2 else "vector", r_[:, c0:c0 + n], m.bank(b)[:, 0:n], [f"ps{b}"], [(f"row{h % 2}", c0)])
        k.dma("sync", d["R"][h], r_, [(f"row{h % 2}", c0) for c0 in range(0, OH_L, 512)], [("R", h)], war=[f"rowall{h % 2}"])


def phase_nsa(k, m, C, d, l):
    scale = HD ** -0.5
    STOP = float(os.environ.get("NSA_STOP", "99"))
    if STOP <= 0:
        return
    Rt = d["R"].tensor
    tabW = m.alloc([8, 2048], BF16)
    Wc = tabW
    Ws = tabW[:, :, 0:1408]
    Ww = tabW[:, :, 1408:2048]
    for h in range(8):
        base = h * 128 * OH_L
        k.dma("gpsimd", Wc[0:127, h, :], bass.AP(tensor=Rt, offset=base + OH_CO + 2016, ap=[[OH_L - 16, 127], [1, 2048]]), (), [("Wc", h)])
    esel = m.alloc([2048], BF16)
    k.dma("gpsimd", esel[0:32, :], d["esel"], (), ["esel"])
    scadd = m.alloc([16, 32])
    k.dma("sync", scadd, d["scadd"].rearrange("(tt p) j -> p tt j", p=128), (), ["scadd"])
    qg = m.alloc([2])
    kg = m.alloc([1])
    for half in range(2):
        k.dma("sync", qg[64 * half:64 * half + 64, 0:1], d["q_gain"][l].rearrange("(d o) -> d o", o=1), (), [("qg", half)])
        k.dma("sync", kg[64 * half:64 * half + 64, 0:1], d["k_gain"][l].rearrange("(d o) -> d o", o=1), (), [("kg", half)])
    k.ts("vector", qg[:, 1:2], qg[:, 0:1], scale, None, ALU.mult, None, [("qg", 0), ("qg", 1)], ["qgs"])
    qT = m.alloc([4, 2048], BF16)
    ksT = m.alloc([2048], BF16)
    kwT = m.alloc([2048], BF16)
    vx = m.alloc([16, 4 * 65], BF16)
    gt = m.alloc([16, 24])
    cacc = m.alloc([16, 8 * 97])
    rz = m.alloc([16, 8])
    negT = m.alloc([2, 2048], BF16)
    mark = m.top
    xf = m.alloc([2048])
    sq = m.alloc([2048], BF16)
    rs = [m.alloc([512]) for _ in range(2)]
    bi = [0]

    def nb():
        b = bi[0] % 6
        bi[0] += 1
        return b

    def headnorm(chunk, gain, gkeys, dst, dkey):
        k.dma("sync", xf, d["projF"][chunk], (), ["xf"])
        k.act(sq, xf, AF.Square, ["xf"], ["sq"])
        for tb in range(4):
            b = nb()
            cs = slice(tb * 512, (tb + 1) * 512)
            k.mm(m.bank(b), C.cb["blk64"], sq[:, cs], True, True, ["sq"] + C.keys, [f"ps{b}"])
            r_ = rs[tb % 2]
            k.act(r_, m.bank(b), AF.Sqrt, [f"ps{b}"] + C.keys, [f"rs{tb % 2}"], scale=1.0 / HD, bias=C.eps6[:, 0:1])
            k.recip(r_, r_, [f"rs{tb % 2}"], [f"rs{tb % 2}"])
            k.stt(dst[:, cs], xf[:, cs], gain, r_, ALU.mult, ALU.mult, ["xf", f"rs{tb % 2}"] + gkeys, [dkey])

    for j in range(4):
        headnorm(j, qg[:, 1:2], ["qgs"], qT[:, j, :], ("qT", j))
    headnorm(6, kg[:, 0:1], [("kg", 0), ("kg", 1)], ksT, "ksT")
    headnorm(7, kg[:, 0:1], [("kg", 0), ("kg", 1)], kwT, "kwT")
    kcb = m.alloc([2048], BF16)
    vcb = m.alloc([2048], BF16)
    k.dma("gpsimd", kcb, d["projF"][4], (), ["kcb"])
    k.dma("gpsimd", vcb, d["projF"][5], (), ["vcb"])
    k.memset("vector", vx, 1.0, ["vx"])
    vx5 = vx.rearrange("p st (a e) -> p st a e", e=65)
    for a in range(4):
        k.dma("gpsimd", vx5[:, :, a, 0:64], d["projT"][:, a * 64:(a + 1) * 64].rearrange("(st p) c -> p st c", p=128), ["vx"], [("vx", a)])
    vxk = ["vx"] + [("vx", a) for a in range(4)]
    k.dma("sync", gt, d["projT"][:, 256:280].rearrange("(tt p) c -> p tt c", p=128), (), ["gt"])
    k.act(gt, gt, AF.Sigmoid, ["gt"], ["gt"])
    if STOP <= 1:
        return
    W1 = m.alloc([2, 32, 128], BF16)
    W2 = m.alloc([2, 64], BF16)
    posT = m.alloc([2, 32], BF16)
    posF = m.alloc([2, 32])
    for half in range(2):
        pr = slice(64 * half, 64 * half + 64)
        for i in range(2):
            for dup in range(2):
                k.dma("gpsimd", W1[pr, i, :, dup * 64:(dup + 1) * 64], d["cmp_w1"][l, i].rearrange("(l d) e -> d l e", d=64), (), [("W1", half, i, dup)])
        k.dma("sync", posF[pr], d["cmp_pos"][l].rearrange("i l d -> d i l"), (), [("posF", half)])
        k.copy("vector", posT[pr], posF[pr], [("posF", half)], [("posT", half)])
    k.dma("gpsimd", W2[0:64], d["cmp_w2"][l].rearrange("i e f -> e i f"), (), ["W2"])
    wkeys = [("W1", a, b_, c_) for a in range(2) for b_ in range(2) for c_ in range(2)] + [("posT", 0), ("posT", 1), "W2"]
    if STOP <= 1.2:
        return
    hf = m.alloc([256])
    zb = m.alloc([32, 127], BF16)
    hid = m.alloc([256], BF16)
    t1 = m.alloc([256])
    t2 = m.alloc([256])
    kcT = m.alloc([128], BF16)
    k.memset("vector", kcT, 0.0, [("kcT", 0), ("kcT", 1)])
    kcn = m.alloc([256], BF16)
    rc = m.alloc([2, 97], BF16)
    k.memset("vector", rc, 1.0, ["rc"])
    ovf = m.alloc([32])
    k.dma("sync", ovf[0:127, :], d["ovc"], (), ["ovf"])
    for g in range(2):
        k.copy("vector", rc[0:127, g, 65:97], ovf[0:127, :], ["rc", "ovf"], [("rc", "ov", g)])
    if STOP <= 1.25:
        return
    for i, src, skey in ((0, kcb, "kcb"), (1, vcb, "vcb")):
        b = nb()
        sview = bass.AP(tensor=src.tensor, offset=src.offset, ap=[[src.ap[0][0], 128], [1, 32], [16, 127]])
        k.tt("vector", zb, sview, posT[:, i, :].unsqueeze(2).broadcast_to([128, 32, 127]), ALU.add, [skey, ("posT", 0), ("posT", 1)], ["zb"])
        if STOP <= 1.3:
            return
        for g in range(2):
            bg_ = nb()
            pr = slice(64 * g, 64 * g + 64)
            o_ = m.bank(bg_)[:, 0:127]
            for li in range(32):
                k.mm(o_, W1[pr, i, li, :], zb[pr, li, :], li == 0, li == 31, wkeys + ["zb"], [f"ps{bg_}"], signal=(li == 31))
            k.copy("vector", hf[0:64, g * 128:(g + 1) * 128], m.bank(bg_)[0:64, 0:128], [f"ps{bg_}"], [("hf", g)])
        if STOP <= 1.35:
            return
        k.tt("vector", hf[0:64, 0:1], hf[0:64, 0:1], hf[0:64, 0:1], ALU.max, [("hf", 0), ("hf", 1)], ["hf"])
        if STOP <= 1.4:
            return
        gelu_tanh(k, m, hid[0:64, :], hf[0:64, :], t1[0:64, :], t2[0:64, :], ["hf"], ["hid"], ["ct1"])
        if STOP <= 1.5:
            return
        if i == 0:
            b2 = nb()
            for g in range(2):
                k.mm(m.bank(b2)[0:64, g * 128:g * 128 + 127], W2[0:64, 0, :], hid[0:64, g * 128:g * 128 + 127], True, True, ["W2", "hid"], [f"ps{b2}"])
            k.copy("vector", hf[0:64, :], m.bank(b2)[0:64, 0:256], [f"ps{b2}"], ["hf"])
            k.act(sq[0:64, 0:256], hf[0:64, :], AF.Square, ["hf"], ["sq"])
            b3 = nb()
            k.mm(m.bank(b3)[0:64, 0:256], C.cb["ones"][0:64, 0:64], sq[0:64, 0:256], True, True, ["sq"] + C.keys, [f"ps{b3}"])
            k.act(t1[0:64, :], m.bank(b3)[0:64, 0:256], AF.Sqrt, [f"ps{b3}"] + C.keys, ["ct1"], scale=1.0 / HD, bias=C.eps6[0:64, 0:1])
            k.recip(t1[0:64, :], t1[0:64, :], ["ct1"], ["ct1"])
            k.stt(kcn[0:64, :], hf[0:64, :], kg[0:64, 0:1], t1[0:64, :], ALU.mult, ALU.mult, ["hf", "ct1", ("kg", 0)], ["kcn"])
            k.copy("vector", kcT[0:64, 0:127], kcn[0:64, 0:127], ["kcn"], [("kcT", 0)])
            k.copy("vector", kcT[64:128, 0:127], kcn[0:64, 128:255], ["kcn"], [("kcT", 1)])
        else:
            b2 = nb()
            for g in range(2):
                k.mm(m.bank(b2)[0:127, g * 64:(g + 1) * 64], hid[0:64, g * 128:g * 128 + 127], W2[0:64, 1, :], True, True, ["W2", "hid"], [f"ps{b2}"])
            k.copy("vector", rc[0:127, :, 0:64], m.bank(b2)[0:127, 0:128].rearrange("p (g e) -> p g e", e=64), [f"ps{b2}", "rc"], [("rc", "v")])
    rck = ["rc", ("rc", "ov", 0), ("rc", "ov", 1), ("rc", "v")]
    if STOP <= 2:
        return
    cacc4 = cacc.rearrange("p tt (h e) -> p tt h e", e=97)
    ecT = [m.alloc([512], BF16) for _ in range(2)]
    ei = 0
    for j in range(4):
        for g in range(2):
            h = 4 * g + j
            pr = slice(64 * g, 64 * g + 64)
            for tb in range(4):
                cs = slice(tb * 512, (tb + 1) * 512)
                b = nb()
                k.mm(m.bank(b)[:, :], kcT[pr, 0:128], qT[pr, j, cs], True, False, [("kcT", g), ("qT", j)], [f"ps{b}"], signal=False)
                k.mm(m.bank(b)[0:127, :], C.cb["ident"][0:127, 0:127], Wc[0:127, h, cs], False, True, [("Wc", h)] + C.keys, [f"ps{b}"])
                e_ = ecT[ei % 2]
                k.act(e_[0:127, :], m.bank(b)[0:127, :], AF.Exp, [f"ps{b}"], [f"ecT{ei % 2}"])
                for t4 in range(4):
                    tt = 4 * tb + t4
                    b2 = nb()
                    k.mm(m.bank(b2)[:, 0:97], e_[0:127, t4 * 128:(t4 + 1) * 128], rc[0:127, g, :], True, True, [f"ecT{ei % 2}"] + rck, [f"ps{b2}"])
                    k.copy("scalar" if t4 % 2 else "vector", cacc4[:, tt, h, :], m.bank(b2)[:, 0:97], [f"ps{b2}"], [("cacc", tt, h)])
                ei += 1
    if STOP <= 3:
        return
    k.ts("vector", rz, cacc4[:, :, :, 64], 1e-30, None, ALU.max, None, [("cacc", tt, h) for tt in range(16) for h in range(8)], ["rz"])
    k.recip(rz, rz, ["rz"], ["rz"])
    sc = m.alloc([32])
    sc2 = m.alloc([32])
    m8 = m.alloc([16])
    ngm = m.alloc([32])
    for tt in range(NTT):
        for g in range(2):
            for r in range(4):
                h = 4 * g + r
                if r == 0:
                    k.stt(sc, cacc4[:, tt, h, 65:97], rz[:, tt, h:h + 1], scadd[:, tt, :], ALU.mult, ALU.add, ["rz", "scadd"], ["sc"])
                else:
                    k.stt(sc, cacc4[:, tt, h, 65:97], rz[:, tt, h:h + 1], sc, ALU.mult, ALU.add, ["rz", "sc"], ["sc"])
            k.P.op("vector", (lambda o, i_: (lambda e: e.max(out=o, in_=i_)))(m8[:, 0:8], sc), ["sc"], ["m8a"])
            k.P.op("vector", (lambda o, r_, v_: (lambda e: e.match_replace(out=o, in_to_replace=r_, in_values=v_, imm_value=-3.0e38)))(sc2, m8[:, 0:8], sc), ["sc", "m8a"], ["sc2"])
            k.P.op("vector", (lambda o, i_: (lambda e: e.max(out=o, in_=i_)))(m8[:, 8:16], sc2), ["sc2"], ["m8b"])
            k.ts("vector", ngm, sc, m8[:, 15:16], NEG, ALU.is_lt, ALU.mult, ["sc", "m8b"], ["ngm"])
            b = nb()
            k.transpose(m.bank(b)[0:32, 0:128], ngm, C.cf["ident"], ["ngm"] + C.keys, [f"ps{b}"])
            k.copy("scalar", negT[0:32, g, tt * 128:(tt + 1) * 128], m.bank(b)[0:32, 0:128], [f"ps{b}"], [("negT", g, tt)])
    if STOP <= 4:
        return
    _barrier(k.P)
    m.top = mark
    for h in range(8):
        base = h * 128 * OH_L
        k.dma("gpsimd", Ws[:, h, :], bass.AP(tensor=Rt, offset=base + OH_SO + 127, ap=[[OH_L - 1, 128], [1, 1408]]), (), [("Ws", h)])
        k.dma("gpsimd", Ww[:, h, :], bass.AP(tensor=Rt, offset=base + OH_WO + 127, ap=[[OH_L - 1, 128], [1, 640]]), (), [("Ww", h)])
    oa = m.alloc([16, 512])
    PT = [m.alloc([512], BF16) for _ in range(3)]
    sacc = [m.alloc([4, 65]) for _ in range(2)]
    cf_ = m.alloc([16])
    tmp = m.alloc([4, 64])
    stage = m.alloc([4, 128])
    items = []
    gi = 0
    for tb in range(4):
        for h in range(8):
            for br in range(2):
                si_lo = 0 if br == 0 else max(0, 4 * tb - 4)
                si_list = list(range(si_lo, 4 * tb + 4))
                for si in si_list:
                    items.append(dict(tb=tb, h=h, br=br, si=si, first=(si == si_list[0]), last=(si == si_list[-1]), gi=gi,
                                      last_of_tb=(h == 7 and br == 1 and si == si_list[-1])))
                gi += 1

    def geom(it):
        tb, si, br = it["tb"], it["si"], it["br"]
        t0 = tb * 512
        s0 = si * 128
        c0 = max(0, s0 - t0)
        c1 = 512 if br == 0 else min(512, s0 + 640 - t0)
        return t0, s0, c0, c1

    def score_stage(n):
        it = items[n]
        tb, h, br, si = it["tb"], it["h"], it["br"], it["si"]
        g = h // 4
        j = h % 4
        pr = slice(64 * g, 64 * g + 64)
        t0, s0, c0, c1 = geom(it)
        cols = slice(c0, c1)
        tcols = slice(t0 + c0, t0 + c1)
        b = n % 4
        p_ = PT[n % 3]
        pk = f"PT{n % 3}"
        kT_ = ksT if br == 0 else kwT
        W_ = Ws if br == 0 else Ww
        m0 = t0 + c0 - s0
        k.mm(m.bank(b)[:, cols], kT_[pr, s0:s0 + 128], qT[pr, j, tcols], True, False, [], [f"ps{b}"], signal=False)
        if br == 0 and m0 + (c1 - c0) > 1408:
            for ca in range(c0, c1, 256):
                cb_ = min(c1, ca + 256)
                k.mm(m.bank(b)[:, ca:cb_], C.cb["ident"], W_[:, h, 1152:1152 + (cb_ - ca)], False, False, [("Ws", h)], [f"ps{b}"], signal=False)
        else:
            k.mm(m.bank(b)[:, cols], C.cb["ident"], W_[:, h, m0:m0 + (c1 - c0)], False, br == 1, [("Ws", h), ("Ww", h)], [f"ps{b}"], signal=(br == 1))
        if br == 0:
            k.mm(m.bank(b)[:, cols], esel[0:32, s0:s0 + 128], negT[0:32, g, tcols], False, True, ["esel"], [f"ps{b}"])
        k.act(p_[:, cols], m.bank(b)[:, cols], AF.Exp, [f"ps{b}"], [pk])

    def pv_stage(n):
        it = items[n]
        tb, h, br, si = it["tb"], it["h"], it["br"], it["si"]
        g = h // 4
        t0, s0, c0, c1 = geom(it)
        bo = 4 + (it["gi"] % 2)
        bok = f"ps{bo}"
        bank_o = m.bank(bo)[:, 0:260].rearrange("p (a e) -> p a e", e=65)
        p_ = PT[n % 3]
        pk = f"PT{n % 3}"
        if it["first"]:
            k.mm(m.bank(bo)[:, 0:260], C.zeros[:, 0:128], C.zeros[:, 0:260], True, False, [("zeros",)], [bok], signal=True)
        for t4 in range(4):
            if t4 * 128 < c0 or t4 * 128 >= c1:
                continue
            a = br * 2 + g
            k.mm(bank_o[:, t4, :], p_[:, t4 * 128:(t4 + 1) * 128], vx5[:, si, a, :], False, it["last"] and t4 == 3, [pk], [bok], signal=True)
        if not it["last"]:
            return
        sa = sacc[it["gi"] % 2]
        sk = f"sacc{it['gi'] % 2}"
        k.copy("vector", sa, bank_o, [bok], [sk])
        k.ts("vector", cf_[:, 0:4], sa[:, :, 64], 1e-30, None, ALU.max, None, [sk], ["cf"])
        k.recip(cf_[:, 0:4], cf_[:, 0:4], ["cf"], ["cf"])
        k.tt("vector", cf_[:, 4:8], cf_[:, 0:4], gt[:, 4 * tb:4 * tb + 4, 3 * h + 1 + br], ALU.mult, ["cf"], ["cf2"])
        dst = oa[:, 4 * tb:4 * tb + 4, h * 64:(h + 1) * 64]
        if br == 0:
            k.tt("vector", cf_[:, 8:12], rz[:, 4 * tb:4 * tb + 4, h], gt[:, 4 * tb:4 * tb + 4, 3 * h], ALU.mult, [], ["cf3"])
            k.tt("vector", dst, cacc4[:, 4 * tb:4 * tb + 4, h, 0:64], cf_[:, 8:12].unsqueeze(2).broadcast_to([128, 4, 64]), ALU.mult, ["cf3"], [("oa", tb, h)])
        k.tt("vector", tmp, sa[:, :, 0:64], cf_[:, 4:8].unsqueeze(2).broadcast_to([128, 4, 64]), ALU.mult, [sk, "cf2"], ["tmpo"])
        k.tt("vector", dst, dst, tmp, ALU.add, ["tmpo", ("oa", tb, h)], [("oa", tb, h)])
        if it["last_of_tb"]:
            for t4 in range(4):
                tt = 4 * tb + t4
                for c in range(4):
                    b2 = 6 + (c % 2)
                    k.transpose(m.bank(b2)[:, 0:128], oa[:, tt, c * 128:(c + 1) * 128], C.cf["ident"], [("oa", tb, h_) for h_ in (2 * c, 2 * c + 1)], [f"ps{b2}"])
                    k.copy("scalar", stage[:, c, :], m.bank(b2)[:, 0:128], [f"ps{b2}"], [("stg", c)])
                    k.dma("sync", d["mixF"][c][:, tt * 128:(tt + 1) * 128], stage[:, c, :], [("stg", c)], [("mixF", c, tt)])

    for n in range(len(items) + 1):
        if n < len(items):
            score_stage(n)
        if n >= 1:
            pv_stage(n - 1)


_CST = None


def _get_cst():
    global _CST
    if _CST is None:
        c = host_consts()
        _CST = np.ascontiguousarray(np.concatenate([c[n] for n in ["ident", "ones", "blk64", "tril_qp", "lt_st", "uincl", "lstrict"]], axis=1).astype(np.float32))
    return _CST


def kernel(**inputs):
    x = np.asarray(inputs["x"], dtype=np.float32)
    nc = build()
    shared = {name: np.ascontiguousarray(np.asarray(inputs[name], dtype=np.float32)) for name, _ in INPUT_SPECS if name != "x"}
    shared["cst"] = _get_cst()
    shared.update(host_nsa_consts())
    in_maps = []
    for b in range(8):
        mp = dict(shared)
        mp["x"] = np.ascontiguousarray(x[b])
        in_maps.append(mp)
    res = run_bass_kernel_spmd(nc, in_maps, core_ids=list(range(8)))
    return np.stack([np.asarray(r["out"], dtype=np.float32) for r in res.results], axis=0)
```

```python
import math
import os
import numpy as np
import ml_dtypes
import concourse.bass as bass
import concourse.mybir as mybir
from concourse.bass_utils import run_bass_kernel_spmd

F32 = mybir.dt.float32
BF16 = mybir.dt.bfloat16
AF = mybir.ActivationFunctionType
ALU = mybir.AluOpType
AX = mybir.AxisListType

D_MODEL = 2048
T = 2048
DEPTH = 4
HD = 64
GW = 512
NH = 8
G = 2
R = 4
CMP_LEN = 32
CMP_STRIDE = 16
N_CMP = 127
SLC_LEN = 64
N_SLC = 32
SLC_TOP = 16
WINDOW = 512
N_BUCKETS = 32
MAX_DISTANCE = 1024
D_FF = 5632
W_IN_COLS = 5400
NEG = -30000.0
NKC = D_MODEL // 128
NTT = T // 128
NFC = D_FF // 128


class Prog:
    ENGS = ("tensor", "vector", "scalar", "gpsimd", "sync")

    def __init__(self, nc, n_dma_sems=10):
        self.nc = nc
        self.ops = {e: [] for e in self.ENGS}
        self.sem = {e: nc.alloc_semaphore(name=f"sem_{e}") for e in ("tensor", "vector", "scalar", "gpsimd")}
        self.cnt = {e: 0 for e in self.sem}
        nsem = {"sync": n_dma_sems, "gpsimd": 4}
        self.dma_sems = {q: [nc.alloc_semaphore(name=f"dsem_{q}{i}") for i in range(nsem[q])] for q in ("sync", "gpsimd")}
        self.dma_cnt = {q: [0] * nsem[q] for q in ("sync", "gpsimd")}
        self.dma_rr = {q: 0 for q in ("sync", "gpsimd")}
        self.waited = {e: {} for e in self.ENGS}
        self.last_w = {}
        self.readers = {}
        self.sem_by_id = {}
        self.pending = {}

    def _ev_id(self, sem):
        i = id(sem)
        self.sem_by_id[i] = sem
        return i

    def _need(self, eng, ev, waits):
        if ev is None:
            return
        sid, val, src_eng = ev
        if src_eng == "tensor" and eng == "tensor":
            return
        if self.waited[eng].get(sid, 0) >= val:
            return
        waits[sid] = max(waits.get(sid, 0), val)

    def op(self, eng, fn, reads=(), writes=(), signal=True):
        waits = {}
        for k in reads:
            self._need(eng, self.last_w.get(k), waits)
        for k in writes:
            self._need(eng, self.last_w.get(k), waits)
            for ev in self.readers.get(k, {}).values():
                self._need(eng, ev, waits)
        for sid, val in waits.items():
            self.waited[eng][sid] = val
        ev = None
        if signal:
            self.cnt[eng] += 1
            ev = (self._ev_id(self.sem[eng]), self.cnt[eng], eng)
        self.ops[eng].append((list(waits.items()), fn, (self.sem[eng], 1) if signal else None))
        if ev is not None:
            pr, pw = self.pending.pop(eng, ([], []))
            for k in list(reads) + pr:
                self.readers.setdefault(k, {})[ev[0]] = ev
            for k in list(writes) + pw:
                self.last_w[k] = ev
                self.readers[k] = {}
        else:
            assert eng == "tensor"
            pr, pw = self.pending.setdefault(eng, ([], []))
            pr.extend(reads)
            pw.extend(writes)
        return ev

    def dma(self, q, fn, reads=(), writes=(), war=()):
        waits = {}
        for k in war:
            for ev in self.readers.get(k, {}).values():
                self._need(q, ev, waits)
        for k in reads:
            self._need(q, self.last_w.get(k), waits)
        for k in writes:
            self._need(q, self.last_w.get(k), waits)
            for ev in self.readers.get(k, {}).values():
                self._need(q, ev, waits)
        i = self.dma_rr[q]
        self.dma_rr[q] = (i + 1) % len(self.dma_sems[q])
        s = self.dma_sems[q][i]
        sid = self._ev_id(s)
        prev = self.dma_cnt[q][i]
        if prev > 0 and self.waited[q].get(sid, 0) < prev:
            waits[sid] = max(waits.get(sid, 0), prev)
        for sd, val in waits.items():
            self.waited[q][sd] = val
        self.dma_cnt[q][i] = prev + 16
        ev = (sid, prev + 16, "dma_" + q)
        self.ops[q].append((list(waits.items()), fn, (s, 16)))
        for k in reads:
            self.readers.setdefault(k, {})[("d", q, i)] = ev
        for k in writes:
            self.last_w[k] = ev
            self.readers[k] = {}
        return ev

    def finish(self, out_keys):
        waits = {}
        for k in out_keys:
            self._need("sync", self.last_w.get(k), waits)
        final_waits = list(waits.items())
        nc = self.nc
        prog = self

        def emit(e, name):
            for waits_, fn, sig in prog.ops[name]:
                for sid, val in waits_:
                    e.wait_ge(prog.sem_by_id[sid], val)
                if fn is None:
                    continue
                inst = fn(e)
                if sig is not None:
                    inst.then_inc(sig[0], sig[1])

        with nc.Block() as block:
            @block.tensor
            def _(e):
                emit(e, "tensor")

            @block.vector
            def _(e):
                emit(e, "vector")

            @block.scalar
            def _(e):
                emit(e, "scalar")

            @block.gpsimd
            def _(e):
                emit(e, "gpsimd")

            @block.sync
            def _(e):
                emit(e, "sync")
                for q in ("sync", "gpsimd"):
                    for i, s_ in enumerate(prog.dma_sems[q]):
                        if prog.dma_cnt[q][i] > 0:
                            e.wait_ge(s_, prog.dma_cnt[q][i])


class K:
    def __init__(self, nc):
        self.nc = nc
        self.P = Prog(nc)
        self.ps_rr = 0

    def act(self, out, in_, func, reads, writes, **kw):
        return self.P.op("scalar", lambda e: e.activation(out=out, in_=in_, func=func, **kw), reads, writes)

    def ts(self, eng, out, in0, s1, s2, op0, op1, reads, writes, **kw):
        if op1 is None:
            return self.P.op(eng, lambda e: e.tensor_scalar(out=out, in0=in0, scalar1=s1, scalar2=None, op0=op0, **kw), reads, writes)
        return self.P.op(eng, lambda e: e.tensor_scalar(out=out, in0=in0, scalar1=s1, scalar2=s2, op0=op0, op1=op1, **kw), reads, writes)

    def tt(self, eng, out, in0, in1, op, reads, writes):
        return self.P.op(eng, lambda e: e.tensor_tensor(out=out, in0=in0, in1=in1, op=op), reads, writes)

    def stt(self, out, in0, scalar, in1, op0, op1, reads, writes):
        return self.P.op("vector", lambda e: e.scalar_tensor_tensor(out=out, in0=in0, scalar=scalar, in1=in1, op0=op0, op1=op1), reads, writes)

    def copy(self, eng, out, in_, reads, writes):
        if eng == "scalar":
            return self.P.op("scalar", lambda e: e.activation(out=out, in_=in_, func=AF.Copy), reads, writes)
        return self.P.op(eng, lambda e: e.tensor_copy(out=out, in_=in_), reads, writes)

    def recip(self, out, in_, reads, writes):
        return self.P.op("vector", lambda e: e.reciprocal(out=out, in_=in_), reads, writes)

    def memset(self, eng, ap, val, writes):
        return self.P.op(eng, lambda e: e.memset(ap, val), (), writes)

    def mm(self, out, lhsT, rhs, start, stop, reads, writes, signal=None, **kw):
        if signal is None:
            signal = stop
        return self.P.op("tensor", lambda e: e.matmul(out, lhsT, rhs, start=start, stop=stop, **kw), reads, writes, signal=signal)

    def transpose(self, out, in_, ident, reads, writes, signal=True):
        return self.P.op("tensor", lambda e: e.transpose(out, in_, ident), reads, writes, signal=signal)

    def dma(self, q, out, in_, reads, writes, war=()):
        return self.P.dma(q, lambda e: e.dma_start(out=out, in_=in_), reads, writes, war)


class Mem:
    def __init__(self, nc):
        self.big = nc.alloc_sbuf_tensor("big", [128, 192 * 256], F32)
        self.top = 0
        self.floor = 0
        self.ps = nc.alloc_psum_tensor("ps", [128, 8 * 512], F32)

    def alloc(self, free_shape, dtype=F32):
        n = int(np.prod(free_shape))
        words = n if dtype == F32 else (n + 1) // 2
        words = (words + 15) // 16 * 16
        a = self.top
        self.top += words
        assert self.top <= 192 * 256, f"SBUF overflow {self.top}"
        v = self.big[:, a:a + words]
        if dtype != F32:
            v = v.bitcast(dtype)
        v = v[:, 0:n]
        if len(free_shape) == 2:
            v = v.rearrange("p (a b) -> p a b", b=free_shape[1])
        elif len(free_shape) == 3:
            v = v.rearrange("p (a b c) -> p a b c", b=free_shape[1], c=free_shape[2])
        return v

    def set_floor(self):
        self.floor = self.top

    def reset(self):
        self.top = self.floor

    def bank(self, i, dtype=F32):
        v = self.ps[:, i * 512:(i + 1) * 512]
        if dtype != F32:
            v = v.bitcast(dtype)
        return v


def _barrier(P):
    evs = []
    for e, s in P.sem.items():
        if P.cnt[e] > 0:
            evs.append((P._ev_id(s), P.cnt[e]))
    for q in ("sync", "gpsimd"):
        for i, s in enumerate(P.dma_sems[q]):
            if P.dma_cnt[q][i] > 0:
                evs.append((P._ev_id(s), P.dma_cnt[q][i]))
    for eng in P.ENGS:
        waits = []
        for sid, val in evs:
            if P.waited[eng].get(sid, 0) < val:
                waits.append((sid, val))
                P.waited[eng][sid] = val
        if waits:
            P.ops[eng].append((waits, None, None))
    P.last_w = {}
    P.readers = {}


FM_GROUPS = [
    [(128 * j, 64 * j, 64) for j in range(4)] + [(128 * j + 64, 64 * (4 + j), 64) for j in range(4)],
    [(0, 512, 128), (128, 640, 128), (256, 768, 128), (384, 1024, 128)],
    [(0, 1304, 512)], [(0, 1816, 512)], [(0, 2328, 512)],
    [(0, 2840, 512)],
    [(0, 3864, 512)], [(0, 4376, 512)],
]
TM_BLOCKS = [
    (0, [(0, 896, 128), (128, 1152, 128), (256, 1280, 24)], 280),
    (280, [(0, 3352, 512)], 512),
    (792, [(0, 4888, 512)], 512),
]
N_FM = 32
N_TM = 1304


def _rel_bucket_np(n):
    n = np.maximum(n, 0)
    max_exact = N_BUCKETS // 2
    nf = np.maximum(n, 1).astype(np.float32)
    large = max_exact + (np.log(nf / np.float32(max_exact)) / np.float32(math.log(MAX_DISTANCE / max_exact)) * np.float32(N_BUCKETS - max_exact)).astype(np.int32)
    large = np.minimum(large, N_BUCKETS - 1)
    return np.where(n < max_exact, n, large)


def host_consts():
    c = {}
    c["ident"] = np.eye(128, dtype=np.float32)
    c["ones"] = np.ones((128, 128), np.float32)
    blk = np.zeros((128, 128), np.float32)
    blk[:64, :64] = 1
    blk[64:, 64:] = 1
    c["blk64"] = blk
    i = np.arange(128)
    c["tril_qp"] = (i[:, None] <= i[None, :]).astype(np.float32)
    c["lt_st"] = (i[:, None] < i[None, :]).astype(np.float32)
    c["uincl"] = (i[:, None] >= i[None, :]).astype(np.float32)
    c["lstrict"] = (i[:, None] < i[None, :]).astype(np.float32)
    return c


class Ctx:
    pass


def setup_consts(k, m, d, C):
    nc = k.nc
    C.cf = {}
    C.cb = {}
    names = ["ident", "ones", "blk64", "tril_qp", "lt_st", "uincl", "lstrict"]
    for i, n in enumerate(names):
        if n in ("ident", "tril_qp", "ones"):
            t = m.alloc([128])
            k.dma("sync", t, d["cst"][:, i * 128:(i + 1) * 128], (), [("cf", n)])
            C.cf[n] = t
        tb = m.alloc([128], BF16)
        k.dma("gpsimd", tb, d["cst"][:, i * 128:(i + 1) * 128], (), [("cb", n)])
        C.cb[n] = tb
    C.zeros = m.alloc([512], BF16)
    k.memset("vector", C.zeros, 0.0, [("zeros",)])
    C.eps6 = m.alloc([1])
    k.memset("vector", C.eps6, 1e-6, [("eps6",)])
    C.eps5 = m.alloc([1])
    k.memset("vector", C.eps5, 1e-5, [("eps5",)])
    C.one1 = m.alloc([1])
    k.memset("vector", C.one1, 1.0, [("one1",)])
    C.gmix = m.alloc([DEPTH * 16])
    C.gffn = m.alloc([DEPTH * 16])
    C.ggrp = m.alloc([DEPTH * 16])
    for nm, t in (("norm_mix", C.gmix), ("norm_ffn", C.gffn), ("group_gain", C.ggrp)):
        k.dma("sync", t, d[nm].rearrange("l (kc p) -> p (l kc)", p=128), (), [("g", nm)])
    C.keys = [("cf", n) for n in C.cf] + [("cb", n) for n in C.cb] + [("zeros",), ("eps6",), ("eps5",), ("one1",), ("g", "norm_mix"), ("g", "norm_ffn"), ("g", "group_gain")]


def phase_norm(k, m, C, x_ap, gvec, hT):
    xt = [m.alloc([2048]) for _ in range(2)]
    xs = [m.alloc([2048], BF16) for _ in range(2)]
    junk = m.alloc([2048], BF16)
    st = m.alloc([64])
    for tt in range(NTT):
        s = tt % 2
        k.dma("sync", xt[s], x_ap[tt * 128:(tt + 1) * 128, :], (), [f"xt{s}"])
        k.act(junk, xt[s], AF.Square, [f"xt{s}"], ["junk", ("ss", tt)], accum_out=st[:, tt:tt + 1])
        k.act(st[:, 16 + tt:17 + tt], st[:, tt:tt + 1], AF.Sqrt, [("ss", tt), ("eps6",)], [("sq", tt)], scale=1.0 / D_MODEL, bias=C.eps6[:, 0:1])
        k.recip(st[:, 32 + tt:33 + tt], st[:, 16 + tt:17 + tt], [("sq", tt)], [("rs", tt)])
        k.ts("vector", xs[s], xt[s], st[:, 32 + tt:33 + tt], None, ALU.mult, None, [f"xt{s}", ("rs", tt)], [f"xs{s}"])
        for half in range(2):
            pst = m.bank(6 + half, BF16)
            for j in range(8):
                kc = half * 8 + j
                k.transpose(pst[:, j * 128:(j + 1) * 128], xs[s][:, kc * 128:(kc + 1) * 128], C.cb["ident"],
                            [f"xs{s}", ("cb", "ident")], [f"pst{half}"], signal=(j == 7))
            k.tt("vector", hT[:, half * 8:(half + 1) * 8, tt * 128:(tt + 1) * 128],
                 pst.rearrange("p (a b) -> p a b", b=128),
                 gvec[:, half * 8:(half + 1) * 8].unsqueeze(2).broadcast_to([128, 8, 128]),
                 ALU.mult, [f"pst{half}"] + C.keys, [("hT", tt)])


def phase_inproj(k, m, C, d, l, hT):
    w_l = d["w_in"][l].rearrange("(kc p) c -> p kc c", p=128)
    wt = [m.alloc([16, 512], BF16) for _ in range(2)]
    stage = [m.alloc([2048]) for _ in range(3)]
    ci = 0
    bi = 0
    for g, segs in enumerate(FM_GROUPS):
        s = g % 2
        for (dst, src, n) in segs:
            k.dma("gpsimd", wt[s][:, :, dst:dst + n], w_l[:, :, src:src + n], (), [(f"wt{s}", dst)], war=[f"wtall{s}"])
        for c in range(4):
            segkeys = [f"wtall{s}"] + [(f"wt{s}", dst) for (dst, src, n) in segs if dst < (c + 1) * 128 and dst + n > c * 128]
            sg = stage[ci % 3]
            for tb in range(4):
                b = bi % 6
                bi += 1
                bank = m.bank(b)
                for kc in range(16):
                    k.mm(bank, wt[s][:, kc, c * 128:(c + 1) * 128], hT[:, kc, tb * 512:(tb + 1) * 512], kc == 0, kc == 15,
                         segkeys + [("hT", 4 * tb + i) for i in range(4)], [f"ps{b}"])
                k.copy("scalar" if (bi % 2) else "vector", sg[:, tb * 512:(tb + 1) * 512], bank, [f"ps{b}"], [(f"stage{ci % 3}", tb)])
            k.dma("sync", d["projF"][4 * g + c], sg, [(f"stage{ci % 3}", tb) for tb in range(4)], [("projF", 4 * g + c)])
            ci += 1
    for bidx, (col0, segs, ncols) in enumerate(TM_BLOCKS):
        s = bidx % 2
        for (dst, src, n) in segs:
            k.dma("gpsimd", wt[s][:, :, dst:dst + n], w_l[:, :, src:src + n], (), [(f"wt{s}", dst)], war=[f"wtall{s}"])
        segkeys = [f"wtall{s}"] + [(f"wt{s}", dst) for (dst, src, n) in segs]
        for tt in range(NTT):
            b = bi % 6
            bi += 1
            bank = m.bank(b)
            sg = stage[ci % 3]
            for kc in range(16):
                k.mm(bank[:, 0:ncols], hT[:, kc, tt * 128:(tt + 1) * 128], wt[s][:, kc, 0:ncols], kc == 0, kc == 15,
                     segkeys + [("hT", tt)], [f"ps{b}"])
            k.copy("scalar" if (bi % 2) else "vector", sg[:, 0:ncols], bank[:, 0:ncols], [f"ps{b}"], [(f"stage{ci % 3}", 0)])
            k.dma("sync", d["projT"][tt * 128:(tt + 1) * 128, col0:col0 + ncols], sg[:, 0:ncols], [(f"stage{ci % 3}", 0)], [("projT", bidx, tt)])
            ci += 1


def phase_wout(k, m, C, d, l, x_in, x_out):
    mixT = m.alloc([16, 2048], BF16)
    xg = [m.alloc([4, 2048]) for _ in range(1)]
    sq = m.alloc([4, 2048], BF16)
    tmp = [m.alloc([512]) for _ in range(2)]
    bi = 0
    for grp in range(4):
        x4 = xg[0]
        for c in range(4):
            k.dma("sync", x4[:, c, :], d["mixF"][4 * grp + c], (), [("xg", c)])
            k.act(sq[:, c, :], x4[:, c, :], AF.Square, [("xg", c)], [("sq", c)])
        for tb in range(4):
            b = bi % 6
            bi += 1
            bank = m.bank(b)
            for c in range(4):
                k.mm(bank, C.cb["ones"], sq[:, c, tb * 512:(tb + 1) * 512], c == 0, c == 3, [("sq", c)] + C.keys, [f"ps{b}"])
            t_ = tmp[tb % 2]
            k.act(t_, bank, AF.Sqrt, [f"ps{b}"] + C.keys, [f"tmp{tb % 2}"], scale=1.0 / GW, bias=C.eps6[:, 0:1])
            k.recip(t_, t_, [f"tmp{tb % 2}"], [f"tmp{tb % 2}"])
            for c in range(4):
                ch = 4 * grp + c
                k.stt(mixT[:, ch, tb * 512:(tb + 1) * 512], x4[:, c, tb * 512:(tb + 1) * 512], C.ggrp[:, l * 16 + ch:l * 16 + ch + 1], t_,
                      ALU.mult, ALU.mult, [("xg", c), f"tmp{tb % 2}"] + C.keys, [("mixT", ch, tb)])
    w_l = d["w_out"][l].rearrange("(kc p) n -> p kc n", p=128)
    wt = [m.alloc([16, 512], BF16) for _ in range(2)]
    xin = [m.alloc([512]) for _ in range(3)]
    xi = 0
    for nb in range(4):
        s = nb % 2
        k.dma("gpsimd", wt[s], w_l[:, :, nb * 512:(nb + 1) * 512], (), [f"wo{s}"])
        for tt in range(NTT):
            b = bi % 6
            bi += 1
            bank = m.bank(b)
            xs_ = xin[xi % 3]
            k.dma("sync", xs_, x_in[tt * 128:(tt + 1) * 128, nb * 512:(nb + 1) * 512], (), [f"xin{xi % 3}"])
            for kc in range(16):
                k.mm(bank, mixT[:, kc, tt * 128:(tt + 1) * 128], wt[s][:, kc, :], kc == 0, kc == 15,
                     [f"wo{s}", ("mixT", kc, tt // 4)], [f"ps{b}"])
            k.tt("vector", xs_, bank, xs_, ALU.add, [f"ps{b}", f"xin{xi % 3}"], [f"xin{xi % 3}"])
            k.dma("sync", x_out[tt * 128:(tt + 1) * 128, nb * 512:(nb + 1) * 512], xs_, [f"xin{xi % 3}"], [("xout", nb, tt)])
            xi += 1


def phase_ffn_up(k, m, C, d, l, hT):
    wg_l = d["w_ffn_gate"][l].rearrange("(kc p) f -> p kc f", p=128)
    wu_l = d["w_ffn_up"][l].rearrange("(kc p) f -> p kc f", p=128)
    wg = [m.alloc([16, 512], BF16) for _ in range(2)]
    wu = [m.alloc([16, 512], BF16) for _ in range(2)]
    act = [m.alloc([2048], BF16) for _ in range(3)]
    sil = [m.alloc([512]) for _ in range(2)]
    bi = 0
    ai = 0
    for fg in range(NFC // 4):
        s = fg % 2
        k.dma("gpsimd", wg[s], wg_l[:, :, fg * 512:(fg + 1) * 512], (), [f"wg{s}"])
        k.dma("gpsimd", wu[s], wu_l[:, :, fg * 512:(fg + 1) * 512], (), [f"wu{s}"])
        for c in range(4):
            fc = fg * 4 + c
            a_ = act[ai % 3]
            for tb in range(4):
                bg = bi % 6
                bu = (bi + 1) % 6
                bi += 2
                hk = [("hT", 4 * tb + i) for i in range(4)]
                for kc in range(16):
                    k.mm(m.bank(bg), wg[s][:, kc, c * 128:(c + 1) * 128], hT[:, kc, tb * 512:(tb + 1) * 512], kc == 0, kc == 15, [f"wg{s}"] + hk, [f"ps{bg}"])
                for kc in range(16):
                    k.mm(m.bank(bu), wu[s][:, kc, c * 128:(c + 1) * 128], hT[:, kc, tb * 512:(tb + 1) * 512], kc == 0, kc == 15, [f"wu{s}"] + hk, [f"ps{bu}"])
                s_ = sil[tb % 2]
                k.act(s_, m.bank(bg), AF.Silu, [f"ps{bg}"], [f"sil{tb % 2}"])
                k.tt("vector", a_[:, tb * 512:(tb + 1) * 512], m.bank(bu), s_, ALU.mult, [f"ps{bu}", f"sil{tb % 2}"], [(f"act{ai % 3}", tb)])
            k.dma("sync", d["actD"][fc], a_, [(f"act{ai % 3}", tb) for tb in range(4)], [("actD", fc)])
            ai += 1


def phase_ffn_down(k, m, C, d, l, x_in, x_out):
    wd_l = d["w_ffn_down"][l].rearrange("(fc p) n -> p fc n", p=128)
    actv = d["actD"].rearrange("fc p t -> p fc t")
    wd = [m.alloc([NFC, 512], BF16) for _ in range(2)]
    ab = [m.alloc([NFC, 512], BF16) for _ in range(2)]
    xin = [m.alloc([512]) for _ in range(3)]
    bi = 0
    xi = 0
    ai = 0
    for nbi, nb in enumerate([int(c) for c in os.environ.get("NB_ORDER", "0123")]):
        s = nbi % 2
        for h in range(2):
            k.dma("gpsimd", wd[s][:, h * 22:(h + 1) * 22, :], wd_l[:, h * 22:(h + 1) * 22, nb * 512:(nb + 1) * 512], (), [(f"wd{s}", h)])
        for tg in range(4):
            a_ = ab[ai % 2]
            for h in range(2):
                k.dma("sync", a_[:, h * 22:(h + 1) * 22, :], actv[:, h * 22:(h + 1) * 22, tg * 512:(tg + 1) * 512], (), [(f"ab{ai % 2}", h)])
            for t4 in range(4):
                tt = tg * 4 + t4
                b = bi % 6
                bi += 1
                bank = m.bank(b)
                xs_ = xin[xi % 3]
                k.dma("sync", xs_, x_in[tt * 128:(tt + 1) * 128, nb * 512:(nb + 1) * 512], (), [f"xin{xi % 3}"])
                for fc in range(NFC):
                    k.mm(bank, a_[:, fc, t4 * 128:(t4 + 1) * 128], wd[s][:, fc, :], fc == 0, fc == NFC - 1,
                         [(f"wd{s}", fc // 22), (f"ab{ai % 2}", fc // 22)], [f"ps{b}"])
                k.tt("vector", xs_, bank, xs_, ALU.add, [f"ps{b}", f"xin{xi % 3}"], [f"xin{xi % 3}"])
                k.dma("sync", x_out[tt * 128:(tt + 1) * 128, nb * 512:(nb + 1) * 512], xs_, [f"xin{xi % 3}"], [("xout", nb, tt)])
                xi += 1
            ai += 1


INPUT_SPECS = [
    ("x", [T, D_MODEL]), ("w_in", [DEPTH, D_MODEL, W_IN_COLS]), ("w_out", [DEPTH, D_MODEL, D_MODEL]),
    ("norm_mix", [DEPTH, D_MODEL]), ("norm_ffn", [DEPTH, D_MODEL]), ("q_gain", [DEPTH, HD]), ("k_gain", [DEPTH, HD]),
    ("cmp_pos", [DEPTH, 2, CMP_LEN, HD]), ("cmp_w1", [DEPTH, 2, CMP_LEN * HD, HD]), ("cmp_w2", [DEPTH, 2, HD, HD]),
    ("rel_table", [N_BUCKETS, NH]), ("conv_w", [DEPTH, 3, GW]), ("sgu_w", [DEPTH, NH, 128, 128]), ("sgu_b", [DEPTH, NH, 128]),
    ("group_gain", [DEPTH, D_MODEL]), ("w_ffn_gate", [DEPTH, D_MODEL, D_FF]), ("w_ffn_up", [DEPTH, D_MODEL, D_FF]),
    ("w_ffn_down", [DEPTH, D_FF, D_MODEL]),
]
N_CST = 7


def build(n_layers=DEPTH, dbg=(), phases=None, mix=("conv", "sgu", "sb", "nsa")):
    nc = bass.Bass("TRN2", target_bir_lowering=False)
    d = {}
    for name, shape in INPUT_SPECS:
        d[name] = nc.dram_tensor(name, shape, F32, kind="ExternalInput").ap()
    d["cst"] = nc.dram_tensor("cst", [128, N_CST * 128], F32, kind="ExternalInput").ap()

    def scratch(name, shape, dt=F32):
        kind = "ExternalOutput" if name in dbg else "Internal"
        d[name] = nc.dram_tensor(name, shape, dt, kind=kind).ap()

    d["oh"] = nc.dram_tensor("oh", [33, OH_L], F32, kind="ExternalInput").ap()
    d["scadd"] = nc.dram_tensor("scadd", [T, N_SLC], F32, kind="ExternalInput").ap()
    d["esel"] = nc.dram_tensor("esel", [N_SLC, T], F32, kind="ExternalInput").ap()
    d["ovc"] = nc.dram_tensor("ovc", [N_CMP, N_SLC], F32, kind="ExternalInput").ap()
    scratch("R", [8, 128, OH_L])
    scratch("projF", [N_FM, 128, T])
    scratch("projT", [T, N_TM])
    scratch("mixF", [16, 128, T])
    scratch("actD", [NFC, 128, T], BF16)
    scratch("xa", [T, D_MODEL])
    scratch("xb", [T, D_MODEL])
    d["out"] = nc.dram_tensor("out", [T, D_MODEL], F32, kind="ExternalOutput").ap()

    k = K(nc)
    m = Mem(nc)
    C = Ctx()
    with nc.allow_non_contiguous_dma(reason="small parameter loads"):
        setup_consts(k, m, d, C)
        m.set_floor()
        _barrier(k.P)
        if "nsa" in mix:
            setup_nsa_tables(k, m, C, d)
            _barrier(k.P)
        x_cur = d["x"]
        for l in range(n_layers):
            ph = phases if phases is not None else ("norm1", "inproj", "mixers", "wout", "ffn")
            x2 = d["out"] if l == n_layers - 1 else d["xb"]
            if "inproj" in ph:
                m.reset()
                hT = m.alloc([16, T], BF16)
                phase_norm(k, m, C, x_cur, C.gmix[:, l * 16:(l + 1) * 16], hT)
                phase_inproj(k, m, C, d, l, hT)
                _barrier(k.P)
            if "mixers" in ph:
                phase_mixers(k, m, C, d, l, which=mix)
            if "wout" in ph:
                m.reset()
                phase_wout(k, m, C, d, l, x_cur, d["xa"])
                _barrier(k.P)
            if "ffn" in ph:
                m.reset()
                hT = m.alloc([16, T], BF16)
                phase_norm(k, m, C, d["xa"], C.gffn[:, l * 16:(l + 1) * 16], hT)
                phase_ffn_up(k, m, C, d, l, hT)
                _barrier(k.P)
                m.reset()
                phase_ffn_down(k, m, C, d, l, d["xa"], x2)
                _barrier(k.P)
            x_cur = x2
        k.P.finish([])
    return nc


def conv_setup(k, m, C, d, l):
    cv = Ctx()
    cv.cw = m.alloc([12])
    k.dma("sync", cv.cw.rearrange("p (w j) -> p w j", j=4), d["conv_w"][l].rearrange("w (j p) -> p w j", p=128), (), ["cw"])
    cv.bg = [m.alloc([2048]) for _ in range(2)]
    cv.cg = [m.alloc([2048]) for _ in range(2)]
    cv.hh = [m.alloc([2048]) for _ in range(2)]
    cv.z = [m.alloc([2050]) for _ in range(2)]
    cv.y = [m.alloc([2048]) for _ in range(2)]
    for s in range(2):
        k.memset("gpsimd", cv.z[s][:, 0:2], 0.0, [f"z{s}"])
    return cv


def conv_piece(k, m, C, d, l, cv, j):
    s = j % 2
    cw, bg, cg, hh, z, y = cv.cw, cv.bg, cv.cg, cv.hh, cv.z, cv.y
    k.dma("sync", bg[s], d["projF"][8 + j], (), [f"bg{s}"])
    k.dma("sync", cg[s], d["projF"][12 + j], (), [f"cg{s}"])
    k.dma("sync", hh[s], d["projF"][16 + j], (), [f"hh{s}"])
    k.tt("gpsimd", z[s][:, 2:2050], cg[s], hh[s], ALU.mult, [f"cg{s}", f"hh{s}"], [f"z{s}"])
    k.ts("vector", y[s], z[s][:, 2:2050], cw[:, 8 + j:9 + j], None, ALU.mult, None, [f"z{s}", "cw"], [f"y{s}"])
    k.stt(y[s], z[s][:, 1:2049], cw[:, 4 + j:5 + j], y[s], ALU.mult, ALU.add, [f"z{s}", "cw", f"y{s}"], [f"y{s}"])
    k.stt(y[s], z[s][:, 0:2048], cw[:, j:j + 1], y[s], ALU.mult, ALU.add, [f"z{s}", "cw", f"y{s}"], [f"y{s}"])
    k.tt("vector", y[s], y[s], bg[s], ALU.mult, [f"y{s}", f"bg{s}"], [f"y{s}"])
    k.dma("sync", d["mixF"][4 + j], y[s], [f"y{s}"], [("mixF", 4 + j)])


def phase_conv(k, m, C, d, l):
    cv = conv_setup(k, m, C, d, l)
    for j in range(4):
        conv_piece(k, m, C, d, l, cv, j)


def gelu_tanh(k, m, out, x, tmp, tmp2, rk, wk, tk):
    c = 1.5957691216057308
    k.tt("vector", tmp, x, x, ALU.mult, rk, tk)
    k.ts("vector", tmp, tmp, 0.044715 * c, c, ALU.mult, ALU.add, tk, tk)
    k.tt("vector", tmp, tmp, x, ALU.mult, rk + tk, tk)
    k.act(tmp2, tmp, AF.Sigmoid, tk, [tk[0] + "_2"])
    k.tt("vector", out, x, tmp2, ALU.mult, rk + [tk[0] + "_2"], wk)


def phase_sgu(k, m, C, d, l):
    wraw = m.alloc([8, 128])
    k.dma("sync", wraw, d["sgu_w"][l].rearrange("h p q -> p h q"), (), ["wraw"])
    wT = m.alloc([8, 128], BF16)
    bsb = m.alloc([8])
    k.dma("sync", bsb, d["sgu_b"][l].rearrange("h p -> p h"), (), ["bsb"])
    for h in range(8):
        b = h % 2
        k.transpose(m.bank(b)[:, 0:128], wraw[:, h, :], C.cf["ident"], ["wraw"] + C.keys, [f"ps{b}"])
        k.tt("vector", wT[:, h, :], m.bank(b)[:, 0:128], C.cf["tril_qp"], ALU.mult, [f"ps{b}"] + C.keys, [("wT", h)])
    uv = [m.alloc([1024]) for _ in range(2)]
    t1 = m.alloc([1024])
    t2 = m.alloc([1024])
    gl = [m.alloc([1024]) for _ in range(2)]
    vln = [m.alloc([512], BF16) for _ in range(2)]
    st = m.alloc([16])
    oc = [m.alloc([512]) for _ in range(2)]
    stage = [m.alloc([4, 512]) for _ in range(2)]
    for tt in range(NTT):
        s = tt % 2
        for c in range(4):
            k.dma("sync", stage[s][:, c, 0:128], d["projF"][20 + c][:, tt * 128:(tt + 1) * 128], (), [(f"ufm{s}", c)])
        for c in range(4):
            b = 2 + c % 2
            k.transpose(m.bank(b)[:, 0:128], stage[s][:, c, 0:128], C.cf["ident"], [(f"ufm{s}", c)] + C.keys, [f"ps{b}"])
            k.copy("scalar", uv[s][:, c * 128:(c + 1) * 128], m.bank(b)[:, 0:128], [f"ps{b}"], [(f"uv{s}", c)])
        k.dma("sync", uv[s][:, 512:1024], d["projT"][tt * 128:(tt + 1) * 128, 280:792], (), [(f"uv{s}", 4)])
        gelu_tanh(k, m, gl[s], uv[s], t1, t2, [(f"uv{s}", c) for c in range(5)], [f"gl{s}"], ["t1"])
        k.P.op("vector", (lambda o, i: (lambda e: e.bn_stats(out=o, in_=i)))(st[:, 0:6], gl[s][:, 512:1024]), [f"gl{s}"], ["bst"])
        k.P.op("vector", (lambda o, i: (lambda e: e.bn_aggr(out=o, in_=i)))(st[:, 8:10], st[:, 0:6]), ["bst"], ["bag"])
        k.act(st[:, 10:11], st[:, 9:10], AF.Sqrt, ["bag"] + C.keys, ["lnsd"], scale=1.0, bias=C.eps5[:, 0:1])
        k.recip(st[:, 11:12], st[:, 10:11], ["lnsd"], ["lnrs"])
        k.ts("vector", vln[s], gl[s][:, 512:1024], st[:, 8:9], st[:, 11:12], ALU.subtract, ALU.mult, [f"gl{s}", "bag", "lnrs"], [f"vln{s}"])
        b = 4 + tt % 2
        for h in range(8):
            k.mm(m.bank(b)[:, h * 64:(h + 1) * 64], wT[:, h, :], vln[s][:, h * 64:(h + 1) * 64], True, True,
                 [("wT", h), f"vln{s}"], [f"ps{b}"], signal=(h == 7))
        k.tt("vector", oc[s].rearrange("p (h e) -> p h e", e=64), m.bank(b).rearrange("p (h e) -> p h e", e=64),
             bsb.unsqueeze(2).broadcast_to([128, 8, 64]), ALU.add, [f"ps{b}", "bsb"], [f"oc{s}"])
        k.tt("vector", oc[s], oc[s], gl[s][:, 0:512], ALU.mult, [f"oc{s}", f"gl{s}"], [f"oc{s}"])
        for c in range(4):
            b2 = 2 + c % 2
            k.transpose(m.bank(b2)[:, 128:256], oc[s][:, c * 128:(c + 1) * 128], C.cf["ident"], [f"oc{s}"] + C.keys, [f"ps{b2}"])
            k.copy("scalar", stage[s][:, c, 128:256], m.bank(b2)[:, 128:256], [f"ps{b2}"], [(f"ofm{s}", c)])
            k.dma("sync", d["mixF"][8 + c][:, tt * 128:(tt + 1) * 128], stage[s][:, c, 128:256], [(f"ofm{s}", c)], [("mixF", 8 + c, tt)])


def phase_sb(k, m, C, d, l, cv=None):
    scale = HD ** -0.5
    qf = m.alloc([2048])
    kf = m.alloc([2048])
    qs = [m.alloc([2048], BF16) for _ in range(2)]
    qn = [m.alloc([2048], BF16) for _ in range(2)]
    kb = [m.alloc([2048], BF16) for _ in range(2)]
    vb = m.alloc([16, 512], BF16)
    k.dma("gpsimd", vb, d["projT"][:, 792:1304].rearrange("(st p) c -> p st c", p=128), (), ["vb"])
    ef = [[m.alloc([512]) for _ in range(3)] for _ in range(2)]
    sp = [[m.alloc([512], BF16) for _ in range(3)] for _ in range(2)]
    aT = [[m.alloc([512], BF16) for _ in range(2)] for _ in range(2)]
    osb = [[m.alloc([512]) for _ in range(2)] for _ in range(2)]
    for j in range(4):
        jj = j % 2
        k.dma("sync", qf, d["projF"][24 + j], (), ["qf"])
        k.dma("sync", kf, d["projF"][28 + j], (), ["kf"])
        k.ts("vector", qs[jj], qf, scale, None, ALU.mult, None, ["qf"], [f"qs{jj}"])
        k.ts("gpsimd", qn[jj], qf, -scale, None, ALU.mult, None, ["qf"], [f"qn{jj}"])
        k.copy("gpsimd", kb[jj], kf, ["kf"], [f"kb{jj}"])
        if cv is not None:
            conv_piece(k, m, C, d, l, cv, j)
        QS, QN, KB = qs[jj], qn[jj], kb[jj]
        qsk, qnk, kbk = f"qs{jj}", f"qn{jj}", f"kb{jj}"
        for tb in range(4):
            t0 = tb * 512
            steps = list(range(4 * tb + 3, -1, -1))
            for ch in range(2):
                k.mm(m.bank(3 * ch + 1), C.zeros[:, 0:128], C.zeros, True, False, [("zeros",)], [f"ps{3 * ch + 1}"], signal=True)
                k.mm(m.bank(3 * ch + 2)[0:64, :], C.zeros[:, 0:64], C.zeros, True, False, [("zeros",)], [f"ps{3 * ch + 2}"], signal=True)

            def geom(si):
                s0 = si * 128
                c0 = max(0, s0 - t0)
                return s0, c0, s0 >= t0, slice(c0, 512), slice(t0 + c0, t0 + 512)

            def sp_qk(ch, n):
                si = steps[n]
                s0, c0, diag, cols, tcols = geom(si)
                pr = slice(64 * ch, 64 * ch + 64)
                bz = m.bank(3 * ch)
                zk = f"ps{3 * ch}"
                k.mm(bz[:, cols], KB[pr, s0:s0 + 128], QS[pr, tcols], True, True, [kbk, qsk], [zk])

            def sp_act(ch, n):
                si = steps[n]
                s0, c0, diag, cols, tcols = geom(si)
                sl = n % 3
                bz = m.bank(3 * ch)
                zk = f"ps{3 * ch}"
                k.act(ef[ch][sl][:, cols], bz[:, cols], AF.Exp, [zk], [f"ef{ch}{sl}"])
                k.act(sp[ch][sl][:, cols], ef[ch][sl][:, cols], AF.Ln, [f"ef{ch}{sl}"] + C.keys, [f"sp{ch}{sl}"], bias=C.one1[:, 0:1], scale=1.0)
                if diag:
                    k.tt("gpsimd", sp[ch][sl][:, c0:c0 + 128], sp[ch][sl][:, c0:c0 + 128], C.cb["lt_st"], ALU.mult, [f"sp{ch}{sl}"] + C.keys, [f"sp{ch}{sl}"])

            def chain_a(ch, n):
                si = steps[n]
                s0, c0, diag, cols, tcols = geom(si)
                pr = slice(64 * ch, 64 * ch + 64)
                sl = n % 3
                bc = m.bank(3 * ch + 1)
                ck = f"ps{3 * ch + 1}"
                k.mm(bc[:, cols], C.cb["uincl"], sp[ch][sl][:, cols], False, False, [f"sp{ch}{sl}"] + C.keys, [ck], signal=False)
                k.mm(bc[:, cols], KB[pr, s0:s0 + 128], QN[pr, tcols], False, False, [kbk, qnk], [ck], signal=True)

            def chain_b(ch, n):
                si = steps[n]
                s0, c0, diag, cols, tcols = geom(si)
                sl = n % 2
                bc = m.bank(3 * ch + 1)
                ck = f"ps{3 * ch + 1}"
                k.act(aT[ch][sl][:, cols], bc[:, cols], AF.Exp, [ck], [f"aT{ch}{sl}"], scale=-1.0)
                if diag:
                    k.tt("gpsimd", aT[ch][sl][:, c0:c0 + 128], aT[ch][sl][:, c0:c0 + 128], C.cb["lt_st"], ALU.mult, [f"aT{ch}{sl}"] + C.keys, [f"aT{ch}{sl}"])

            def chain_c(ch, n):
                si = steps[n]
                s0, c0, diag, cols, tcols = geom(si)
                pr = slice(64 * ch, 64 * ch + 64)
                sl = n % 2
                h = 2 * j + ch
                bc = m.bank(3 * ch + 1)
                ck = f"ps{3 * ch + 1}"
                bo = m.bank(3 * ch + 2)
                ok_ = f"ps{3 * ch + 2}"
                k.mm(bc[:, cols], KB[pr, s0:s0 + 128], QS[pr, tcols], False, False, [kbk, qsk], [ck], signal=False)
                s3 = n % 3
                k.mm(bc[:, cols], C.cb["lstrict"], sp[ch][s3][:, cols], False, si == 0, [f"sp{ch}{s3}"] + C.keys, [ck], signal=True)
                k.mm(bo[0:64, cols], vb[:, si, h * 64:(h + 1) * 64], aT[ch][sl][:, cols], False, si == 0, ["vb", f"aT{ch}{sl}"], [ok_], signal=True)

            ns = len(steps)
            for n0 in range(min(2, ns)):
                for ch in range(2):
                    sp_qk(ch, n0)
                    sp_act(ch, n0)
            for n in range(ns):
                for ch in range(2):
                    chain_a(ch, n)
                if n + 2 < ns:
                    for ch in range(2):
                        sp_qk(ch, n + 2)
                for ch in range(2):
                    chain_b(ch, n)
                if n + 2 < ns:
                    for ch in range(2):
                        sp_act(ch, n + 2)
                for ch in range(2):
                    chain_c(ch, n)
            for ch in range(2):
                o_ = osb[ch][tb % 2]
                ok_ = f"ps{3 * ch + 2}"
                k.copy("vector", o_[0:64, :], m.bank(3 * ch + 2)[0:64, :], [ok_], [f"osb{ch}{tb % 2}"])
                k.dma("sync", d["mixF"][12 + j][64 * ch:64 * ch + 64, t0:t0 + 512], o_[0:64, :], [f"osb{ch}{tb % 2}"], [("mixF", 12 + j, ch, tb)])


def phase_mixers(k, m, C, d, l, which=("conv", "sgu", "sb", "nsa")):
    fold = ("conv" in which) and ("sb" in which)
    if "conv" in which and not fold:
        m.reset()
        phase_conv(k, m, C, d, l)
        _barrier(k.P)
    if "sgu" in which:
        m.reset()
        phase_sgu(k, m, C, d, l)
        _barrier(k.P)
    if "sb" in which:
        m.reset()
        cv = conv_setup(k, m, C, d, l) if fold else None
        phase_sb(k, m, C, d, l, cv)
        _barrier(k.P)
    if "nsa" in which:
        m.reset()
        phase_nsa(k, m, C, d, l)
        _barrier(k.P)


OH_L = 6366
OH_SO, OH_WO, OH_CO = 0, 1535, 2302


def host_nsa_consts():
    oh = np.zeros((33, OH_L), np.float32)
    x = np.arange(1535) - 127
    b = np.where(x < 0, 32, _rel_bucket_np(x))
    oh[b, OH_SO + np.arange(1535)] = 1
    x = np.arange(767) - 127
    b = np.where((x < 0) | (x >= WINDOW), 32, _rel_bucket_np(x))
    oh[b, OH_WO + np.arange(767)] = 1
    x = np.arange(4064) - 2016 - 31
    b = np.where(x < 0, 32, _rel_bucket_np(x))
    oh[b, OH_CO + np.arange(4064)] = 1
    t = np.arange(T)[:, None]
    jj = np.arange(N_SLC)[None, :]
    cur = t // SLC_LEN
    valid = jj * SLC_LEN <= t
    forced = (jj == 0) | (jj == cur) | (jj == cur - 1)
    scadd = np.where(valid, np.where(forced, 1000.0, 0.0), -1e30).astype(np.float32)
    esel = (np.arange(T)[None, :] // SLC_LEN == np.arange(N_SLC)[:, None]).astype(np.float32)
    c0 = np.arange(N_CMP)[:, None] * CMP_STRIDE
    s0 = np.arange(N_SLC)[None, :] * SLC_LEN
    ov = np.minimum(c0 + CMP_LEN, s0 + SLC_LEN) - np.maximum(c0, s0)
    ovc = (np.maximum(ov, 0) / CMP_LEN).astype(np.float32)
    return {"oh": oh, "scadd": scadd, "esel": esel, "ovc": ovc}


def setup_nsa_tables(k, m, C, d):
    tabx = m.alloc([8])
    k.memset("vector", tabx[0:64, :], NEG, ["tabx"])
    k.dma("sync", tabx[0:32, :], d["rel_table"], (), ["tabx"])
    oh = m.alloc([OH_L])
    k.dma("sync", oh[0:33, :], d["oh"], (), ["oh"])
    row = [m.alloc([OH_L]) for _ in range(2)]
    bi = 0
    for h in range(8):
        r_ = row[h % 2]
        for c0 in range(0, OH_L, 512):
            n = min(512, OH_L - c0)
            b = bi % 6
            bi += 1
            k.mm(m.bank(b)[:, 0:n], tabx[0:33, h:h + 1].broadcast_to([33, 128]), oh[0:33, c0:c0 + n], True, True, ["tabx", "oh"], [f"ps{b}"])
            k.copy("scalar" if bi % 2 else "vector", r_[:, c0:c0 + n], m.bank(b)[:, 0:n], [f"ps{b}"], [(f"row{h % 2}", c0)])
        k.dma("sync", d["R"][h], r_, [(f"row{h % 2}", c0) for c0 in range(0, OH_L, 512)], [("R", h)], war=[f"rowall{h % 2}"])


def phase_nsa(k, m, C, d, l):
    scale = HD ** -0.5
    STOP = float(os.environ.get("NSA_STOP", "99"))
    if STOP <= 0:
        return
    Rt = d["R"].tensor
    tabW = m.alloc([8, 2048], BF16)
    Wc = tabW
    Ws = tabW[:, :, 0:1408]
    Ww = tabW[:, :, 1408:2048]
    for h in range(8):
        base = h * 128 * OH_L
        k.dma("gpsimd", Wc[0:127, h, :], bass.AP(tensor=Rt, offset=base + OH_CO + 2016, ap=[[OH_L - 16, 127], [1, 2048]]), (), [("Wc", h)])
    esel = m.alloc([2048], BF16)
    k.dma("gpsimd", esel[0:32, :], d["esel"], (), ["esel"])
    scadd = m.alloc([16, 32])
    k.dma("sync", scadd, d["scadd"].rearrange("(tt p) j -> p tt j", p=128), (), ["scadd"])
    qg = m.alloc([2])
    kg = m.alloc([1])
    for half in range(2):
        k.dma("sync", qg[64 * half:64 * half + 64, 0:1], d["q_gain"][l].rearrange("(d o) -> d o", o=1), (), [("qg", half)])
        k.dma("sync", kg[64 * half:64 * half + 64, 0:1], d["k_gain"][l].rearrange("(d o) -> d o", o=1), (), [("kg", half)])
    k.ts("vector", qg[:, 1:2], qg[:, 0:1], scale, None, ALU.mult, None, [("qg", 0), ("qg", 1)], ["qgs"])
    qT = m.alloc([4, 2048], BF16)
    ksT = m.alloc([2048], BF16)
    kwT = m.alloc([2048], BF16)
    vx = m.alloc([16, 4 * 65], BF16)
    gt = m.alloc([16, 24])
    cacc = m.alloc([16, 8 * 97])
    rz = m.alloc([16, 8])
    negT = m.alloc([2, 2048], BF16)
    mark = m.top
    xf = m.alloc([2048])
    sq = m.alloc([2048], BF16)
    rs = [m.alloc([512]) for _ in range(2)]
    bi = [0]

    def nb():
        b = bi[0] % 6
        bi[0] += 1
        return b

    def headnorm(chunk, gain, gkeys, dst, dkey):
        k.dma("sync", xf, d["projF"][chunk], (), ["xf"])
        k.act(sq, xf, AF.Square, ["xf"], ["sq"])
        for tb in range(4):
            b = nb()
            cs = slice(tb * 512, (tb + 1) * 512)
            k.mm(m.bank(b), C.cb["blk64"], sq[:, cs], True, True, ["sq"] + C.keys, [f"ps{b}"])
            r_ = rs[tb % 2]
            k.act(r_, m.bank(b), AF.Sqrt, [f"ps{b}"] + C.keys, [f"rs{tb % 2}"], scale=1.0 / HD, bias=C.eps6[:, 0:1])
            k.recip(r_, r_, [f"rs{tb % 2}"], [f"rs{tb % 2}"])
            k.stt(dst[:, cs], xf[:, cs], gain, r_, ALU.mult, ALU.mult, ["xf", f"rs{tb % 2}"] + gkeys, [dkey])

    for j in range(4):
        headnorm(j, qg[:, 1:2], ["qgs"], qT[:, j, :], ("qT", j))
    headnorm(6, kg[:, 0:1], [("kg", 0), ("kg", 1)], ksT, "ksT")
    headnorm(7, kg[:, 0:1], [("kg", 0), ("kg", 1)], kwT, "kwT")
    kcb = m.alloc([2048], BF16)
    vcb = m.alloc([2048], BF16)
    k.dma("gpsimd", kcb, d["projF"][4], (), ["kcb"])
    k.dma("gpsimd", vcb, d["projF"][5], (), ["vcb"])
    k.memset("vector", vx, 1.0, ["vx"])
    vx5 = vx.rearrange("p st (a e) -> p st a e", e=65)
    for a in range(4):
        k.dma("gpsimd", vx5[:, :, a, 0:64], d["projT"][:, a * 64:(a + 1) * 64].rearrange("(st p) c -> p st c", p=128), ["vx"], [("vx", a)])
    vxk = ["vx"] + [("vx", a) for a in range(4)]
    k.dma("sync", gt, d["projT"][:, 256:280].rearrange("(tt p) c -> p tt c", p=128), (), ["gt"])
    k.act(gt, gt, AF.Sigmoid, ["gt"], ["gt"])
    if STOP <= 1:
        return
    W1 = m.alloc([2, 32, 128], BF16)
    W2 = m.alloc([2, 64], BF16)
    posT = m.alloc([2, 32], BF16)
    posF = m.alloc([2, 32])
    for half in range(2):
        pr = slice(64 * half, 64 * half + 64)
        for i in range(2):
            for dup in range(2):
                k.dma("gpsimd", W1[pr, i, :, dup * 64:(dup + 1) * 64], d["cmp_w1"][l, i].rearrange("(l d) e -> d l e", d=64), (), [("W1", half, i, dup)])
        k.dma("sync", posF[pr], d["cmp_pos"][l].rearrange("i l d -> d i l"), (), [("posF", half)])
        k.copy("vector", posT[pr], posF[pr], [("posF", half)], [("posT", half)])
    k.dma("gpsimd", W2[0:64], d["cmp_w2"][l].rearrange("i e f -> e i f"), (), ["W2"])
    wkeys = [("W1", a, b_, c_) for a in range(2) for b_ in range(2) for c_ in range(2)] + [("posT", 0), ("posT", 1), "W2"]
    if STOP <= 1.2:
        return
    hf = m.alloc([256])
    zb = m.alloc([32, 127], BF16)
    hid = m.alloc([256], BF16)
    t1 = m.alloc([256])
    t2 = m.alloc([256])
    kcT = m.alloc([128], BF16)
    k.memset("vector", kcT, 0.0, [("kcT", 0), ("kcT", 1)])
    kcn = m.alloc([256], BF16)
    rc = m.alloc([2, 97], BF16)
    k.memset("vector", rc, 1.0, ["rc"])
    ovf = m.alloc([32])
    k.dma("sync", ovf[0:127, :], d["ovc"], (), ["ovf"])
    for g in range(2):
        k.copy("vector", rc[0:127, g, 65:97], ovf[0:127, :], ["rc", "ovf"], [("rc", "ov", g)])
    if STOP <= 1.25:
        return
    for i, src, skey in ((0, kcb, "kcb"), (1, vcb, "vcb")):
        b = nb()
        sview = bass.AP(tensor=src.tensor, offset=src.offset, ap=[[src.ap[0][0], 128], [1, 32], [16, 127]])
        k.tt("vector", zb, sview, posT[:, i, :].unsqueeze(2).broadcast_to([128, 32, 127]), ALU.add, [skey, ("posT", 0), ("posT", 1)], ["zb"])
        if STOP <= 1.3:
            return
        for g in range(2):
            bg_ = nb()
            pr = slice(64 * g, 64 * g + 64)
            o_ = m.bank(bg_)[:, 0:127]
            for li in range(32):
                k.mm(o_, W1[pr, i, li, :], zb[pr, li, :], li == 0, li == 31, wkeys + ["zb"], [f"ps{bg_}"], signal=(li == 31))
            k.copy("vector", hf[0:64, g * 128:(g + 1) * 128], m.bank(bg_)[0:64, 0:128], [f"ps{bg_}"], [("hf", g)])
        if STOP <= 1.35:
            return
        k.tt("vector", hf[0:64, 0:1], hf[0:64, 0:1], hf[0:64, 0:1], ALU.max, [("hf", 0), ("hf", 1)], ["hf"])
        if STOP <= 1.4:
            return
        gelu_tanh(k, m, hid[0:64, :], hf[0:64, :], t1[0:64, :], t2[0:64, :], ["hf"], ["hid"], ["ct1"])
        if STOP <= 1.5:
            return
        if i == 0:
            b2 = nb()
            for g in range(2):
                k.mm(m.bank(b2)[0:64, g * 128:g * 128 + 127], W2[0:64, 0, :], hid[0:64, g * 128:g * 128 + 127], True, True, ["W2", "hid"], [f"ps{b2}"])
            k.copy("vector", hf[0:64, :], m.bank(b2)[0:64, 0:256], [f"ps{b2}"], ["hf"])
            k.act(sq[0:64, 0:256], hf[0:64, :], AF.Square, ["hf"], ["sq"])
            b3 = nb()
            k.mm(m.bank(b3)[0:64, 0:256], C.cb["ones"][0:64, 0:64], sq[0:64, 0:256], True, True, ["sq"] + C.keys, [f"ps{b3}"])
            k.act(t1[0:64, :], m.bank(b3)[0:64, 0:256], AF.Sqrt, [f"ps{b3}"] + C.keys, ["ct1"], scale=1.0 / HD, bias=C.eps6[0:64, 0:1])
            k.recip(t1[0:64, :], t1[0:64, :], ["ct1"], ["ct1"])
            k.stt(kcn[0:64, :], hf[0:64, :], kg[0:64, 0:1], t1[0:64, :], ALU.mult, ALU.mult, ["hf", "ct1", ("kg", 0)], ["kcn"])
            k.copy("vector", kcT[0:64, 0:127], kcn[0:64, 0:127], ["kcn"], [("kcT", 0)])
            k.copy("vector", kcT[64:128, 0:127], kcn[0:64, 128:255], ["kcn"], [("kcT", 1)])
        else:
            b2 = nb()
            for g in range(2):
                k.mm(m.bank(b2)[0:127, g * 64:(g + 1) * 64], hid[0:64, g * 128:g * 128 + 127], W2[0:64, 1, :], True, True, ["W2", "hid"], [f"ps{b2}"])
            k.copy("vector", rc[0:127, :, 0:64], m.bank(b2)[0:127, 0:128].rearrange("p (g e) -> p g e", e=64), [f"ps{b2}", "rc"], [("rc", "v")])
    rck = ["rc", ("rc", "ov", 0), ("rc", "ov", 1), ("rc", "v")]
    if STOP <= 2:
        return
    cacc4 = cacc.rearrange("p tt (h e) -> p tt h e", e=97)
    ecT = [m.alloc([512], BF16) for _ in range(2)]
    ei = 0
    for j in range(4):
        for g in range(2):
            h = 4 * g + j
            pr = slice(64 * g, 64 * g + 64)
            for tb in range(4):
                cs = slice(tb * 512, (tb + 1) * 512)
                b = nb()
                k.mm(m.bank(b)[:, :], kcT[pr, 0:128], qT[pr, j, cs], True, False, [("kcT", g), ("qT", j)], [f"ps{b}"], signal=False)
                k.mm(m.bank(b)[0:127, :], C.cb["ident"][0:127, 0:127], Wc[0:127, h, cs], False, True, [("Wc", h)] + C.keys, [f"ps{b}"])
                e_ = ecT[ei % 2]
                k.act(e_[0:127, :], m.bank(b)[0:127, :], AF.Exp, [f"ps{b}"], [f"ecT{ei % 2}"])
                for t4 in range(4):
                    tt = 4 * tb + t4
                    b2 = nb()
                    k.mm(m.bank(b2)[:, 0:97], e_[0:127, t4 * 128:(t4 + 1) * 128], rc[0:127, g, :], True, True, [f"ecT{ei % 2}"] + rck, [f"ps{b2}"])
                    k.copy("scalar" if t4 % 2 else "vector", cacc4[:, tt, h, :], m.bank(b2)[:, 0:97], [f"ps{b2}"], [("cacc", tt, h)])
                ei += 1
    if STOP <= 3:
        return
    k.ts("vector", rz, cacc4[:, :, :, 64], 1e-30, None, ALU.max, None, [("cacc", tt, h) for tt in range(16) for h in range(8)], ["rz"])
    k.recip(rz, rz, ["rz"], ["rz"])
    sc = m.alloc([32])
    sc2 = m.alloc([32])
    m8 = m.alloc([16])
    ngm = m.alloc([32])
    for tt in range(NTT):
        for g in range(2):
            for r in range(4):
                h = 4 * g + r
                if r == 0:
                    k.stt(sc, cacc4[:, tt, h, 65:97], rz[:, tt, h:h + 1], scadd[:, tt, :], ALU.mult, ALU.add, ["rz", "scadd"], ["sc"])
                else:
                    k.stt(sc, cacc4[:, tt, h, 65:97], rz[:, tt, h:h + 1], sc, ALU.mult, ALU.add, ["rz", "sc"], ["sc"])
            k.P.op("vector", (lambda o, i_: (lambda e: e.max(out=o, in_=i_)))(m8[:, 0:8], sc), ["sc"], ["m8a"])
            k.P.op("vector", (lambda o, r_, v_: (lambda e: e.match_replace(out=o, in_to_replace=r_, in_values=v_, imm_value=-3.0e38)))(sc2, m8[:, 0:8], sc), ["sc", "m8a"], ["sc2"])
            k.P.op("vector", (lambda o, i_: (lambda e: e.max(out=o, in_=i_)))(m8[:, 8:16], sc2), ["sc2"], ["m8b"])
            k.ts("vector", ngm, sc, m8[:, 15:16], NEG, ALU.is_lt, ALU.mult, ["sc", "m8b"], ["ngm"])
            b = nb()
            k.transpose(m.bank(b)[0:32, 0:128], ngm, C.cf["ident"], ["ngm"] + C.keys, [f"ps{b}"])
            k.copy("scalar", negT[0:32, g, tt * 128:(tt + 1) * 128], m.bank(b)[0:32, 0:128], [f"ps{b}"], [("negT", g, tt)])
    if STOP <= 4:
        return
    _barrier(k.P)
    m.top = mark
    for h in range(8):
        base = h * 128 * OH_L
        k.dma("gpsimd", Ws[:, h, :], bass.AP(tensor=Rt, offset=base + OH_SO + 127, ap=[[OH_L - 1, 128], [1, 1408]]), (), [("Ws", h)])
        k.dma("gpsimd", Ww[:, h, :], bass.AP(tensor=Rt, offset=base + OH_WO + 127, ap=[[OH_L - 1, 128], [1, 640]]), (), [("Ww", h)])
    oa = m.alloc([16, 512])
    PT = [m.alloc([512], BF16) for _ in range(3)]
    sacc = [m.alloc([4, 65]) for _ in range(2)]
    cf_ = m.alloc([16])
    tmp = m.alloc([4, 64])
    stage = m.alloc([4, 128])
    items = []
    gi = 0
    for tb in range(4):
        for h in range(8):
            for br in range(2):
                si_lo = 0 if br == 0 else max(0, 4 * tb - 4)
                si_list = list(range(si_lo, 4 * tb + 4))
                for si in si_list:
                    items.append(dict(tb=tb, h=h, br=br, si=si, first=(si == si_list[0]), last=(si == si_list[-1]), gi=gi,
                                      last_of_tb=(h == 7 and br == 1 and si == si_list[-1])))
                gi += 1

    def geom(it):
        tb, si, br = it["tb"], it["si"], it["br"]
        t0 = tb * 512
        s0 = si * 128
        c0 = max(0, s0 - t0)
        c1 = 512 if br == 0 else min(512, s0 + 640 - t0)
        return t0, s0, c0, c1

    def score_stage(n):
        it = items[n]
        tb, h, br, si = it["tb"], it["h"], it["br"], it["si"]
        g = h // 4
        j = h % 4
        pr = slice(64 * g, 64 * g + 64)
        t0, s0, c0, c1 = geom(it)
        cols = slice(c0, c1)
        tcols = slice(t0 + c0, t0 + c1)
        b = n % 4
        p_ = PT[n % 3]
        pk = f"PT{n % 3}"
        kT_ = ksT if br == 0 else kwT
        W_ = Ws if br == 0 else Ww
        m0 = t0 + c0 - s0
        k.mm(m.bank(b)[:, cols], kT_[pr, s0:s0 + 128], qT[pr, j, tcols], True, False, [], [f"ps{b}"], signal=False)
        if br == 0 and m0 + (c1 - c0) > 1408:
            for ca in range(c0, c1, 256):
                cb_ = min(c1, ca + 256)
                k.mm(m.bank(b)[:, ca:cb_], C.cb["ident"], W_[:, h, 1152:1152 + (cb_ - ca)], False, False, [("Ws", h)], [f"ps{b}"], signal=False)
        else:
            k.mm(m.bank(b)[:, cols], C.cb["ident"], W_[:, h, m0:m0 + (c1 - c0)], False, br == 1, [("Ws", h), ("Ww", h)], [f"ps{b}"], signal=(br == 1))
        if br == 0:
            k.mm(m.bank(b)[:, cols], esel[0:32, s0:s0 + 128], negT[0:32, g, tcols], False, True, ["esel"], [f"ps{b}"])
        k.act(p_[:, cols], m.bank(b)[:, cols], AF.Exp, [f"ps{b}"], [pk])

    def pv_stage(n):
        it = items[n]
        tb, h, br, si = it["tb"], it["h"], it["br"], it["si"]
        g = h // 4
        t0, s0, c0, c1 = geom(it)
        bo = 4 + (it["gi"] % 2)
        bok = f"ps{bo}"
        bank_o = m.bank(bo)[:, 0:260].rearrange("p (a e) -> p a e", e=65)
        p_ = PT[n % 3]
        pk = f"PT{n % 3}"
        if it["first"]:
            k.mm(m.bank(bo)[:, 0:260], C.zeros[:, 0:128], C.zeros[:, 0:260], True, False, [("zeros",)], [bok], signal=True)
        for t4 in range(4):
            if t4 * 128 < c0 or t4 * 128 >= c1:
                continue
            a = br * 2 + g
            k.mm(bank_o[:, t4, :], p_[:, t4 * 128:(t4 + 1) * 128], vx5[:, si, a, :], False, it["last"] and t4 == 3, [pk], [bok], signal=True)
        if not it["last"]:
            return
        sa = sacc[it["gi"] % 2]
        sk = f"sacc{it['gi'] % 2}"
        k.copy("vector", sa, bank_o, [bok], [sk])
        k.ts("vector", cf_[:, 0:4], sa[:, :, 64], 1e-30, None, ALU.max, None, [sk], ["cf"])
        k.recip(cf_[:, 0:4], cf_[:, 0:4], ["cf"], ["cf"])
        k.tt("vector", cf_[:, 4:8], cf_[:, 0:4], gt[:, 4 * tb:4 * tb + 4, 3 * h + 1 + br], ALU.mult, ["cf"], ["cf2"])
        dst = oa[:, 4 * tb:4 * tb + 4, h * 64:(h + 1) * 64]
        if br == 0:
            k.tt("vector", cf_[:, 8:12], rz[:, 4 * tb:4 * tb + 4, h], gt[:, 4 * tb:4 * tb + 4, 3 * h], ALU.mult, [], ["cf3"])
            k.tt("vector", dst, cacc4[:, 4 * tb:4 * tb + 4, h, 0:64], cf_[:, 8:12].unsqueeze(2).broadcast_to([128, 4, 64]), ALU.mult, ["cf3"], [("oa", tb, h)])
        k.tt("vector", tmp, sa[:, :, 0:64], cf_[:, 4:8].unsqueeze(2).broadcast_to([128, 4, 64]), ALU.mult, [sk, "cf2"], ["tmpo"])
        k.tt("vector", dst, dst, tmp, ALU.add, ["tmpo", ("oa", tb, h)], [("oa", tb, h)])
        if it["last_of_tb"]:
            for t4 in range(4):
                tt = 4 * tb + t4
                for c in range(4):
                    b2 = 6 + (c % 2)
                    k.transpose(m.bank(b2)[:, 0:128], oa[:, tt, c * 128:(c + 1) * 128], C.cf["ident"], [("oa", tb, h_) for h_ in (2 * c, 2 * c + 1)], [f"ps{b2}"])
                    k.copy("scalar", stage[:, c, :], m.bank(b2)[:, 0:128], [f"ps{b2}"], [("stg", c)])
                    k.dma("sync", d["mixF"][c][:, tt * 128:(tt + 1) * 128], stage[:, c, :], [("stg", c)], [("mixF", c, tt)])

    for n in range(len(items) + 1):
        if n < len(items):
            score_stage(n)
        if n >= 1:
            pv_stage(n - 1)


_CST = None


def _get_cst():
    global _CST
    if _CST is None:
        c = host_consts()
        _CST = np.ascontiguousarray(np.concatenate([c[n] for n in ["ident", "ones", "blk64", "tril_qp", "lt_st", "uincl", "lstrict"]], axis=1).astype(np.float32))
    return _CST


def kernel(**inputs):
    x = np.asarray(inputs["x"], dtype=np.float32)
    nc = build()
    shared = {name: np.ascontiguousarray(np.asarray(inputs[name], dtype=np.float32)) for name, _ in INPUT_SPECS if name != "x"}
    shared["cst"] = _get_cst()
    shared.update(host_nsa_consts())
    in_maps = []
    for b in range(8):
        mp = dict(shared)
        mp["x"] = np.ascontiguousarray(x[b])
        in_maps.append(mp)
    res = run_bass_kernel_spmd(nc, in_maps, core_ids=list(range(8)))
    return np.stack([np.asarray(r["out"], dtype=np.float32) for r in res.results], axis=0)
```

```python
import math
import os
import numpy as np
import ml_dtypes
import concourse.bass as bass
import concourse.mybir as mybir
from concourse.bass_utils import run_bass_kernel_spmd

F32 = mybir.dt.float32
BF16 = mybir.dt.bfloat16
AF = mybir.ActivationFunctionType
ALU = mybir.AluOpType
AX = mybir.AxisListType

D_MODEL = 2048
T = 2048
DEPTH = 4
HD = 64
GW = 512
NH = 8
G = 2
R = 4
CMP_LEN = 32
CMP_STRIDE = 16
N_CMP = 127
SLC_LEN = 64
N_SLC = 32
SLC_TOP = 16
WINDOW = 512
N_BUCKETS = 32
MAX_DISTANCE = 1024
D_FF = 5632
W_IN_COLS = 5400
NEG = -30000.0
NKC = D_MODEL // 128
NTT = T // 128
NFC = D_FF // 128


class Prog:
    ENGS = ("tensor", "vector", "scalar", "gpsimd", "sync")

    def __init__(self, nc, n_dma_sems=10):
        self.nc = nc
        self.ops = {e: [] for e in self.ENGS}
        self.sem = {e: nc.alloc_semaphore(name=f"sem_{e}") for e in ("tensor", "vector", "scalar", "gpsimd")}
        self.cnt = {e: 0 for e in self.sem}
        nsem = {"sync": n_dma_sems, "gpsimd": 4}
        self.dma_sems = {q: [nc.alloc_semaphore(name=f"dsem_{q}{i}") for i in range(nsem[q])] for q in ("sync", "gpsimd")}
        self.dma_cnt = {q: [0] * nsem[q] for q in ("sync", "gpsimd")}
        self.dma_rr = {q: 0 for q in ("sync", "gpsimd")}
        self.waited = {e: {} for e in self.ENGS}
        self.last_w = {}
        self.readers = {}
        self.sem_by_id = {}
        self.pending = {}

    def _ev_id(self, sem):
        i = id(sem)
        self.sem_by_id[i] = sem
        return i

    def _need(self, eng, ev, waits):
        if ev is None:
            return
        sid, val, src_eng = ev
        if src_eng == "tensor" and eng == "tensor":
            return
        if self.waited[eng].get(sid, 0) >= val:
            return
        waits[sid] = max(waits.get(sid, 0), val)

    def op(self, eng, fn, reads=(), writes=(), signal=True):
        waits = {}
        for k in reads:
            self._need(eng, self.last_w.get(k), waits)
        for k in writes:
            self._need(eng, self.last_w.get(k), waits)
            for ev in self.readers.get(k, {}).values():
                self._need(eng, ev, waits)
        for sid, val in waits.items():
            self.waited[eng][sid] = val
        ev = None
        if signal:
            self.cnt[eng] += 1
            ev = (self._ev_id(self.sem[eng]), self.cnt[eng], eng)
        self.ops[eng].append((list(waits.items()), fn, (self.sem[eng], 1) if signal else None))
        if ev is not None:
            pr, pw = self.pending.pop(eng, ([], []))
            for k in list(reads) + pr:
                self.readers.setdefault(k, {})[ev[0]] = ev
            for k in list(writes) + pw:
                self.last_w[k] = ev
                self.readers[k] = {}
        else:
            assert eng == "tensor"
            pr, pw = self.pending.setdefault(eng, ([], []))
            pr.extend(reads)
            pw.extend(writes)
        return ev

    def dma(self, q, fn, reads=(), writes=(), war=()):
        waits = {}
        for k in war:
            for ev in self.readers.get(k, {}).values():
                self._need(q, ev, waits)
        for k in reads:
            self._need(q, self.last_w.get(k), waits)
        for k in writes:
            self._need(q, self.last_w.get(k), waits)
            for ev in self.readers.get(k, {}).values():
                self._need(q, ev, waits)
        i = self.dma_rr[q]
        self.dma_rr[q] = (i + 1) % len(self.dma_sems[q])
        s = self.dma_sems[q][i]
        sid = self._ev_id(s)
        prev = self.dma_cnt[q][i]
        if prev > 0 and self.waited[q].get(sid, 0) < prev:
            waits[sid] = max(waits.get(sid, 0), prev)
        for sd, val in waits.items():
            self.waited[q][sd] = val
        self.dma_cnt[q][i] = prev + 16
        ev = (sid, prev + 16, "dma_" + q)
        self.ops[q].append((list(waits.items()), fn, (s, 16)))
        for k in reads:
            self.readers.setdefault(k, {})[("d", q, i)] = ev
        for k in writes:
            self.last_w[k] = ev
            self.readers[k] = {}
        return ev

    def finish(self, out_keys):
        waits = {}
        for k in out_keys:
            self._need("sync", self.last_w.get(k), waits)
        final_waits = list(waits.items())
        nc = self.nc
        prog = self

        def emit(e, name):
            for waits_, fn, sig in prog.ops[name]:
                for sid, val in waits_:
                    e.wait_ge(prog.sem_by_id[sid], val)
                if fn is None:
                    continue
                inst = fn(e)
                if sig is not None:
                    inst.then_inc(sig[0], sig[1])

        with nc.Block() as block:
            @block.tensor
            def _(e):
                emit(e, "tensor")

            @block.vector
            def _(e):
                emit(e, "vector")

            @block.scalar
            def _(e):
                emit(e, "scalar")

            @block.gpsimd
            def _(e):
                emit(e, "gpsimd")

            @block.sync
            def _(e):
                emit(e, "sync")
                for q in ("sync", "gpsimd"):
                    for i, s_ in enumerate(prog.dma_sems[q]):
                        if prog.dma_cnt[q][i] > 0:
                            e.wait_ge(s_, prog.dma_cnt[q][i])


class K:
    def __init__(self, nc):
        self.nc = nc
        self.P = Prog(nc)
        self.ps_rr = 0

    def act(self, out, in_, func, reads, writes, **kw):
        return self.P.op("scalar", lambda e: e.activation(out=out, in_=in_, func=func, **kw), reads, writes)

    def ts(self, eng, out, in0, s1, s2, op0, op1, reads, writes, **kw):
        if op1 is None:
            return self.P.op(eng, lambda e: e.tensor_scalar(out=out, in0=in0, scalar1=s1, scalar2=None, op0=op0, **kw), reads, writes)
        return self.P.op(eng, lambda e: e.tensor_scalar(out=out, in0=in0, scalar1=s1, scalar2=s2, op0=op0, op1=op1, **kw), reads, writes)

    def tt(self, eng, out, in0, in1, op, reads, writes):
        return self.P.op(eng, lambda e: e.tensor_tensor(out=out, in0=in0, in1=in1, op=op), reads, writes)

    def stt(self, out, in0, scalar, in1, op0, op1, reads, writes):
        return self.P.op("vector", lambda e: e.scalar_tensor_tensor(out=out, in0=in0, scalar=scalar, in1=in1, op0=op0, op1=op1), reads, writes)

    def copy(self, eng, out, in_, reads, writes):
        if eng == "scalar":
            return self.P.op("scalar", lambda e: e.activation(out=out, in_=in_, func=AF.Copy), reads, writes)
        return self.P.op(eng, lambda e: e.tensor_copy(out=out, in_=in_), reads, writes)

    def recip(self, out, in_, reads, writes):
        return self.P.op("vector", lambda e: e.reciprocal(out=out, in_=in_), reads, writes)

    def memset(self, eng, ap, val, writes):
        return self.P.op(eng, lambda e: e.memset(ap, val), (), writes)

    def mm(self, out, lhsT, rhs, start, stop, reads, writes, signal=None, **kw):
        if signal is None:
            signal = stop
        return self.P.op("tensor", lambda e: e.matmul(out, lhsT, rhs, start=start, stop=stop, **kw), reads, writes, signal=signal)

    def transpose(self, out, in_, ident, reads, writes, signal=True):
        return self.P.op("tensor", lambda e: e.transpose(out, in_, ident), reads, writes, signal=signal)

    def dma(self, q, out, in_, reads, writes, war=()):
        return self.P.dma(q, lambda e: e.dma_start(out=out, in_=in_), reads, writes, war)


class Mem:
    def __init__(self, nc):
        self.big = nc.alloc_sbuf_tensor("big", [128, 192 * 256], F32)
        self.top = 0
        self.floor = 0
        self.ps = nc.alloc_psum_tensor("ps", [128, 8 * 512], F32)

    def alloc(self, free_shape, dtype=F32):
        n = int(np.prod(free_shape))
        words = n if dtype == F32 else (n + 1) // 2
        words = (words + 15) // 16 * 16
        a = self.top
        self.top += words
        assert self.top <= 192 * 256, f"SBUF overflow {self.top}"
        v = self.big[:, a:a + words]
        if dtype != F32:
            v = v.bitcast(dtype)
        v = v[:, 0:n]
        if len(free_shape) == 2:
            v = v.rearrange("p (a b) -> p a b", b=free_shape[1])
        elif len(free_shape) == 3:
            v = v.rearrange("p (a b c) -> p a b c", b=free_shape[1], c=free_shape[2])
        return v

    def set_floor(self):
        self.floor = self.top

    def reset(self):
        self.top = self.floor

    def bank(self, i, dtype=F32):
        v = self.ps[:, i * 512:(i + 1) * 512]
        if dtype != F32:
            v = v.bitcast(dtype)
        return v


def _barrier(P):
    evs = []
    for e, s in P.sem.items():
        if P.cnt[e] > 0:
            evs.append((P._ev_id(s), P.cnt[e]))
    for q in ("sync", "gpsimd"):
        for i, s in enumerate(P.dma_sems[q]):
            if P.dma_cnt[q][i] > 0:
                evs.append((P._ev_id(s), P.dma_cnt[q][i]))
    for eng in P.ENGS:
        waits = []
        for sid, val in evs:
            if P.waited[eng].get(sid, 0) < val:
                waits.append((sid, val))
                P.waited[eng][sid] = val
        if waits:
            P.ops[eng].append((waits, None, None))
    P.last_w = {}
    P.readers = {}


FM_GROUPS = [
    [(128 * j, 64 * j, 64) for j in range(4)] + [(128 * j + 64, 64 * (4 + j), 64) for j in range(4)],
    [(0, 512, 128), (128, 640, 128), (256, 768, 128), (384, 1024, 128)],
    [(0, 1304, 512)], [(0, 1816, 512)], [(0, 2328, 512)],
    [(0, 2840, 512)],
    [(0, 3864, 512)], [(0, 4376, 512)],
]
TM_BLOCKS = [
    (0, [(0, 896, 128), (128, 1152, 128), (256, 1280, 24)], 280),
    (280, [(0, 3352, 512)], 512),
    (792, [(0, 4888, 512)], 512),
]
N_FM = 32
N_TM = 1304


def _rel_bucket_np(n):
    n = np.maximum(n, 0)
    max_exact = N_BUCKETS // 2
    nf = np.maximum(n, 1).astype(np.float32)
    large = max_exact + (np.log(nf / np.float32(max_exact)) / np.float32(math.log(MAX_DISTANCE / max_exact)) * np.float32(N_BUCKETS - max_exact)).astype(np.int32)
    large = np.minimum(large, N_BUCKETS - 1)
    return np.where(n < max_exact, n, large)


def host_consts():
    c = {}
    c["ident"] = np.eye(128, dtype=np.float32)
    c["ones"] = np.ones((128, 128), np.float32)
    blk = np.zeros((128, 128), np.float32)
    blk[:64, :64] = 1
    blk[64:, 64:] = 1
    c["blk64"] = blk
    i = np.arange(128)
    c["tril_qp"] = (i[:, None] <= i[None, :]).astype(np.float32)
    c["lt_st"] = (i[:, None] < i[None, :]).astype(np.float32)
    c["uincl"] = (i[:, None] >= i[None, :]).astype(np.float32)
    c["lstrict"] = (i[:, None] < i[None, :]).astype(np.float32)
    return c


class Ctx:
    pass


def setup_consts(k, m, d, C):
    nc = k.nc
    C.cf = {}
    C.cb = {}
    names = ["ident", "ones", "blk64", "tril_qp", "lt_st", "uincl", "lstrict"]
    for i, n in enumerate(names):
        if n in ("ident", "tril_qp", "ones"):
            t = m.alloc([128])
            k.dma("sync", t, d["cst"][:, i * 128:(i + 1) * 128], (), [("cf", n)])
            C.cf[n] = t
        tb = m.alloc([128], BF16)
        k.dma("gpsimd", tb, d["cst"][:, i * 128:(i + 1) * 128], (), [("cb", n)])
        C.cb[n] = tb
    C.zeros = m.alloc([512], BF16)
    k.memset("vector", C.zeros, 0.0, [("zeros",)])
    C.eps6 = m.alloc([1])
    k.memset("vector", C.eps6, 1e-6, [("eps6",)])
    C.eps5 = m.alloc([1])
    k.memset("vector", C.eps5, 1e-5, [("eps5",)])
    C.one1 = m.alloc([1])
    k.memset("vector", C.one1, 1.0, [("one1",)])
    C.gmix = m.alloc([DEPTH * 16])
    C.gffn = m.alloc([DEPTH * 16])
    C.ggrp = m.alloc([DEPTH * 16])
    for nm, t in (("norm_mix", C.gmix), ("norm_ffn", C.gffn), ("group_gain", C.ggrp)):
        k.dma("sync", t, d[nm].rearrange("l (kc p) -> p (l kc)", p=128), (), [("g", nm)])
    C.keys = [("cf", n) for n in C.cf] + [("cb", n) for n in C.cb] + [("zeros",), ("eps6",), ("eps5",), ("one1",), ("g", "norm_mix"), ("g", "norm_ffn"), ("g", "group_gain")]


def phase_norm(k, m, C, x_ap, gvec, hT):
    xt = [m.alloc([2048]) for _ in range(2)]
    xs = [m.alloc([2048], BF16) for _ in range(2)]
    junk = m.alloc([2048], BF16)
    st = m.alloc([64])
    for tt in range(NTT):
        s = tt % 2
        k.dma("sync", xt[s], x_ap[tt * 128:(tt + 1) * 128, :], (), [f"xt{s}"])
        k.act(junk, xt[s], AF.Square, [f"xt{s}"], ["junk", ("ss", tt)], accum_out=st[:, tt:tt + 1])
        k.act(st[:, 16 + tt:17 + tt], st[:, tt:tt + 1], AF.Sqrt, [("ss", tt), ("eps6",)], [("sq", tt)], scale=1.0 / D_MODEL, bias=C.eps6[:, 0:1])
        k.recip(st[:, 32 + tt:33 + tt], st[:, 16 + tt:17 + tt], [("sq", tt)], [("rs", tt)])
        k.ts("vector", xs[s], xt[s], st[:, 32 + tt:33 + tt], None, ALU.mult, None, [f"xt{s}", ("rs", tt)], [f"xs{s}"])
        for half in range(2):
            pst = m.bank(6 + half, BF16)
            for j in range(8):
                kc = half * 8 + j
                k.transpose(pst[:, j * 128:(j + 1) * 128], xs[s][:, kc * 128:(kc + 1) * 128], C.cb["ident"],
                            [f"xs{s}", ("cb", "ident")], [f"pst{half}"], signal=(j == 7))
            k.tt("vector", hT[:, half * 8:(half + 1) * 8, tt * 128:(tt + 1) * 128],
                 pst.rearrange("p (a b) -> p a b", b=128),
                 gvec[:, half * 8:(half + 1) * 8].unsqueeze(2).broadcast_to([128, 8, 128]),
                 ALU.mult, [f"pst{half}"] + C.keys, [("hT", tt)])


def phase_inproj(k, m, C, d, l, hT):
    w_l = d["w_in"][l].rearrange("(kc p) c -> p kc c", p=128)
    wt = [m.alloc([16, 512], BF16) for _ in range(2)]
    stage = [m.alloc([2048]) for _ in range(3)]
    ci = 0
    bi = 0
    for g, segs in enumerate(FM_GROUPS):
        s = g % 2
        for (dst, src, n) in segs:
            k.dma("gpsimd", wt[s][:, :, dst:dst + n], w_l[:, :, src:src + n], (), [(f"wt{s}", dst)], war=[f"wtall{s}"])
        for c in range(4):
            segkeys = [f"wtall{s}"] + [(f"wt{s}", dst) for (dst, src, n) in segs if dst < (c + 1) * 128 and dst + n > c * 128]
            sg = stage[ci % 3]
            for tb in range(4):
                b = bi % 6
                bi += 1
                bank = m.bank(b)
                for kc in range(16):
                    k.mm(bank, wt[s][:, kc, c * 128:(c + 1) * 128], hT[:, kc, tb * 512:(tb + 1) * 512], kc == 0, kc == 15,
                         segkeys + [("hT", 4 * tb + i) for i in range(4)], [f"ps{b}"])
                k.copy("scalar" if (bi % 2) else "vector", sg[:, tb * 512:(tb + 1) * 512], bank, [f"ps{b}"], [(f"stage{ci % 3}", tb)])
            k.dma("sync", d["projF"][4 * g + c], sg, [(f"stage{ci % 3}", tb) for tb in range(4)], [("projF", 4 * g + c)])
            ci += 1
    for bidx, (col0, segs, ncols) in enumerate(TM_BLOCKS):
        s = bidx % 2
        for (dst, src, n) in segs:
            k.dma("gpsimd", wt[s][:, :, dst:dst + n], w_l[:, :, src:src + n], (), [(f"wt{s}", dst)], war=[f"wtall{s}"])
        segkeys = [f"wtall{s}"] + [(f"wt{s}", dst) for (dst, src, n) in segs]
        for tt in range(NTT):
            b = bi % 6
            bi += 1
            bank = m.bank(b)
            sg = stage[ci % 3]
            for kc in range(16):
                k.mm(bank[:, 0:ncols], hT[:, kc, tt * 128:(tt + 1) * 128], wt[s][:, kc, 0:ncols], kc == 0, kc == 15,
                     segkeys + [("hT", tt)], [f"ps{b}"])
            k.copy("scalar" if (bi % 2) else "vector", sg[:, 0:ncols], bank[:, 0:ncols], [f"ps{b}"], [(f"stage{ci % 3}", 0)])
            k.dma("sync", d["projT"][tt * 128:(tt + 1) * 128, col0:col0 + ncols], sg[:, 0:ncols], [(f"stage{ci % 3}", 0)], [("projT", bidx, tt)])
            ci += 1


def phase_wout(k, m, C, d, l, x_in, x_out):
    mixT = m.alloc([16, 2048], BF16)
    xg = [m.alloc([4, 2048]) for _ in range(1)]
    sq = m.alloc([4, 2048], BF16)
    tmp = [m.alloc([512]) for _ in range(2)]
    bi = 0
    for grp in range(4):
        x4 = xg[0]
        for c in range(4):
            k.dma("sync", x4[:, c, :], d["mixF"][4 * grp + c], (), [("xg", c)])
            k.act(sq[:, c, :], x4[:, c, :], AF.Square, [("xg", c)], [("sq", c)])
        for tb in range(4):
            b = bi % 6
            bi += 1
            bank = m.bank(b)
            for c in range(4):
                k.mm(bank, C.cb["ones"], sq[:, c, tb * 512:(tb + 1) * 512], c == 0, c == 3, [("sq", c)] + C.keys, [f"ps{b}"])
            t_ = tmp[tb % 2]
            k.act(t_, bank, AF.Sqrt, [f"ps{b}"] + C.keys, [f"tmp{tb % 2}"], scale=1.0 / GW, bias=C.eps6[:, 0:1])
            k.recip(t_, t_, [f"tmp{tb % 2}"], [f"tmp{tb % 2}"])
            for c in range(4):
                ch = 4 * grp + c
                k.stt(mixT[:, ch, tb * 512:(tb + 1) * 512], x4[:, c, tb * 512:(tb + 1) * 512], C.ggrp[:, l * 16 + ch:l * 16 + ch + 1], t_,
                      ALU.mult, ALU.mult, [("xg", c), f"tmp{tb % 2}"] + C.keys, [("mixT", ch, tb)])
    w_l = d["w_out"][l].rearrange("(kc p) n -> p kc n", p=128)
    wt = [m.alloc([16, 512], BF16) for _ in range(2)]
    xin = [m.alloc([512]) for _ in range(3)]
    xi = 0
    for nb in range(4):
        s = nb % 2
        k.dma("gpsimd", wt[s], w_l[:, :, nb * 512:(nb + 1) * 512], (), [f"wo{s}"])
        for tt in range(NTT):
            b = bi % 6
            bi += 1
            bank = m.bank(b)
            xs_ = xin[xi % 3]
            k.dma("sync", xs_, x_in[tt * 128:(tt + 1) * 128, nb * 512:(nb + 1) * 512], (), [f"xin{xi % 3}"])
            for kc in range(16):
                k.mm(bank, mixT[:, kc, tt * 128:(tt + 1) * 128], wt[s][:, kc, :], kc == 0, kc == 15,
                     [f"wo{s}", ("mixT", kc, tt // 4)], [f"ps{b}"])
            k.tt("vector", xs_, bank, xs_, ALU.add, [f"ps{b}", f"xin{xi % 3}"], [f"xin{xi % 3}"])
            k.dma("sync", x_out[tt * 128:(tt + 1) * 128, nb * 512:(nb + 1) * 512], xs_, [f"xin{xi % 3}"], [("xout", nb, tt)])
            xi += 1


def phase_ffn_up(k, m, C, d, l, hT):
    wg_l = d["w_ffn_gate"][l].rearrange("(kc p) f -> p kc f", p=128)
    wu_l = d["w_ffn_up"][l].rearrange("(kc p) f -> p kc f", p=128)
    wg = [m.alloc([16, 512], BF16) for _ in range(2)]
    wu = [m.alloc([16, 512], BF16) for _ in range(2)]
    act = [m.alloc([2048], BF16) for _ in range(3)]
    sil = [m.alloc([512]) for _ in range(2)]
    bi = 0
    ai = 0
    for fg in range(NFC // 4):
        s = fg % 2
        k.dma("gpsimd", wg[s], wg_l[:, :, fg * 512:(fg + 1) * 512], (), [f"wg{s}"])
        k.dma("gpsimd", wu[s], wu_l[:, :, fg * 512:(fg + 1) * 512], (), [f"wu{s}"])
        for c in range(4):
            fc = fg * 4 + c
            a_ = act[ai % 3]
            for tb in range(4):
                bg = bi % 6
                bu = (bi + 1) % 6
                bi += 2
                hk = [("hT", 4 * tb + i) for i in range(4)]
                for kc in range(16):
                    k.mm(m.bank(bg), wg[s][:, kc, c * 128:(c + 1) * 128], hT[:, kc, tb * 512:(tb + 1) * 512], kc == 0, kc == 15, [f"wg{s}"] + hk, [f"ps{bg}"])
                for kc in range(16):
                    k.mm(m.bank(bu), wu[s][:, kc, c * 128:(c + 1) * 128], hT[:, kc, tb * 512:(tb + 1) * 512], kc == 0, kc == 15, [f"wu{s}"] + hk, [f"ps{bu}"])
                s_ = sil[tb % 2]
                k.act(s_, m.bank(bg), AF.Silu, [f"ps{bg}"], [f"sil{tb % 2}"])
                k.tt("vector", a_[:, tb * 512:(tb + 1) * 512], m.bank(bu), s_, ALU.mult, [f"ps{bu}", f"sil{tb % 2}"], [(f"act{ai % 3}", tb)])
            k.dma("sync", d["actD"][fc], a_, [(f"act{ai % 3}", tb) for tb in range(4)], [("actD", fc)])
            ai += 1


def phase_ffn_down(k, m, C, d, l, x_in, x_out):
    wd_l = d["w_ffn_down"][l].rearrange("(fc p) n -> p fc n", p=128)
    actv = d["actD"].rearrange("fc p t -> p fc t")
    wd = [m.alloc([NFC, 512], BF16) for _ in range(2)]
    ab = [m.alloc([NFC, 512], BF16) for _ in range(2)]
    xin = [m.alloc([512]) for _ in range(3)]
    bi = 0
    xi = 0
    ai = 0
    for nbi, nb in enumerate([int(c) for c in os.environ.get("NB_ORDER", "0123")]):
        s = nbi % 2
        for h in range(2):
            k.dma("gpsimd", wd[s][:, h * 22:(h + 1) * 22, :], wd_l[:, h * 22:(h + 1) * 22, nb * 512:(nb + 1) * 512], (), [(f"wd{s}", h)])
        for tg in range(4):
            a_ = ab[ai % 2]
            for h in range(2):
                k.dma("sync", a_[:, h * 22:(h + 1) * 22, :], actv[:, h * 22:(h + 1) * 22, tg * 512:(tg + 1) * 512], (), [(f"ab{ai % 2}", h)])
            for t4 in range(4):
                tt = tg * 4 + t4
                b = bi % 6
                bi += 1
                bank = m.bank(b)
                xs_ = xin[xi % 3]
                k.dma("sync", xs_, x_in[tt * 128:(tt + 1) * 128, nb * 512:(nb + 1) * 512], (), [f"xin{xi % 3}"])
                for fc in range(NFC):
                    k.mm(bank, a_[:, fc, t4 * 128:(t4 + 1) * 128], wd[s][:, fc, :], fc == 0, fc == NFC - 1,
                         [(f"wd{s}", fc // 22), (f"ab{ai % 2}", fc // 22)], [f"ps{b}"])
                k.tt("vector", xs_, bank, xs_, ALU.add, [f"ps{b}", f"xin{xi % 3}"], [f"xin{xi % 3}"])
                k.dma("sync", x_out[tt * 128:(tt + 1) * 128, nb * 512:(nb + 1) * 512], xs_, [f"xin{xi % 3}"], [("xout", nb, tt)])
                xi += 1
            ai += 1


INPUT_SPECS = [
    ("x", [T, D_MODEL]), ("w_in", [DEPTH, D_MODEL, W_IN_COLS]), ("w_out", [DEPTH, D_MODEL, D_MODEL]),
    ("norm_mix", [DEPTH, D_MODEL]), ("norm_ffn", [DEPTH, D_MODEL]), ("q_gain", [DEPTH, HD]), ("k_gain", [DEPTH, HD]),
    ("cmp_pos", [DEPTH, 2, CMP_LEN, HD]), ("cmp_w1", [DEPTH, 2, CMP_LEN * HD, HD]), ("cmp_w2", [DEPTH, 2, HD, HD]),
    ("rel_table", [N_BUCKETS, NH]), ("conv_w", [DEPTH, 3, GW]), ("sgu_w", [DEPTH, NH, 128, 128]), ("sgu_b", [DEPTH, NH, 128]),
    ("group_gain", [DEPTH, D_MODEL]), ("w_ffn_gate", [DEPTH, D_MODEL, D_FF]), ("w_ffn_up", [DEPTH, D_MODEL, D_FF]),
    ("w_ffn_down", [DEPTH, D_FF, D_MODEL]),
]
N_CST = 7


def build(n_layers=DEPTH, dbg=(), phases=None, mix=("conv", "sgu", "sb", "nsa")):
    nc = bass.Bass("TRN2", target_bir_lowering=False)
    d = {}
    for name, shape in INPUT_SPECS:
        d[name] = nc.dram_tensor(name, shape, F32, kind="ExternalInput").ap()
    d["cst"] = nc.dram_tensor("cst", [128, N_CST * 128], F32, kind="ExternalInput").ap()

    def scratch(name, shape, dt=F32):
        kind = "ExternalOutput" if name in dbg else "Internal"
        d[name] = nc.dram_tensor(name, shape, dt, kind=kind).ap()

    d["oh"] = nc.dram_tensor("oh", [33, OH_L], F32, kind="ExternalInput").ap()
    d["scadd"] = nc.dram_tensor("scadd", [T, N_SLC], F32, kind="ExternalInput").ap()
    d["esel"] = nc.dram_tensor("esel", [N_SLC, T], F32, kind="ExternalInput").ap()
    d["ovc"] = nc.dram_tensor("ovc", [N_CMP, N_SLC], F32, kind="ExternalInput").ap()
    scratch("R", [8, 128, OH_L])
    scratch("projF", [N_FM, 128, T])
    scratch("projT", [T, N_TM])
    scratch("mixF", [16, 128, T])
    scratch("actD", [NFC, 128, T], BF16)
    scratch("xa", [T, D_MODEL])
    scratch("xb", [T, D_MODEL])
    d["out"] = nc.dram_tensor("out", [T, D_MODEL], F32, kind="ExternalOutput").ap()

    k = K(nc)
    m = Mem(nc)
    C = Ctx()
    with nc.allow_non_contiguous_dma(reason="small parameter loads"):
        setup_consts(k, m, d, C)
        m.set_floor()
        _barrier(k.P)
        if "nsa" in mix:
            setup_nsa_tables(k, m, C, d)
            _barrier(k.P)
        x_cur = d["x"]
        for l in range(n_layers):
            ph = phases if phases is not None else ("norm1", "inproj", "mixers", "wout", "ffn")
            x2 = d["out"] if l == n_layers - 1 else d["xb"]
            if "inproj" in ph:
                m.reset()
                hT = m.alloc([16, T], BF16)
                phase_norm(k, m, C, x_cur, C.gmix[:, l * 16:(l + 1) * 16], hT)
                phase_inproj(k, m, C, d, l, hT)
                _barrier(k.P)
            if "mixers" in ph:
                phase_mixers(k, m, C, d, l, which=mix)
            if "wout" in ph:
                m.reset()
                phase_wout(k, m, C, d, l, x_cur, d["xa"])
                _barrier(k.P)
            if "ffn" in ph:
                m.reset()
                hT = m.alloc([16, T], BF16)
                phase_norm(k, m, C, d["xa"], C.gffn[:, l * 16:(l + 1) * 16], hT)
                phase_ffn_up(k, m, C, d, l, hT)
                _barrier(k.P)
                m.reset()
                phase_ffn_down(k, m, C, d, l, d["xa"], x2)
                _barrier(k.P)
            x_cur = x2
        k.P.finish([])
    return nc


def conv_setup(k, m, C, d, l):
    cv = Ctx()
    cv.cw = m.alloc([12])
    k.dma("sync", cv.cw.rearrange("p (w j) -> p w j", j=4), d["conv_w"][l].rearrange("w (j p) -> p w j", p=128), (), ["cw"])
    cv.bg = [m.alloc([2048]) for _ in range(2)]
    cv.cg = [m.alloc([2048]) for _ in range(2)]
    cv.hh = [m.alloc([2048]) for _ in range(2)]
    cv.z = [m.alloc([2050]) for _ in range(2)]
    cv.y = [m.alloc([2048]) for _ in range(2)]
    for s in range(2):
        k.memset("gpsimd", cv.z[s][:, 0:2], 0.0, [f"z{s}"])
    return cv


def conv_piece(k, m, C, d, l, cv, j):
    s = j % 2
    cw, bg, cg, hh, z, y = cv.cw, cv.bg, cv.cg, cv.hh, cv.z, cv.y
    k.dma("sync", bg[s], d["projF"][8 + j], (), [f"bg{s}"])
    k.dma("sync", cg[s], d["projF"][12 + j], (), [f"cg{s}"])
    k.dma("sync", hh[s], d["projF"][16 + j], (), [f"hh{s}"])
    k.tt("gpsimd", z[s][:, 2:2050], cg[s], hh[s], ALU.mult, [f"cg{s}", f"hh{s}"], [f"z{s}"])
    k.ts("vector", y[s], z[s][:, 2:2050], cw[:, 8 + j:9 + j], None, ALU.mult, None, [f"z{s}", "cw"], [f"y{s}"])
    k.stt(y[s], z[s][:, 1:2049], cw[:, 4 + j:5 + j], y[s], ALU.mult, ALU.add, [f"z{s}", "cw", f"y{s}"], [f"y{s}"])
    k.stt(y[s], z[s][:, 0:2048], cw[:, j:j + 1], y[s], ALU.mult, ALU.add, [f"z{s}", "cw", f"y{s}"], [f"y{s}"])
    k.tt("vector", y[s], y[s], bg[s], ALU.mult, [f"y{s}", f"bg{s}"], [f"y{s}"])
    k.dma("sync", d["mixF"][4 + j], y[s], [f"y{s}"], [("mixF", 4 + j)])


def phase_conv(k, m, C, d, l):
    cv = conv_setup(k, m, C, d, l)
    for j in range(4):
        conv_piece(k, m, C, d, l, cv, j)


def gelu_tanh(k, m, out, x, tmp, tmp2, rk, wk, tk):
    c = 1.5957691216057308
    k.tt("vector", tmp, x, x, ALU.mult, rk, tk)
    k.ts("vector", tmp, tmp, 0.044715 * c, c, ALU.mult, ALU.add, tk, tk)
    k.tt("vector", tmp, tmp, x, ALU.mult, rk + tk, tk)
    k.act(tmp2, tmp, AF.Sigmoid, tk, [tk[0] + "_2"])
    k.tt("vector", out, x, tmp2, ALU.mult, rk + [tk[0] + "_2"], wk)


def phase_sgu(k, m, C, d, l):
    wraw = m.alloc([8, 128])
    k.dma("sync", wraw, d["sgu_w"][l].rearrange("h p q -> p h q"), (), ["wraw"])
    wT = m.alloc([8, 128], BF16)
    bsb = m.alloc([8])
    k.dma("sync", bsb, d["sgu_b"][l].rearrange("h p -> p h"), (), ["bsb"])
    for h in range(8):
        b = h % 2
        k.transpose(m.bank(b)[:, 0:128], wraw[:, h, :], C.cf["ident"], ["wraw"] + C.keys, [f"ps{b}"])
        k.tt("vector", wT[:, h, :], m.bank(b)[:, 0:128], C.cf["tril_qp"], ALU.mult, [f"ps{b}"] + C.keys, [("wT", h)])
    uv = [m.alloc([1024]) for _ in range(2)]
    t1 = m.alloc([1024])
    t2 = m.alloc([1024])
    gl = [m.alloc([1024]) for _ in range(2)]
    vln = [m.alloc([512], BF16) for _ in range(2)]
    st = m.alloc([16])
    oc = [m.alloc([512]) for _ in range(2)]
    stage = [m.alloc([4, 512]) for _ in range(2)]
    for tt in range(NTT):
        s = tt % 2
        for c in range(4):
            k.dma("sync", stage[s][:, c, 0:128], d["projF"][20 + c][:, tt * 128:(tt + 1) * 128], (), [(f"ufm{s}", c)])
        for c in range(4):
            b = 2 + c % 2
            k.transpose(m.bank(b)[:, 0:128], stage[s][:, c, 0:128], C.cf["ident"], [(f"ufm{s}", c)] + C.keys, [f"ps{b}"])
            k.copy("scalar", uv[s][:, c * 128:(c + 1) * 128], m.bank(b)[:, 0:128], [f"ps{b}"], [(f"uv{s}", c)])
        k.dma("sync", uv[s][:, 512:1024], d["projT"][tt * 128:(tt + 1) * 128, 280:792], (), [(f"uv{s}", 4)])
        gelu_tanh(k, m, gl[s], uv[s], t1, t2, [(f"uv{s}", c) for c in range(5)], [f"gl{s}"], ["t1"])
        k.P.op("vector", (lambda o, i: (lambda e: e.bn_stats(out=o, in_=i)))(st[:, 0:6], gl[s][:, 512:1024]), [f"gl{s}"], ["bst"])
        k.P.op("vector", (lambda o, i: (lambda e: e.bn_aggr(out=o, in_=i)))(st[:, 8:10], st[:, 0:6]), ["bst"], ["bag"])
        k.act(st[:, 10:11], st[:, 9:10], AF.Sqrt, ["bag"] + C.keys, ["lnsd"], scale=1.0, bias=C.eps5[:, 0:1])
        k.recip(st[:, 11:12], st[:, 10:11], ["lnsd"], ["lnrs"])
        k.ts("vector", vln[s], gl[s][:, 512:1024], st[:, 8:9], st[:, 11:12], ALU.subtract, ALU.mult, [f"gl{s}", "bag", "lnrs"], [f"vln{s}"])
        b = 4 + tt % 2
        for h in range(8):
            k.mm(m.bank(b)[:, h * 64:(h + 1) * 64], wT[:, h, :], vln[s][:, h * 64:(h + 1) * 64], True, True,
                 [("wT", h), f"vln{s}"], [f"ps{b}"], signal=(h == 7))
        k.tt("vector", oc[s].rearrange("p (h e) -> p h e", e=64), m.bank(b).rearrange("p (h e) -> p h e", e=64),
             bsb.unsqueeze(2).broadcast_to([128, 8, 64]), ALU.add, [f"ps{b}", "bsb"], [f"oc{s}"])
        k.tt("vector", oc[s], oc[s], gl[s][:, 0:512], ALU.mult, [f"oc{s}", f"gl{s}"], [f"oc{s}"])
        for c in range(4):
            b2 = 2 + c % 2
            k.transpose(m.bank(b2)[:, 128:256], oc[s][:, c * 128:(c + 1) * 128], C.cf["ident"], [f"oc{s}"] + C.keys, [f"ps{b2}"])
            k.copy("scalar", stage[s][:, c, 128:256], m.bank(b2)[:, 128:256], [f"ps{b2}"], [(f"ofm{s}", c)])
            k.dma("sync", d["mixF"][8 + c][:, tt * 128:(tt + 1) * 128], stage[s][:, c, 128:256], [(f"ofm{s}", c)], [("mixF", 8 + c, tt)])


def phase_sb(k, m, C, d, l, cv=None):
    scale = HD ** -0.5
    qf = m.alloc([2048])
    kf = m.alloc([2048])
    qs = [m.alloc([2048], BF16) for _ in range(2)]
    qn = [m.alloc([2048], BF16) for _ in range(2)]
    kb = [m.alloc([2048], BF16) for _ in range(2)]
    vb = m.alloc([16, 512], BF16)
    k.dma("gpsimd", vb, d["projT"][:, 792:1304].rearrange("(st p) c -> p st c", p=128), (), ["vb"])
    ef = [[m.alloc([512]) for _ in range(3)] for _ in range(2)]
    sp = [[m.alloc([512], BF16) for _ in range(3)] for _ in range(2)]
    aT = [[m.alloc([512], BF16) for _ in range(2)] for _ in range(2)]
    osb = [[m.alloc([512]) for _ in range(2)] for _ in range(2)]
    for j in range(4):
        jj = j % 2
        k.dma("sync", qf, d["projF"][24 + j], (), ["qf"])
        k.dma("sync", kf, d["projF"][28 + j], (), ["kf"])
        k.ts("vector", qs[jj], qf, scale, None, ALU.mult, None, ["qf"], [f"qs{jj}"])
        k.ts("gpsimd", qn[jj], qf, -scale, None, ALU.mult, None, ["qf"], [f"qn{jj}"])
        k.copy("gpsimd", kb[jj], kf, ["kf"], [f"kb{jj}"])
        if cv is not None:
            conv_piece(k, m, C, d, l, cv, j)
        QS, QN, KB = qs[jj], qn[jj], kb[jj]
        qsk, qnk, kbk = f"qs{jj}", f"qn{jj}", f"kb{jj}"
        for tb in range(4):
            t0 = tb * 512
            steps = list(range(4 * tb + 3, -1, -1))
            for ch in range(2):
                k.mm(m.bank(3 * ch + 1), C.zeros[:, 0:128], C.zeros, True, False, [("zeros",)], [f"ps{3 * ch + 1}"], signal=True, skip_group_check=True)
                k.mm(m.bank(3 * ch + 2)[0:64, :], C.zeros[:, 0:64], C.zeros, True, False, [("zeros",)], [f"ps{3 * ch + 2}"], signal=True)

            def geom(si):
                s0 = si * 128
                c0 = max(0, s0 - t0)
                return s0, c0, s0 >= t0, slice(c0, 512), slice(t0 + c0, t0 + 512)

            def sp_qk(ch, n):
                si = steps[n]
                s0, c0, diag, cols, tcols = geom(si)
                pr = slice(64 * ch, 64 * ch + 64)
                bz = m.bank(3 * ch)
                zk = f"ps{3 * ch}"
                k.mm(bz[:, cols], KB[pr, s0:s0 + 128], QS[pr, tcols], True, True, [kbk, qsk], [zk])

            def sp_act(ch, n):
                si = steps[n]
                s0, c0, diag, cols, tcols = geom(si)
                sl = n % 3
                bz = m.bank(3 * ch)
                zk = f"ps{3 * ch}"
                k.act(ef[ch][sl][:, cols], bz[:, cols], AF.Exp, [zk], [f"ef{ch}{sl}"])
                k.act(sp[ch][sl][:, cols], ef[ch][sl][:, cols], AF.Ln, [f"ef{ch}{sl}"] + C.keys, [f"sp{ch}{sl}"], bias=C.one1[:, 0:1], scale=1.0)
                if diag:
                    k.tt("gpsimd", sp[ch][sl][:, c0:c0 + 128], sp[ch][sl][:, c0:c0 + 128], C.cb["lt_st"], ALU.mult, [f"sp{ch}{sl}"] + C.keys, [f"sp{ch}{sl}"])

            def chain_a(ch, n):
                si = steps[n]
                s0, c0, diag, cols, tcols = geom(si)
                pr = slice(64 * ch, 64 * ch + 64)
                sl = n % 3
                bc = m.bank(3 * ch + 1)
                ck = f"ps{3 * ch + 1}"
                k.mm(bc[:, cols], C.cb["uincl"], sp[ch][sl][:, cols], False, False, [f"sp{ch}{sl}"] + C.keys, [ck], signal=False, skip_group_check=True)
                k.mm(bc[:, cols], KB[pr, s0:s0 + 128], QN[pr, tcols], False, False, [kbk, qnk], [ck], signal=True, skip_group_check=True)

            def chain_b(ch, n):
                si = steps[n]
                s0, c0, diag, cols, tcols = geom(si)
                sl = n % 2
                bc = m.bank(3 * ch + 1)
                ck = f"ps{3 * ch + 1}"
                k.act(aT[ch][sl][:, cols], bc[:, cols], AF.Exp, [ck], [f"aT{ch}{sl}"], scale=-1.0)
                if diag:
                    k.tt("gpsimd", aT[ch][sl][:, c0:c0 + 128], aT[ch][sl][:, c0:c0 + 128], C.cb["lt_st"], ALU.mult, [f"aT{ch}{sl}"] + C.keys, [f"aT{ch}{sl}"])

            def chain_c(ch, n):
                si = steps[n]
                s0, c0, diag, cols, tcols = geom(si)
                pr = slice(64 * ch, 64 * ch + 64)
                sl = n % 2
                h = 2 * j + ch
                bc = m.bank(3 * ch + 1)
                ck = f"ps{3 * ch + 1}"
                bo = m.bank(3 * ch + 2)
                ok_ = f"ps{3 * ch + 2}"
                k.mm(bc[:, cols], KB[pr, s0:s0 + 128], QS[pr, tcols], False, False, [kbk, qsk], [ck], signal=False, skip_group_check=True)
                s3 = n % 3
                k.mm(bc[:, cols], C.cb["lstrict"], sp[ch][s3][:, cols], False, si == 0, [f"sp{ch}{s3}"] + C.keys, [ck], signal=True, skip_group_check=True)
                k.mm(bo[0:64, cols], vb[:, si, h * 64:(h + 1) * 64], aT[ch][sl][:, cols], False, si == 0, ["vb", f"aT{ch}{sl}"], [ok_], signal=True)

            ns = len(steps)
            for n0 in range(min(2, ns)):
                for ch in range(2):
                    sp_qk(ch, n0)
                    sp_act(ch, n0)
            for n in range(ns):
                for ch in range(2):
                    chain_a(ch, n)
                if n + 2 < ns:
                    for ch in range(2):
                        sp_qk(ch, n + 2)
                for ch in range(2):
                    chain_b(ch, n)
                if n + 2 < ns:
                    for ch in range(2):
                        sp_act(ch, n + 2)
                for ch in range(2):
                    chain_c(ch, n)
            for ch in range(2):
                o_ = osb[ch][tb % 2]
                ok_ = f"ps{3 * ch + 2}"
                k.copy("vector", o_[0:64, :], m.bank(3 * ch + 2)[0:64, :], [ok_], [f"osb{ch}{tb % 2}"])
                k.dma("sync", d["mixF"][12 + j][64 * ch:64 * ch + 64, t0:t0 + 512], o_[0:64, :], [f"osb{ch}{tb % 2}"], [("mixF", 12 + j, ch, tb)])


def phase_mixers(k, m, C, d, l, which=("conv", "sgu", "sb", "nsa")):
    fold = ("conv" in which) and ("sb" in which)
    if "conv" in which and not fold:
        m.reset()
        phase_conv(k, m, C, d, l)
        _barrier(k.P)
    if "sgu" in which:
        m.reset()
        phase_sgu(k, m, C, d, l)
        _barrier(k.P)
    if "sb" in which:
        m.reset()
        cv = conv_setup(k, m, C, d, l) if fold else None
        phase_sb(k, m, C, d, l, cv)
        _barrier(k.P)
    if "nsa" in which:
        m.reset()
        phase_nsa(k, m, C, d, l)
        _barrier(k.P)


OH_L = 6366
OH_SO, OH_WO, OH_CO = 0, 1535, 2302


def host_nsa_consts():
    oh = np.zeros((33, OH_L), np.float32)
    x = np.arange(1535) - 127
    b = np.where(x < 0, 32, _rel_bucket_np(x))
    oh[b, OH_SO + np.arange(1535)] = 1
    x = np.arange(767) - 127
    b = np.where((x < 0) | (x >= WINDOW), 32, _rel_bucket_np(x))
    oh[b, OH_WO + np.arange(767)] = 1
    x = np.arange(4064) - 2016 - 31
    b = np.where(x < 0, 32, _rel_bucket_np(x))
    oh[b, OH_CO + np.arange(4064)] = 1
    t = np.arange(T)[:, None]
    jj = np.arange(N_SLC)[None, :]
    cur = t // SLC_LEN
    valid = jj * SLC_LEN <= t
    forced = (jj == 0) | (jj == cur) | (jj == cur - 1)
    scadd = np.where(valid, np.where(forced, 1000.0, 0.0), -1e30).astype(np.float32)
    esel = (np.arange(T)[None, :] // SLC_LEN == np.arange(N_SLC)[:, None]).astype(np.float32)
    c0 = np.arange(N_CMP)[:, None] * CMP_STRIDE
    s0 = np.arange(N_SLC)[None, :] * SLC_LEN
    ov = np.minimum(c0 + CMP_LEN, s0 + SLC_LEN) - np.maximum(c0, s0)
    ovc = (np.maximum(ov, 0) / CMP_LEN).astype(np.float32)
    return {"oh": oh, "scadd": scadd, "esel": esel, "ovc": ovc}


def setup_nsa_tables(k, m, C, d):
    tabx = m.alloc([8])
    k.memset("vector", tabx[0:64, :], NEG, ["tabx"])
    k.dma("sync", tabx[0:32, :], d["rel_table"], (), ["tabx"])
    oh = m.alloc([OH_L])
    k.dma("sync", oh[0:33, :], d["oh"], (), ["oh"])
    row = [m.alloc([OH_L]) for _ in range(2)]
    bi = 0
    for h in range(8):
        r_ = row[h % 2]
        for c0 in range(0, OH_L, 512):
            n = min(512, OH_L - c0)
            b = bi % 6
            bi += 1
            k.mm(m.bank(b)[:, 0:n], tabx[0:33, h:h + 1].broadcast_to([33, 128]), oh[0:33, c0:c0 + n], True, True, ["tabx", "oh"], [f"ps{b}"])
            k.copy("scalar" if bi % 2 else "vector", r_[:, c0:c0 + n], m.bank(b)[:, 0:n], [f"ps{b}"], [(f"row{h % 2}", c0)])
        k.dma("sync", d["R"][h], r_, [(f"row{h % 2}", c0) for c0 in range(0, OH_L, 512)], [("R", h)], war=[f"rowall{h % 2}"])


def phase_nsa(k, m, C, d, l):
    scale = HD ** -0.5
    STOP = float(os.environ.get("NSA_STOP", "99"))
    if STOP <= 0:
        return
    Rt = d["R"].tensor
    tabW = m.alloc([8, 2048], BF16)
    Wc = tabW
    Ws = tabW[:, :, 0:1408]
    Ww = tabW[:, :, 1408:2048]
    for h in range(8):
        base = h * 128 * OH_L
        k.dma("gpsimd", Wc[0:127, h, :], bass.AP(tensor=Rt, offset=base + OH_CO + 2016, ap=[[OH_L - 16, 127], [1, 2048]]), (), [("Wc", h)])
    esel = m.alloc([2048], BF16)
    k.dma("gpsimd", esel[0:32, :], d["esel"], (), ["esel"])
    scadd = m.alloc([16, 32])
    k.dma("sync", scadd, d["scadd"].rearrange("(tt p) j -> p tt j", p=128), (), ["scadd"])
    qg = m.alloc([2])
    kg = m.alloc([1])
    for half in range(2):
        k.dma("sync", qg[64 * half:64 * half + 64, 0:1], d["q_gain"][l].rearrange("(d o) -> d o", o=1), (), [("qg", half)])
        k.dma("sync", kg[64 * half:64 * half + 64, 0:1], d["k_gain"][l].rearrange("(d o) -> d o", o=1), (), [("kg", half)])
    k.ts("vector", qg[:, 1:2], qg[:, 0:1], scale, None, ALU.mult, None, [("qg", 0), ("qg", 1)], ["qgs"])
    qT = m.alloc([4, 2048], BF16)
    ksT = m.alloc([2048], BF16)
    kwT = m.alloc([2048], BF16)
    vx = m.alloc([16, 4 * 65], BF16)
    gt = m.alloc([16, 24])
    cacc = m.alloc([16, 8 * 97])
    rz = m.alloc([16, 8])
    negT = m.alloc([2, 2048], BF16)
    mark = m.top
    xf = m.alloc([2048])
    sq = m.alloc([2048], BF16)
    rs = [m.alloc([512]) for _ in range(2)]
    bi = [0]

    def nb():
        b = bi[0] % 6
        bi[0] += 1
        return b

    def headnorm(chunk, gain, gkeys, dst, dkey):
        k.dma("sync", xf, d["projF"][chunk], (), ["xf"])
        k.act(sq, xf, AF.Square, ["xf"], ["sq"])
        for tb in range(4):
            b = nb()
            cs = slice(tb * 512, (tb + 1) * 512)
            k.mm(m.bank(b), C.cb["blk64"], sq[:, cs], True, True, ["sq"] + C.keys, [f"ps{b}"])
            r_ = rs[tb % 2]
            k.act(r_, m.bank(b), AF.Sqrt, [f"ps{b}"] + C.keys, [f"rs{tb % 2}"], scale=1.0 / HD, bias=C.eps6[:, 0:1])
            k.recip(r_, r_, [f"rs{tb % 2}"], [f"rs{tb % 2}"])
            k.stt(dst[:, cs], xf[:, cs], gain, r_, ALU.mult, ALU.mult, ["xf", f"rs{tb % 2}"] + gkeys, [dkey])

    for j in range(4):
        headnorm(j, qg[:, 1:2], ["qgs"], qT[:, j, :], ("qT", j))
    headnorm(6, kg[:, 0:1], [("kg", 0), ("kg", 1)], ksT, "ksT")
    headnorm(7, kg[:, 0:1], [("kg", 0), ("kg", 1)], kwT, "kwT")
    kcb = m.alloc([2048], BF16)
    vcb = m.alloc([2048], BF16)
    k.dma("gpsimd", kcb, d["projF"][4], (), ["kcb"])
    k.dma("gpsimd", vcb, d["projF"][5], (), ["vcb"])
    k.memset("vector", vx, 1.0, ["vx"])
    vx5 = vx.rearrange("p st (a e) -> p st a e", e=65)
    for a in range(4):
        k.dma("gpsimd", vx5[:, :, a, 0:64], d["projT"][:, a * 64:(a + 1) * 64].rearrange("(st p) c -> p st c", p=128), ["vx"], [("vx", a)])
    vxk = ["vx"] + [("vx", a) for a in range(4)]
    k.dma("sync", gt, d["projT"][:, 256:280].rearrange("(tt p) c -> p tt c", p=128), (), ["gt"])
    k.act(gt, gt, AF.Sigmoid, ["gt"], ["gt"])
    if STOP <= 1:
        return
    W1 = m.alloc([2, 32, 128], BF16)
    W2 = m.alloc([2, 64], BF16)
    posT = m.alloc([2, 32], BF16)
    posF = m.alloc([2, 32])
    for half in range(2):
        pr = slice(64 * half, 64 * half + 64)
        for i in range(2):
            for dup in range(2):
                k.dma("gpsimd", W1[pr, i, :, dup * 64:(dup + 1) * 64], d["cmp_w1"][l, i].rearrange("(l d) e -> d l e", d=64), (), [("W1", half, i, dup)])
        k.dma("sync", posF[pr], d["cmp_pos"][l].rearrange("i l d -> d i l"), (), [("posF", half)])
        k.copy("vector", posT[pr], posF[pr], [("posF", half)], [("posT", half)])
    k.dma("gpsimd", W2[0:64], d["cmp_w2"][l].rearrange("i e f -> e i f"), (), ["W2"])
    wkeys = [("W1", a, b_, c_) for a in range(2) for b_ in range(2) for c_ in range(2)] + [("posT", 0), ("posT", 1), "W2"]
    if STOP <= 1.2:
        return
    hf = m.alloc([256])
    zb = m.alloc([32, 127], BF16)
    hid = m.alloc([256], BF16)
    t1 = m.alloc([256])
    t2 = m.alloc([256])
    kcT = m.alloc([128], BF16)
    k.memset("vector", kcT, 0.0, [("kcT", 0), ("kcT", 1)])
    kcn = m.alloc([256], BF16)
    rc = m.alloc([2, 97], BF16)
    k.memset("vector", rc, 1.0, ["rc"])
    ovf = m.alloc([32])
    k.dma("sync", ovf[0:127, :], d["ovc"], (), ["ovf"])
    for g in range(2):
        k.copy("vector", rc[0:127, g, 65:97], ovf[0:127, :], ["rc", "ovf"], [("rc", "ov", g)])
    if STOP <= 1.25:
        return
    for i, src, skey in ((0, kcb, "kcb"), (1, vcb, "vcb")):
        b = nb()
        sview = bass.AP(tensor=src.tensor, offset=src.offset, ap=[[src.ap[0][0], 128], [1, 32], [16, 127]])
        k.tt("vector", zb, sview, posT[:, i, :].unsqueeze(2).broadcast_to([128, 32, 127]), ALU.add, [skey, ("posT", 0), ("posT", 1)], ["zb"])
        if STOP <= 1.3:
            return
        for g in range(2):
            bg_ = nb()
            pr = slice(64 * g, 64 * g + 64)
            o_ = m.bank(bg_)[:, 0:127]
            for li in range(32):
                k.mm(o_, W1[pr, i, li, :], zb[pr, li, :], li == 0, li == 31, wkeys + ["zb"], [f"ps{bg_}"], signal=(li == 31))
            k.copy("vector", hf[0:64, g * 128:(g + 1) * 128], m.bank(bg_)[0:64, 0:128], [f"ps{bg_}"], [("hf", g)])
        if STOP <= 1.35:
            return
        k.tt("vector", hf[0:64, 0:1], hf[0:64, 0:1], hf[0:64, 0:1], ALU.max, [("hf", 0), ("hf", 1)], ["hf"])
        if STOP <= 1.4:
            return
        gelu_tanh(k, m, hid[0:64, :], hf[0:64, :], t1[0:64, :], t2[0:64, :], ["hf"], ["hid"], ["ct1"])
        if STOP <= 1.5:
            return
        if i == 0:
            b2 = nb()
            for g in range(2):
                k.mm(m.bank(b2)[0:64, g * 128:g * 128 + 127], W2[0:64, 0, :], hid[0:64, g * 128:g * 128 + 127], True, True, ["W2", "hid"], [f"ps{b2}"])
            k.copy("vector", hf[0:64, :], m.bank(b2)[0:64, 0:256], [f"ps{b2}"], ["hf"])
            k.act(sq[0:64, 0:256], hf[0:64, :], AF.Square, ["hf"], ["sq"])
            b3 = nb()
            k.mm(m.bank(b3)[0:64, 0:256], C.cb["ones"][0:64, 0:64], sq[0:64, 0:256], True, True, ["sq"] + C.keys, [f"ps{b3}"])
            k.act(t1[0:64, :], m.bank(b3)[0:64, 0:256], AF.Sqrt, [f"ps{b3}"] + C.keys, ["ct1"], scale=1.0 / HD, bias=C.eps6[0:64, 0:1])
            k.recip(t1[0:64, :], t1[0:64, :], ["ct1"], ["ct1"])
            k.stt(kcn[0:64, :], hf[0:64, :], kg[0:64, 0:1], t1[0:64, :], ALU.mult, ALU.mult, ["hf", "ct1", ("kg", 0)], ["kcn"])
            k.copy("vector", kcT[0:64, 0:127], kcn[0:64, 0:127], ["kcn"], [("kcT", 0)])
            k.copy("vector", kcT[64:128, 0:127], kcn[0:64, 128:255], ["kcn"], [("kcT", 1)])
        else:
            b2 = nb()
            for g in range(2):
                k.mm(m.bank(b2)[0:127, g * 64:(g + 1) * 64], hid[0:64, g * 128:g * 128 + 127], W2[0:64, 1, :], True, True, ["W2", "hid"], [f"ps{b2}"])
            k.copy("vector", rc[0:127, :, 0:64], m.bank(b2)[0:127, 0:128].rearrange("p (g e) -> p g e", e=64), [f"ps{b2}", "rc"], [("rc", "v")])
    rck = ["rc", ("rc", "ov", 0), ("rc", "ov", 1), ("rc", "v")]
    if STOP <= 2:
        return
    cacc4 = cacc.rearrange("p tt (h e) -> p tt h e", e=97)
    ecT = [m.alloc([512], BF16) for _ in range(2)]
    ei = 0
    for j in range(4):
        for g in range(2):
            h = 4 * g + j
            pr = slice(64 * g, 64 * g + 64)
            for tb in range(4):
                cs = slice(tb * 512, (tb + 1) * 512)
                b = nb()
                k.mm(m.bank(b)[:, :], kcT[pr, 0:128], qT[pr, j, cs], True, False, [("kcT", g), ("qT", j)], [f"ps{b}"], signal=False)
                k.mm(m.bank(b)[0:127, :], C.cb["ident"][0:127, 0:127], Wc[0:127, h, cs], False, True, [("Wc", h)] + C.keys, [f"ps{b}"])
                e_ = ecT[ei % 2]
                k.act(e_[0:127, :], m.bank(b)[0:127, :], AF.Exp, [f"ps{b}"], [f"ecT{ei % 2}"])
                for t4 in range(4):
                    tt = 4 * tb + t4
                    b2 = nb()
                    k.mm(m.bank(b2)[:, 0:97], e_[0:127, t4 * 128:(t4 + 1) * 128], rc[0:127, g, :], True, True, [f"ecT{ei % 2}"] + rck, [f"ps{b2}"])
                    k.copy("scalar" if t4 % 2 else "vector", cacc4[:, tt, h, :], m.bank(b2)[:, 0:97], [f"ps{b2}"], [("cacc", tt, h)])
                ei += 1
    if STOP <= 3:
        return
    k.ts("vector", rz, cacc4[:, :, :, 64], 1e-30, None, ALU.max, None, [("cacc", tt, h) for tt in range(16) for h in range(8)], ["rz"])
    k.recip(rz, rz, ["rz"], ["rz"])
    sc = m.alloc([32])
    sc2 = m.alloc([32])
    m8 = m.alloc([16])
    ngm = m.alloc([32])
    for tt in range(NTT):
        for g in range(2):
            for r in range(4):
                h = 4 * g + r
                if r == 0:
                    k.stt(sc, cacc4[:, tt, h, 65:97], rz[:, tt, h:h + 1], scadd[:, tt, :], ALU.mult, ALU.add, ["rz", "scadd"], ["sc"])
                else:
                    k.stt(sc, cacc4[:, tt, h, 65:97], rz[:, tt, h:h + 1], sc, ALU.mult, ALU.add, ["rz", "sc"], ["sc"])
            k.P.op("vector", (lambda o, i_: (lambda e: e.max(out=o, in_=i_)))(m8[:, 0:8], sc), ["sc"], ["m8a"])
            k.P.op("vector", (lambda o, r_, v_: (lambda e: e.match_replace(out=o, in_to_replace=r_, in_values=v_, imm_value=-3.0e38)))(sc2, m8[:, 0:8], sc), ["sc", "m8a"], ["sc2"])
            k.P.op("vector", (lambda o, i_: (lambda e: e.max(out=o, in_=i_)))(m8[:, 8:16], sc2), ["sc2"], ["m8b"])
            k.ts("vector", ngm, sc, m8[:, 15:16], NEG, ALU.is_lt, ALU.mult, ["sc", "m8b"], ["ngm"])
            b = nb()
            k.transpose(m.bank(b)[0:32, 0:128], ngm, C.cf["ident"], ["ngm"] + C.keys, [f"ps{b}"])
            k.copy("scalar", negT[0:32, g, tt * 128:(tt + 1) * 128], m.bank(b)[0:32, 0:128], [f"ps{b}"], [("negT", g, tt)])
    if STOP <= 4:
        return
    _barrier(k.P)
    m.top = mark
    for h in range(8):
        base = h * 128 * OH_L
        k.dma("gpsimd", Ws[:, h, :], bass.AP(tensor=Rt, offset=base + OH_SO + 127, ap=[[OH_L - 1, 128], [1, 1408]]), (), [("Ws", h)])
        k.dma("gpsimd", Ww[:, h, :], bass.AP(tensor=Rt, offset=base + OH_WO + 127, ap=[[OH_L - 1, 128], [1, 640]]), (), [("Ww", h)])
    oa = m.alloc([16, 512])
    PT = [m.alloc([512], BF16) for _ in range(3)]
    sacc = [m.alloc([4, 65]) for _ in range(2)]
    cf_ = m.alloc([16])
    tmp = m.alloc([4, 64])
    stage = m.alloc([4, 128])
    items = []
    gi = 0
    for tb in range(4):
        for h in range(8):
            for br in range(2):
                si_lo = 0 if br == 0 else max(0, 4 * tb - 4)
                si_list = list(range(si_lo, 4 * tb + 4))
                for si in si_list:
                    items.append(dict(tb=tb, h=h, br=br, si=si, first=(si == si_list[0]), last=(si == si_list[-1]), gi=gi,
                                      last_of_tb=(h == 7 and br == 1 and si == si_list[-1])))
                gi += 1

    def geom(it):
        tb, si, br = it["tb"], it["si"], it["br"]
        t0 = tb * 512
        s0 = si * 128
        c0 = max(0, s0 - t0)
        c1 = 512 if br == 0 else min(512, s0 + 640 - t0)
        return t0, s0, c0, c1

    def score_stage(n):
        it = items[n]
        tb, h, br, si = it["tb"], it["h"], it["br"], it["si"]
        g = h // 4
        j = h % 4
        pr = slice(64 * g, 64 * g + 64)
        t0, s0, c0, c1 = geom(it)
        cols = slice(c0, c1)
        tcols = slice(t0 + c0, t0 + c1)
        b = n % 4
        p_ = PT[n % 3]
        pk = f"PT{n % 3}"
        kT_ = ksT if br == 0 else kwT
        W_ = Ws if br == 0 else Ww
        m0 = t0 + c0 - s0
        k.mm(m.bank(b)[:, cols], kT_[pr, s0:s0 + 128], qT[pr, j, tcols], True, False, [], [f"ps{b}"], signal=False)
        if br == 0 and m0 + (c1 - c0) > 1408:
            for ca in range(c0, c1, 256):
                cb_ = min(c1, ca + 256)
                k.mm(m.bank(b)[:, ca:cb_], C.cb["ident"], W_[:, h, 1152:1152 + (cb_ - ca)], False, False, [("Ws", h)], [f"ps{b}"], signal=False)
        else:
            k.mm(m.bank(b)[:, cols], C.cb["ident"], W_[:, h, m0:m0 + (c1 - c0)], False, br == 1, [("Ws", h), ("Ww", h)], [f"ps{b}"], signal=(br == 1))
        if br == 0:
            k.mm(m.bank(b)[:, cols], esel[0:32, s0:s0 + 128], negT[0:32, g, tcols], False, True, ["esel"], [f"ps{b}"])
        k.act(p_[:, cols], m.bank(b)[:, cols], AF.Exp, [f"ps{b}"], [pk])

    def pv_stage(n):
        it = items[n]
        tb, h, br, si = it["tb"], it["h"], it["br"], it["si"]
        g = h // 4
        t0, s0, c0, c1 = geom(it)
        bo = 4 + (it["gi"] % 2)
        bok = f"ps{bo}"
        bank_o = m.bank(bo)[:, 0:260].rearrange("p (a e) -> p a e", e=65)
        p_ = PT[n % 3]
        pk = f"PT{n % 3}"
        if it["first"]:
            k.mm(m.bank(bo)[:, 0:260], C.zeros[:, 0:128], C.zeros[:, 0:260], True, False, [("zeros",)], [bok], signal=True)
        for t4 in range(4):
            if t4 * 128 < c0 or t4 * 128 >= c1:
                continue
            a = br * 2 + g
            k.mm(bank_o[:, t4, :], p_[:, t4 * 128:(t4 + 1) * 128], vx5[:, si, a, :], False, it["last"] and t4 == 3, [pk], [bok], signal=True)
        if not it["last"]:
            return
        sa = sacc[it["gi"] % 2]
        sk = f"sacc{it['gi'] % 2}"
        k.copy("vector", sa, bank_o, [bok], [sk])
        k.ts("vector", cf_[:, 0:4], sa[:, :, 64], 1e-30, None, ALU.max, None, [sk], ["cf"])
        k.recip(cf_[:, 0:4], cf_[:, 0:4], ["cf"], ["cf"])
        k.tt("vector", cf_[:, 4:8], cf_[:, 0:4], gt[:, 4 * tb:4 * tb + 4, 3 * h + 1 + br], ALU.mult, ["cf"], ["cf2"])
        dst = oa[:, 4 * tb:4 * tb + 4, h * 64:(h + 1) * 64]
        if br == 0:
            k.tt("vector", cf_[:, 8:12], rz[:, 4 * tb:4 * tb + 4, h], gt[:, 4 * tb:4 * tb + 4, 3 * h], ALU.mult, [], ["cf3"])
            k.tt("vector", dst, cacc4[:, 4 * tb:4 * tb + 4, h, 0:64], cf_[:, 8:12].unsqueeze(2).broadcast_to([128, 4, 64]), ALU.mult, ["cf3"], [("oa", tb, h)])
        k.tt("vector", tmp, sa[:, :, 0:64], cf_[:, 4:8].unsqueeze(2).broadcast_to([128, 4, 64]), ALU.mult, [sk, "cf2"], ["tmpo"])
        k.tt("vector", dst, dst, tmp, ALU.add, ["tmpo", ("oa", tb, h)], [("oa", tb, h)])
        if it["last_of_tb"]:
            for t4 in range(4):
                tt = 4 * tb + t4
                for c in range(4):
                    b2 = 6 + (c % 2)
                    k.transpose(m.bank(b2)[:, 0:128], oa[:, tt, c * 128:(c + 1) * 128], C.cf["ident"], [("oa", tb, h_) for h_ in (2 * c, 2 * c + 1)], [f"ps{b2}"])
                    k.copy("scalar", stage[:, c, :], m.bank(b2)[:, 0:128], [f"ps{b2}"], [("stg", c)])
                    k.dma("sync", d["mixF"][c][:, tt * 128:(tt + 1) * 128], stage[:, c, :], [("stg", c)], [("mixF", c, tt)])

    for n in range(len(items) + 1):
        if n < len(items):
            score_stage(n)
        if n >= 1:
            pv_stage(n - 1)


_CST = None


def _get_cst():
    global _CST
    if _CST is None:
        c = host_consts()
        _CST = np.ascontiguousarray(np.concatenate([c[n] for n in ["ident", "ones", "blk64", "tril_qp", "lt_st", "uincl", "lstrict"]], axis=1).astype(np.float32))
    return _CST


def kernel(**inputs):
    x = np.asarray(inputs["x"], dtype=np.float32)
    nc = build()
    shared = {name: np.ascontiguousarray(np.asarray(inputs[name], dtype=np.float32)) for name, _ in INPUT_SPECS if name != "x"}
    shared["cst"] = _get_cst()
    shared.update(host_nsa_consts())
    in_maps = []
    for b in range(8):
        mp = dict(shared)
        mp["x"] = np.ascontiguousarray(x[b])
        in_maps.append(mp)
    res = run_bass_kernel_spmd(nc, in_maps, core_ids=list(range(8)))
    return np.stack([np.asarray(r["out"], dtype=np.float32) for r in res.results], axis=0)
```
